# Optimizing a Trainium2 kernel written in Bass

```python
import math
import jax
import jax.numpy as jnp
from jax import lax
import numpy as np

D_MODEL = 1024
BATCH = 8
SEQ = 4096
DEPTH = 2

GRID_W = 64
CTX_LEN = 256
N_MOD = 9
D_FF = 2816
EPS = 1e-6
DT_MIN = 1e-3
DT_MAX = 1e-1

GLA_HEADS = 4
GLA_DK = 48
GLA_DV = 96
GLA_RANK = 16
GLA_TAU = 16.0
GLA_CHUNK = 64
GLA_QK = GLA_HEADS * GLA_DK
GLA_WIDTH = GLA_HEADS * GLA_DV

S5_WIDTH = 256
S5_GROUP = 16
S5_GROUPS = S5_WIDTH // S5_GROUP
S5_STATE = 64

SSD_HEADS = 6
SSD_HEADDIM = 64
SSD_GROUPS = 2
SSD_REP = SSD_HEADS // SSD_GROUPS
SSD_STATE = 128
SSD_CONV = 5
SSD_CHUNK = 64
SSD_WIDTH = SSD_HEADS * SSD_HEADDIM
SSD_XBC = SSD_WIDTH + 2 * SSD_GROUPS * SSD_STATE

MIX_WIDTH = GLA_WIDTH + S5_WIDTH + SSD_WIDTH
IN_SPLITS = (GLA_QK, GLA_QK, GLA_WIDTH, GLA_WIDTH, 2 * GLA_RANK, S5_WIDTH, SSD_WIDTH, SSD_XBC, 2 * SSD_HEADS)
IN_WIDTH = sum(IN_SPLITS)

kernel_name = "hybrid_gla_s5_ssd_macaron_prefix"


def rms_norm(x, g):
    xf = x.astype(jnp.float32)
    y = xf * lax.rsqrt(jnp.mean(xf * xf, axis=-1, keepdims=True) + EPS)
    return (y * g.astype(jnp.float32)).astype(x.dtype)


def modulate(x, shift, scale):
    return x * (1.0 + scale) + shift


def swiglu(x, w_gate, w_up, w_down):
    return (jax.nn.silu(x @ w_gate) * (x @ w_up)) @ w_down


def ffn_sublayer(h, mod, g, w_gate, w_up, w_down, base):
    u = modulate(rms_norm(h, g), mod[:, base], mod[:, base + 1])
    return h + 0.5 * mod[:, base + 2] * swiglu(u, w_gate, w_up, w_down)


def flip(t):
    return jnp.flip(t, axis=1)


def to_col_major(t):
    b, n, ch = t.shape
    rows = n // GRID_W
    return t.reshape(b, rows, GRID_W, ch).transpose(0, 2, 1, 3).reshape(b, n, ch)


def from_col_major(t):
    b, n, ch = t.shape
    rows = n // GRID_W
    return t.reshape(b, GRID_W, rows, ch).transpose(0, 2, 1, 3).reshape(b, n, ch)


def split_points():
    pts, acc = [], 0
    for s in IN_SPLITS[:-1]:
        acc += s
        pts.append(acc)
    return pts


def chunk_state_scan(decay, contrib, s0):
    def step(s, inp):
        d, u = inp
        return d * s + u, s
    s_final, starts = lax.scan(step, s0, (decay, contrib))
    return starts, s_final


def run_direction(scan_fn, seqs, s0, reverse):
    if reverse:
        y, s = scan_fn(*[flip(t) for t in seqs], s0)
        return flip(y), s
    return scan_fn(*seqs, s0)


def gla_chunked(q, k, v, log_a, s0):
    b, n, h, dk = q.shape
    dv = v.shape[-1]
    nc = n // GLA_CHUNK
    q = q.reshape(b, nc, GLA_CHUNK, h, dk)
    k = k.reshape(b, nc, GLA_CHUNK, h, dk)
    log_a = log_a.reshape(b, nc, GLA_CHUNK, h, dk)
    v = v.reshape(b, nc, GLA_CHUNK, h, dv)
    cum = jnp.cumsum(log_a, axis=2)
    q_dec = q * jnp.exp(cum)
    k_inv = k * jnp.exp(-cum)
    causal = jnp.tril(jnp.ones((GLA_CHUNK, GLA_CHUNK), dtype=bool))
    scores = jnp.einsum("bcihk,bcjhk->bchij", q_dec, k_inv)
    scores = jnp.where(causal, scores, 0.0)
    o_intra = jnp.einsum("bchij,bcjhv->bcihv", scores, v)
    cum_last = cum[:, :, -1]
    k_end = k * jnp.exp(cum_last[:, :, None] - cum)
    contrib = jnp.einsum("bcjhk,bcjhv->cbhkv", k_end, v)
    decay = jnp.exp(cum_last).transpose(1, 0, 2, 3)[..., None]
    starts, s_final = chunk_state_scan(decay, contrib, s0)
    o_inter = jnp.einsum("bcihk,cbhkv->bcihv", q_dec, starts)
    return (o_intra + o_inter).reshape(b, n, h, dv), s_final


def gla_prep(q, k, v, g_lr, w_gate, b_gate):
    b, n, _ = q.shape
    f32 = jnp.float32
    q = q.astype(f32).reshape(b, n, GLA_HEADS, GLA_DK) * GLA_DK ** -0.5
    k = k.astype(f32).reshape(b, n, GLA_HEADS, GLA_DK)
    v = v.astype(f32).reshape(b, n, GLA_HEADS, GLA_DV)
    g = jnp.einsum("bndr,drk->bndk", g_lr.astype(f32).reshape(b, n, 2, GLA_RANK), w_gate.astype(f32)) + b_gate.astype(f32)
    log_a = (jax.nn.log_sigmoid(g) / GLA_TAU).reshape(b, n, 2, GLA_HEADS, GLA_DK)
    return q, k, v, log_a


def gla_readout(o, r, norm_g):
    b, n = o.shape[:2]
    o = o * lax.rsqrt(jnp.mean(o * o, axis=-1, keepdims=True) + EPS)
    o = o.reshape(b, n, GLA_WIDTH) * norm_g.astype(jnp.float32)
    return (o * jax.nn.silu(r.astype(jnp.float32))).astype(r.dtype)


def gla_mixer(parts_c, parts_l, w_gate, b_gate, norm_g, ctx_out):
    qc, kc, vc, lac = gla_prep(parts_c[0], parts_c[1], parts_c[2], parts_c[4], w_gate, b_gate)
    ql, kl, vl, lal = gla_prep(parts_l[0], parts_l[1], parts_l[2], parts_l[4], w_gate, b_gate)
    zeros = jnp.zeros((qc.shape[0], GLA_HEADS, GLA_DK, GLA_DV), jnp.float32)
    o_c = 0.0
    o_l = 0.0
    for direction, reverse in ((0, False), (1, True)):
        oc, s_ctx = run_direction(gla_chunked, (qc, kc, vc, lac[:, :, direction]), zeros, reverse)
        ol, _ = run_direction(gla_chunked, (ql, kl, vl, lal[:, :, direction]), s_ctx, reverse)
        o_c = o_c + oc
        o_l = o_l + ol
    y_l = gla_readout(o_l, parts_l[3], norm_g)
    y_c = gla_readout(o_c, parts_c[3], norm_g) if ctx_out else None
    return y_c, y_l


def s5_discretize(a_re, a_im, log_dt, b_re, b_im):
    dt = jnp.exp(log_dt)[:, None]
    mag = jnp.exp(dt * a_re)
    ab_re = mag * jnp.cos(dt * a_im)
    ab_im = mag * jnp.sin(dt * a_im)
    den = a_re * a_re + a_im * a_im
    num_re = ab_re - 1.0
    num_im = ab_im
    f_re = (num_re * a_re + num_im * a_im) / den
    f_im = (num_im * a_re - num_re * a_im) / den
    bb_re = f_re[..., None] * b_re - f_im[..., None] * b_im
    bb_im = f_re[..., None] * b_im + f_im[..., None] * b_re
    return ab_re, ab_im, bb_re, bb_im


def complex_affine_combine(e1, e2):
    a1r, a1i, b1r, b1i = e1
    a2r, a2i, b2r, b2i = e2
    return (a2r * a1r - a2i * a1i,
            a2r * a1i + a2i * a1r,
            a2r * b1r - a2i * b1i + b2r,
            a2r * b1i + a2i * b1r + b2i)


def s5_scan(u, ab_re, ab_im, bb_re, bb_im, x0_re, x0_im, reverse):
    bu_re = jnp.einsum("bngh,gph->bngp", u, bb_re)
    bu_im = jnp.einsum("bngh,gph->bngp", u, bb_im)
    first = -1 if reverse else 0
    bu_re = bu_re.at[:, first].add(ab_re * x0_re - ab_im * x0_im)
    bu_im = bu_im.at[:, first].add(ab_re * x0_im + ab_im * x0_re)
    a_re = jnp.broadcast_to(ab_re, bu_re.shape)
    a_im = jnp.broadcast_to(ab_im, bu_im.shape)
    _, _, xr, xi = lax.associative_scan(complex_affine_combine, (a_re, a_im, bu_re, bu_im), reverse=reverse, axis=1)
    return xr, xi


def s5_readout(xr, xi, c_re, c_im):
    return jnp.einsum("bngp,ghp->bngh", xr, c_re) - jnp.einsum("bngp,ghp->bngh", xi, c_im)


def s5_output(y, u, d, w_glu, b_glu, dtype):
    b, n = y.shape[:2]
    f32 = jnp.float32
    y = (y + u * d.astype(f32).reshape(S5_GROUPS, S5_GROUP)).reshape(b, n, S5_WIDTH)
    g = jax.nn.gelu(y)
    return (g * jax.nn.sigmoid(g @ w_glu.astype(f32) + b_glu.astype(f32))).astype(dtype)


def s5_mixer(u_c, u_l, a_re, a_im, log_dt, b_re, b_im, c_re, c_im, d, w_glu, b_glu, ctx_out):
    f32 = jnp.float32
    dtype = u_l.dtype
    bc, nc_, _ = u_c.shape
    bl, nl_, _ = u_l.shape
    uc = u_c.astype(f32).reshape(bc, nc_, S5_GROUPS, S5_GROUP)
    ul = u_l.astype(f32).reshape(bl, nl_, S5_GROUPS, S5_GROUP)
    c_re = c_re.astype(f32)
    c_im = c_im.astype(f32)
    zeros = jnp.zeros((bc, S5_GROUPS, S5_STATE), f32)
    y_c = 0.0
    y_l = 0.0
    for direction, reverse in ((0, False), (1, True)):
        disc = s5_discretize(a_re[direction].astype(f32), a_im[direction].astype(f32),
                             log_dt[direction].astype(f32), b_re.astype(f32), b_im.astype(f32))
        xr_c, xi_c = s5_scan(uc, *disc, zeros, zeros, reverse)
        last = 0 if reverse else -1
        xr_l, xi_l = s5_scan(ul, *disc, xr_c[:, last], xi_c[:, last], reverse)
        y_l = y_l + s5_readout(xr_l, xi_l, c_re, c_im)
        if ctx_out:
            y_c = y_c + s5_readout(xr_c, xi_c, c_re, c_im)
    out_l = s5_output(y_l, ul, d, w_glu, b_glu, dtype)
    out_c = s5_output(y_c, uc, d, w_glu, b_glu, dtype) if ctx_out else None
    return out_c, out_l


def depthwise_conv(x, w):
    ch = x.shape[-1]
    pad = (w.shape[0] - 1) // 2
    return lax.conv_general_dilated(x, w.astype(x.dtype)[:, None, :], window_strides=(1,),
                                    padding=[(pad, pad)], dimension_numbers=("NWC", "WIO", "NWC"),
                                    feature_group_count=ch)


def ssd_chunked(x, log_a, bm, cm, s0):
    b, n, g, r, p = x.shape
    ns = bm.shape[-1]
    nc = n // SSD_CHUNK
    x = x.reshape(b, nc, SSD_CHUNK, g, r, p)
    log_a = log_a.reshape(b, nc, SSD_CHUNK, g, r)
    bm = bm.reshape(b, nc, SSD_CHUNK, g, ns)
    cm = cm.reshape(b, nc, SSD_CHUNK, g, ns)
    cum = jnp.cumsum(log_a, axis=2)
    causal = jnp.tril(jnp.ones((SSD_CHUNK, SSD_CHUNK), dtype=bool))[:, :, None, None]
    seg = cum[:, :, :, None] - cum[:, :, None, :]
    lmat = jnp.exp(jnp.where(causal, seg, -jnp.inf))
    scores = jnp.einsum("bcign,bcjgn->bcijg", cm, bm)
    y_diag = jnp.einsum("bcijgr,bcjgrp->bcigrp", scores[..., None] * lmat, x)
    cum_last = cum[:, :, -1]
    contrib = jnp.einsum("bcjgn,bcjgr,bcjgrp->cbgrpn", bm, jnp.exp(cum_last[:, :, None] - cum), x)
    decay = jnp.exp(cum_last).transpose(1, 0, 2, 3)[..., None, None]
    starts, s_final = chunk_state_scan(decay, contrib, s0)
    y_off = jnp.einsum("bcign,cbgrpn,bcigr->bcigrp", cm, starts, jnp.exp(cum))
    return (y_diag + y_off).reshape(b, n, g, r, p), s_final


def ssd_prep(xbc, dt_raw, conv_w, conv_b, dt_bias, a_log):
    b, n, _ = xbc.shape
    f32 = jnp.float32
    xbc = jax.nn.silu(depthwise_conv(xbc, conv_w) + conv_b.astype(xbc.dtype)).astype(f32)
    xs, bm, cm = jnp.split(xbc, [SSD_WIDTH, SSD_WIDTH + SSD_GROUPS * SSD_STATE], axis=-1)
    xs = xs.reshape(b, n, SSD_GROUPS, SSD_REP, SSD_HEADDIM)
    bm = bm.reshape(b, n, SSD_GROUPS, SSD_STATE)
    cm = cm.reshape(b, n, SSD_GROUPS, SSD_STATE)
    dt = jax.nn.softplus(dt_raw.astype(f32).reshape(b, n, 2, SSD_HEADS) + dt_bias.astype(f32))
    log_a = dt * -jnp.exp(a_log.astype(f32))
    dt = dt.reshape(b, n, 2, SSD_GROUPS, SSD_REP)
    log_a = log_a.reshape(b, n, 2, SSD_GROUPS, SSD_REP)
    return xs, bm, cm, dt, log_a


def ssd_gate_norm(y, z, norm_g):
    b, n = y.shape[:2]
    y = y.reshape(b, n, SSD_WIDTH) * jax.nn.silu(z.astype(jnp.float32))
    return rms_norm(y, norm_g).astype(z.dtype)


def ssd_mixer(parts_c, parts_l, conv_w, conv_b, dt_bias, a_log, d, norm_g, ctx_out):
    z_c, xbc_c, dt_c = parts_c
    z_l, xbc_l, dt_l = (to_col_major(t) for t in parts_l)
    xc, bc, cc, dtc, lac = ssd_prep(xbc_c, dt_c, conv_w, conv_b, dt_bias, a_log)
    xl, bl, cl, dtl, lal = ssd_prep(xbc_l, dt_l, conv_w, conv_b, dt_bias, a_log)
    zeros = jnp.zeros((xc.shape[0], SSD_GROUPS, SSD_REP, SSD_HEADDIM, SSD_STATE), jnp.float32)
    skip = d.astype(jnp.float32).reshape(SSD_GROUPS, SSD_REP, 1)
    y_c = xc * skip
    y_l = xl * skip
    for direction, reverse in ((0, False), (1, True)):
        yc, s_ctx = run_direction(ssd_chunked, (xc * dtc[:, :, direction, ..., None], lac[:, :, direction], bc, cc), zeros, reverse)
        yl, _ = run_direction(ssd_chunked, (xl * dtl[:, :, direction, ..., None], lal[:, :, direction], bl, cl), s_ctx, reverse)
        y_c = y_c + yc
        y_l = y_l + yl
    out_l = from_col_major(ssd_gate_norm(y_l, z_l, norm_g))
    out_c = ssd_gate_norm(y_c, z_c, norm_g) if ctx_out else None
    return out_c, out_l


def token_mixing(u, uc, w_in, w_out, gla_params, s5_params, ssd_params, ctx_out):
    pts = split_points()
    p_l = jnp.split(u @ w_in, pts, axis=-1)
    p_c = jnp.split(uc @ w_in, pts, axis=-1)
    ga_c, ga_l = gla_mixer(p_c[0:5], p_l[0:5], *gla_params, ctx_out)
    sb_c, sb_l = s5_mixer(p_c[5], p_l[5], *s5_params, ctx_out)
    sc_c, sc_l = ssd_mixer(p_c[6:9], p_l[6:9], *ssd_params, ctx_out)
    y_l = jnp.concatenate([ga_l, sb_l, sc_l], axis=-1) @ w_out
    y_c = jnp.concatenate([ga_c, sb_c, sc_c], axis=-1) @ w_out if ctx_out else None
    return y_c, y_l


def setup_inputs(seed: int = 0) -> dict:
    key = jax.random.key(seed)
    ks = iter(jax.random.split(key, 40))

    def nrm(shape, scale):
        return jax.random.normal(next(ks), shape, jnp.float32) * scale

    def near_one(shape):
        return 1.0 + nrm(shape, 0.02)

    nl = DEPTH
    s5_n = jnp.arange(S5_STATE, dtype=jnp.float32)
    s5_log_dt = jax.random.uniform(next(ks), (nl, 2, S5_GROUPS), jnp.float32,
                                   minval=math.log(DT_MIN), maxval=math.log(DT_MAX))
    ssd_dt = jnp.exp(jax.random.uniform(next(ks), (nl, 2, SSD_HEADS), jnp.float32,
                                        minval=math.log(DT_MIN), maxval=math.log(DT_MAX)))
    ssd_a_log = jnp.log(jax.random.uniform(next(ks), (nl, 2, SSD_HEADS), jnp.float32, minval=1.0, maxval=16.0))
    return {
        "x": nrm((BATCH, SEQ, D_MODEL), 1.0),
        "c": nrm((BATCH, D_MODEL), 1.0),
        "ctx": nrm((BATCH, CTX_LEN, D_MODEL), 1.0),
        "c_ctx": nrm((D_MODEL,), 1.0),
        "ada_w": nrm((nl, D_MODEL, N_MOD * D_MODEL), 0.5 * D_MODEL ** -0.5),
        "ada_b": nrm((nl, N_MOD * D_MODEL), 0.02),
        "norm_g": near_one((nl, 3, D_MODEL)),
        "w_in": nrm((nl, D_MODEL, IN_WIDTH), D_MODEL ** -0.5),
        "w_out": nrm((nl, MIX_WIDTH, D_MODEL), MIX_WIDTH ** -0.5),
        "ff_w_gate": nrm((nl, 2, D_MODEL, D_FF), D_MODEL ** -0.5),
        "ff_w_up": nrm((nl, 2, D_MODEL, D_FF), D_MODEL ** -0.5),
        "ff_w_down": nrm((nl, 2, D_FF, D_MODEL), D_FF ** -0.5),
        "gla_w_gate": nrm((nl, 2, GLA_RANK, GLA_QK), GLA_RANK ** -0.5),
        "gla_b_gate": nrm((nl, 2, GLA_QK), 0.1),
        "gla_norm_g": near_one((nl, GLA_WIDTH)),
        "s5_a_re": -0.5 + nrm((nl, 2, S5_GROUPS, S5_STATE), 0.01),
        "s5_a_im": math.pi * s5_n + nrm((nl, 2, S5_GROUPS, S5_STATE), 0.01),
        "s5_log_dt": s5_log_dt,
        "s5_b_re": nrm((nl, S5_GROUPS, S5_STATE, S5_GROUP), (2 * S5_GROUP) ** -0.5),
        "s5_b_im": nrm((nl, S5_GROUPS, S5_STATE, S5_GROUP), (2 * S5_GROUP) ** -0.5),
        "s5_c_re": nrm((nl, S5_GROUPS, S5_GROUP, S5_STATE), (2 * S5_STATE) ** -0.5),
        "s5_c_im": nrm((nl, S5_GROUPS, S5_GROUP, S5_STATE), (2 * S5_STATE) ** -0.5),
        "s5_d": nrm((nl, S5_WIDTH), 1.0),
        "s5_w_glu": nrm((nl, S5_WIDTH, S5_WIDTH), S5_WIDTH ** -0.5),
        "s5_b_glu": nrm((nl, S5_WIDTH), 0.02),
        "ssd_conv_w": nrm((nl, SSD_CONV, SSD_XBC), SSD_CONV ** -0.5),
        "ssd_conv_b": nrm((nl, SSD_XBC), 0.02),
        "ssd_dt_bias": ssd_dt + jnp.log(-jnp.expm1(-ssd_dt)),
        "ssd_a_log": ssd_a_log,
        "ssd_d": 1.0 + nrm((nl, SSD_HEADS), 0.1),
        "ssd_norm_g": near_one((nl, SSD_WIDTH)),
        "final_norm_g": near_one((D_MODEL,)),
    }


def reference(x, c, ctx, c_ctx, ada_w, ada_b, norm_g, w_in, w_out, ff_w_gate, ff_w_up, ff_w_down,
              gla_w_gate, gla_b_gate, gla_norm_g,
              s5_a_re, s5_a_im, s5_log_dt, s5_b_re, s5_b_im, s5_c_re, s5_c_im, s5_d, s5_w_glu, s5_b_glu,
              ssd_conv_w, ssd_conv_b, ssd_dt_bias, ssd_a_log, ssd_d, ssd_norm_g, final_norm_g):
    h = x
    hc = ctx
    for i in range(DEPTH):
        ctx_out = i < DEPTH - 1
        mod = (jax.nn.silu(c) @ ada_w[i] + ada_b[i]).reshape(-1, N_MOD, 1, D_MODEL)
        mod_c = (jax.nn.silu(c_ctx) @ ada_w[i] + ada_b[i]).reshape(1, N_MOD, 1, D_MODEL)
        ff1 = (norm_g[i, 0], ff_w_gate[i, 0], ff_w_up[i, 0], ff_w_down[i, 0])
        ff2 = (norm_g[i, 2], ff_w_gate[i, 1], ff_w_up[i, 1], ff_w_down[i, 1])
        gla_params = (gla_w_gate[i], gla_b_gate[i], gla_norm_g[i])
        s5_params = (s5_a_re[i], s5_a_im[i], s5_log_dt[i], s5_b_re[i], s5_b_im[i],
                     s5_c_re[i], s5_c_im[i], s5_d[i], s5_w_glu[i], s5_b_glu[i])
        ssd_params = (ssd_conv_w[i], ssd_conv_b[i], ssd_dt_bias[i], ssd_a_log[i], ssd_d[i], ssd_norm_g[i])

        h = ffn_sublayer(h, mod, *ff1, 0)
        hc = ffn_sublayer(hc, mod_c, *ff1, 0)

        u = modulate(rms_norm(h, norm_g[i, 1]), mod[:, 3], mod[:, 4])
        uc = modulate(rms_norm(hc, norm_g[i, 1]), mod_c[:, 3], mod_c[:, 4])
        y_c, y_l = token_mixing(u, uc, w_in[i], w_out[i], gla_params, s5_params, ssd_params, ctx_out)
        h = h + mod[:, 5] * y_l

        h = ffn_sublayer(h, mod, *ff2, 6)
        if ctx_out:
            hc = hc + mod_c[:, 5] * y_c
            hc = ffn_sublayer(hc, mod_c, *ff2, 6)
    return rms_norm(h, final_norm_g)
```

```python
import contextlib
import numpy as np
import concourse.bass as bass
import concourse.mybir as mybir
from concourse.bass_utils import run_bass_kernel_spmd

F32 = mybir.dt.float32
BF16 = mybir.dt.bfloat16
AF = mybir.ActivationFunctionType
ALU = mybir.AluOpType

ENGS = ("pe", "act", "dve", "pool", "sp")
NDMA = 20


class Res:
    __slots__ = ("name", "lw", "rd")

    def __init__(self, name=""):
        self.name = name
        self.lw = None
        self.rd = []


SEMBLK = 12000


def _ck(key, eng):
    return key == eng or (isinstance(key, tuple) and key[0] == "c" and key[1] == eng)


class Prog:
    def __init__(self, nc):
        self.nc = nc
        self.ops = {e: [] for e in ENGS}
        self.cnt = {e: 0 for e in ENGS}
        self.known = {e: {} for e in ENGS}
        self.dma_n = {e: 0 for e in ENGS}
        self.dma_ev = {e: [None] * NDMA for e in ENGS}
        self.sems = {}
        self.ckeys = set()
        self.last_ev = {}

    def _need(self, eng, ev, waits):
        if ev is None:
            return
        e2, key, val, vc = ev
        if e2 == eng and eng == "pe" and _ck(key, "pe"):
            return
        kn = self.known[eng]
        if kn.get(key, 0) >= val:
            return
        if waits.get(key, 0) < val:
            waits[key] = val
        if vc:
            for k, v in vc.items():
                if kn.get(k, 0) < v:
                    kn[k] = v
        kn[key] = val

    def _deps(self, eng, reads, writes, waits, dwrites=()):
        for r in reads:
            for lw in (r.lw or ()):
                self._need(eng, lw, waits)
        for w in writes:
            for lw in (w.lw or ()):
                if not (lw[0] == eng and _ck(lw[1], eng)):
                    self._need(eng, lw, waits)
            for ev in w.rd:
                if ev[0] == eng and _ck(ev[1], eng):
                    continue
                self._need(eng, ev, waits)
        for w in dwrites:
            for ev in w.rd:
                if ev[0] == eng and _ck(ev[1], eng):
                    continue
                self._need(eng, ev, waits)

    @staticmethod
    def _compact(lst):
        best = {}
        for e in lst:
            if e[1] not in best or best[e[1]][2] < e[2]:
                best[e[1]] = e
        return list(best.values())

    def _commit(self, ev, reads, writes, dwrites=()):
        for w in dwrites:
            w.lw = (w.lw or []) + [ev]
            if len(w.lw) > 48:
                w.lw = self._compact(w.lw)
            w.rd = []
        for r in reads:
            r.rd.append(ev)
            if len(r.rd) > 40:
                r.rd = self._compact(r.rd)
        for w in writes:
            w.lw = [ev]
            w.rd = []

    def op(self, eng, fn, reads=(), writes=()):
        waits = {}
        self._deps(eng, reads, writes, waits)
        self.cnt[eng] += 1
        blk, off = divmod(self.cnt[eng] - 1, SEMBLK)
        key = eng if blk == 0 else ("c", eng, blk)
        self.ckeys.add(key)
        ev = (eng, key, off + 1, dict(self.known[eng]))
        self.last_ev[eng] = ev
        self.ops[eng].append((fn, list(waits.items()), (key, 1)))
        self._commit(ev, reads, writes)
        return ev

    def dma(self, eng, out, in_, reads=(), writes=(), dwrites=(), **kw):
        waits = {}
        self._deps(eng, reads, writes, waits, dwrites)
        n = self.dma_n[eng]
        self.dma_n[eng] += 1
        slot = n % NDMA
        gen = n // NDMA + 1
        key = ("dma", eng, slot)
        self._need(eng, self.dma_ev[eng][slot], waits)
        ev = (eng, key, 16 * gen, dict(self.known[eng]))
        self.dma_ev[eng][slot] = ev

        def fn(e, out=out, in_=in_, kw=kw):
            return e.dma_start(out=out, in_=in_, **kw)
        self.ops[eng].append((fn, list(waits.items()), (key, 16)))
        self._commit(ev, reads, writes, dwrites)
        return ev

    def barrier(self):
        evs = []
        for e in ENGS:
            if e in self.last_ev:
                evs.append(self.last_ev[e])
            for ev in self.dma_ev[e]:
                if ev is not None:
                    evs.append(ev)
        for e in ENGS:
            waits = {}
            for ev in evs:
                if ev[0] == e and _ck(ev[1], e):
                    continue
                kn = self.known[e]
                if kn.get(ev[1], 0) >= ev[2]:
                    continue
                waits[ev[1]] = max(waits.get(ev[1], 0), ev[2])
                kn[ev[1]] = ev[2]
            if waits:
                self.ops[e].append((None, list(waits.items()), None))

    def emit(self):
        nc = self.nc
        with contextlib.ExitStack() as st:
            keys = list(ENGS) + [kk for kk in self.ckeys if not isinstance(kk, str)]
            for e in ("sp", "act", "pool"):
                for s in range(NDMA):
                    keys.append(("dma", e, s))
            for k in keys:
                nm = k if isinstance(k, str) else "%s_%s_%d" % (k[0], k[1], k[2])
                self.sems[k] = st.enter_context(nc.semaphore("s_" + nm))
            fin = {}
            for e in ENGS:
                if e in self.last_ev:
                    fin[self.last_ev[e][1]] = self.last_ev[e][2]
                for ev in self.dma_ev[e]:
                    if ev is not None:
                        fin[ev[1]] = max(fin.get(ev[1], 0), ev[2])
            block = st.enter_context(nc.Block())
            sems = self.sems

            def run(engobj, name, extra_final=None):
                for fn, waits, inc in self.ops[name]:
                    for k, v in waits:
                        engobj.wait_ge(sems[k], v)
                    if fn is None:
                        continue
                    ins = fn(engobj)
                    ins.then_inc(sems[inc[0]], inc[1])
                if extra_final:
                    for k, v in extra_final.items():
                        engobj.wait_ge(sems[k], v)

            @block.tensor
            def _(e):
                run(e, "pe")

            @block.scalar
            def _(e):
                run(e, "act")

            @block.vector
            def _(e):
                run(e, "dve")

            @block.gpsimd
            def _(e):
                run(e, "pool")

            @block.sync
            def _(e):
                run(e, "sp", fin)


D = 1024
FF = 2816
NFC = 22
TC = 256
TL = 4096
T = TC + TL
DEPTH = 2
INW = 2732
Q0, K0, V0, R0, GL0, S50, Z0, XBC0, DT0 = 0, 192, 384, 768, 1152, 1184, 1440, 1824, 2720
EPS = 1e-6


class TB:
    def __init__(self, t, name=""):
        self.t = t
        self.r = Res(name)

    def __getitem__(self, k):
        return self.t[k]


class K:
    pass


def build(n_layers=DEPTH, debug=None, stop=None, mix_on=("gla", "ssd", "s5")):
    nc = bass.Bass("TRN2", target_bir_lowering=False)
    P = Prog(nc)
    k = K()
    k.nc, k.P = nc, P
    k.debug = debug or ()
    k.stop = stop
    k.mix_on = mix_on

    def din(name, shape, dt=F32):
        return nc.dram_tensor(name, list(shape), dt, kind="ExternalInput").ap()

    I = {}
    I["x"] = din("x", [TL, D])
    I["ctx"] = din("ctx", [TC, D])
    I["cc"] = din("cc", [2, D])
    I["ada_w"] = din("ada_w", [DEPTH, D, 9 * D])
    I["ada_b"] = din("ada_b", [DEPTH, 9 * D])
    I["norm_g"] = din("norm_g", [DEPTH, 3, D])
    I["w_in"] = din("w_in", [DEPTH, D, INW])
    I["w_out"] = din("w_out", [DEPTH, D, D])
    I["ff_w_gate"] = din("ff_w_gate", [DEPTH, 2, D, FF])
    I["ff_w_up"] = din("ff_w_up", [DEPTH, 2, D, FF])
    I["ff_w_down"] = din("ff_w_down", [DEPTH, 2, FF, D])
    I["gla_w_gate"] = din("gla_w_gate", [DEPTH, 2, 16, 192])
    I["gla_b_gate"] = din("gla_b_gate", [DEPTH, 2, 192])
    I["gla_norm_g"] = din("gla_norm_g", [DEPTH, 384])
    I["s5_a_re"] = din("s5_a_re", [DEPTH, 2, 16, 64])
    I["s5_a_im"] = din("s5_a_im", [DEPTH, 2, 16, 64])
    I["s5_log_dt"] = din("s5_log_dt", [DEPTH, 2, 16])
    I["s5_b_re"] = din("s5_b_re", [DEPTH, 16, 64, 16])
    I["s5_b_im"] = din("s5_b_im", [DEPTH, 16, 64, 16])
    I["s5_c_re"] = din("s5_c_re", [DEPTH, 16, 16, 64])
    I["s5_c_im"] = din("s5_c_im", [DEPTH, 16, 16, 64])
    I["s5_d"] = din("s5_d", [DEPTH, 256])
    I["s5_w_glu"] = din("s5_w_glu", [DEPTH, 256, 256])
    I["s5_b_glu"] = din("s5_b_glu", [DEPTH, 256])
    I["ssd_conv_w"] = din("ssd_conv_w", [DEPTH, 5, 896])
    I["ssd_conv_b"] = din("ssd_conv_b", [DEPTH, 896])
    I["ssd_dt_bias"] = din("ssd_dt_bias", [DEPTH, 2, 6])
    I["ssd_a_log"] = din("ssd_a_log", [DEPTH, 2, 6])
    I["ssd_d"] = din("ssd_d", [DEPTH, 6])
    I["ssd_norm_g"] = din("ssd_norm_g", [DEPTH, 384])
    I["final_norm_g"] = din("final_norm_g", [1, D])
    I["cst_idb"] = din("cst_idb", [128, 128], BF16)
    I["cst_idf"] = din("cst_idf", [128, 128])
    I["cst_m"] = din("cst_m", [NCST, 128, 128])
    I["cst_s5"] = din("cst_s5", [5, 128, 256])
    k.I = I
    k.out = nc.dram_tensor("out", [TL, D], F32, kind="ExternalOutput").ap()

    def dscr(name, shape, dt=F32):
        kind = "ExternalOutput" if name in k.debug else "Internal"
        return TB(nc.dram_tensor(name, list(shape), dt, kind=kind).ap(), name)

    S = {}
    S["hbuf"] = dscr("hbuf", [T, D])
    S["tm"] = dscr("tm", [T, INW])
    S["fm"] = dscr("fm", [416, T])
    S["mix"] = dscr("mix", [T, D], BF16)
    for l in range(DEPTH):
        for j in range(2):
            S["wg%d%d" % (l, j)] = dscr("wg%d%d" % (l, j), [D, FF], BF16)
            S["wu%d%d" % (l, j)] = dscr("wu%d%d" % (l, j), [D, FF], BF16)
            S["wd%d%d" % (l, j)] = dscr("wd%d%d" % (l, j), [FF, D], BF16)
        S["win%d" % l] = dscr("win%d" % l, [D, INW], BF16)
        S["wout%d" % l] = dscr("wout%d" % l, [D, D], BF16)
    S["gla_o"] = dscr("gla_o", [T, 384])
    S["ssd_y"] = dscr("ssd_y", [T, 384])
    k.S = S

    with contextlib.ExitStack() as gst:
        k.gst = gst
        k.idb = sbuf(k, gst, "idb", [128, 128], BF16)
        k.idf = sbuf(k, gst, "idf", [128, 128], F32)
        P.dma("sp", k.idb[:], I["cst_idb"], writes=[k.idb.r])
        P.dma("sp", k.idf[:], I["cst_idf"], writes=[k.idf.r])
        k.scl = sbuf(k, gst, "scl", [128, DEPTH, 2, 3, 8], F32)
        k.sh = sbuf(k, gst, "shf", [128, DEPTH, 2, 3, 8], F32)
        k.psb = [psum(k, gst, "psb%d" % i, [128, 512], F32) for i in range(8)]
        prologue(k, n_layers)
        pipeline(k, n_layers)
        P.emit()
    return nc


_uid = [0]


def sbuf(k, st, name, shape, dt):
    _uid[0] += 1
    return TB(st.enter_context(k.nc.sbuf_tensor("sb%d_%s" % (_uid[0], name), list(shape), dt)), name)


def psum(k, st, name, shape, dt):
    _uid[0] += 1
    return TB(st.enter_context(k.nc.psum_tensor("ps%d_%s" % (_uid[0], name), list(shape), dt)), name)


NCST = 8
_rr = [0]


def rr_eng(engs=("act", "dve")):
    _rr[0] += 1
    return engs[_rr[0] % len(engs)]


def copy_op(k, eng, out, in_, reads, writes):
    P = k.P
    if eng == "act":
        P.op("act", lambda e: e.copy(out, in_), reads=reads, writes=writes)
    elif eng == "dve":
        P.op("dve", lambda e: e.tensor_copy(out, in_), reads=reads, writes=writes)
    else:
        P.op("pool", lambda e: e.tensor_copy(out, in_), reads=reads, writes=writes)


def prologue(k, n_layers):
    P, I, S, nc = k.P, k.I, k.S, k.nc
    with contextlib.ExitStack() as st:
        cc = sbuf(k, st, "cc", [16, 128], F32)
        P.dma("sp", cc[:], I["cc"].rearrange("r (c p) -> (r c) p", p=128), writes=[cc.r])
        ccs = sbuf(k, st, "ccs", [16, 128], F32)
        P.op("act", lambda e: e.activation(ccs[:], cc[:], AF.Silu), reads=[cc.r], writes=[ccs.r])
        scT = sbuf(k, st, "scT", [128, 16], F32)
        P.op("pe", lambda e: e.transpose(k.psb[0][:, 0:16], ccs[:], k.idf[0:16, 0:16]), reads=[ccs.r, k.idf.r], writes=[k.psb[0].r])
        P.op("dve", lambda e: e.tensor_copy(scT[:], k.psb[0][:, 0:16]), reads=[k.psb[0].r], writes=[scT.r])
        scR = sbuf(k, st, "scR", [128, 8, 2], F32)
        P.op("dve", lambda e: e.tensor_copy(scR[:], scT[:].rearrange("p (r c) -> p c r", r=2)), reads=[scT.r], writes=[scR.r])
        adab = sbuf(k, st, "adab", [72, 128], F32)
        adabT = sbuf(k, st, "adabT", [128, 72], F32)
        ng = sbuf(k, st, "ng", [24, 128], F32)
        ngT = sbuf(k, st, "ngT", [128, 24], F32)
        modT = sbuf(k, st, "modT", [128, 72, 2], F32)
        slabs = [sbuf(k, st, "adaw%d" % i, [128, 8, 512], F32) for i in range(2)]
        stg = [sbuf(k, st, "cvs%d" % i, [128, 6144], F32) for i in range(3)]
        stb = [sbuf(k, st, "cvb%d" % i, [128, 6144], BF16) for i in range(3)]
        jobs_a, jobs_b = [], []
        for l_ in range(n_layers):
            for j_ in range(2):
                js = [(I["ff_w_gate"][l_, j_], S["wg%d%d" % (l_, j_)], D, FF), (I["ff_w_up"][l_, j_], S["wu%d%d" % (l_, j_)], D, FF),
                      (I["ff_w_down"][l_, j_], S["wd%d%d" % (l_, j_)], FF, D)]
                if j_ == 0:
                    js.append((I["w_in"][l_], S["win%d" % l_], D, INW))
                    js.append((I["w_out"][l_], S["wout%d" % l_], D, D))
                for jb in js:
                    (jobs_a if (l_ == 0 and j_ == 0 and jb[1] is not S["wout0"]) else jobs_b).append(jb)
        cv_iter = (th for jb in jobs_a + jobs_b for th in conv_chunks(k, stg, stb, *jb))
        k.conv_pending = []
        k.growd = TB(nc.dram_tensor("growd", [DEPTH, 2, 3, D], F32, kind="Internal").ap(), "growd")
        growsb = sbuf(k, st, "growsb", [1, 2, 2, 512], F32)
        gbias = sbuf(k, st, "gbias", [1, 3, D], F32)
        for l in range(n_layers):
            P.dma("sp", adab[:], I["ada_b"][l].rearrange("(c p) -> c p", p=128), writes=[adab.r])
            for gi_ in range(3):
                j_ = (2, 5, 8)[gi_]
                P.dma("sp", gbias[0:1, gi_, :], I["ada_b"][l, j_ * D:(j_ + 1) * D].rearrange("(o n) -> o n", o=1), writes=[gbias.r])
            P.op("pe", lambda e: e.transpose(k.psb[1][:, 0:72], adab[:], k.idf[0:72, 0:72]), reads=[adab.r, k.idf.r], writes=[k.psb[1].r])
            P.op("dve", lambda e: e.tensor_copy(adabT[:], k.psb[1][:, 0:72]), reads=[k.psb[1].r], writes=[adabT.r])
            P.dma("sp", ng[:], I["norm_g"][l].rearrange("j (c p) -> (j c) p", p=128), writes=[ng.r])
            P.op("pe", lambda e: e.transpose(k.psb[1][:, 128:152], ng[:], k.idf[0:24, 0:24]), reads=[ng.r, k.idf.r], writes=[k.psb[1].r])
            P.op("dve", lambda e: e.tensor_copy(ngT[:], k.psb[1][:, 128:152]), reads=[k.psb[1].r], writes=[ngT.r])
            for sl in range(18):
                sb_ = slabs[sl % 2]
                P.dma("sp", sb_[:], I["ada_w"][l].rearrange("(c p) n -> p c n", p=128)[:, :, sl * 512:(sl + 1) * 512], writes=[sb_.r])
                pm = k.psb[2 + (sl % 2)]
                for m in range(4):
                    for kc in range(8):
                        P.op("pe", lambda e, sb_=sb_, m=m, kc=kc, pm=pm: e.matmul(pm[:, m * 2:m * 2 + 2], sb_[:, kc, m * 128:(m + 1) * 128], scR[:, kc, :], start=(kc == 0), stop=(kc == 7)),
                             reads=[sb_.r, scR.r], writes=[pm.r])
                P.op("dve", lambda e, pm=pm, sl=sl: e.tensor_tensor(modT[:, sl * 4:sl * 4 + 4, :], pm[:, 0:8].rearrange("p (m r) -> p m r", r=2),
                                                                     adabT[:, sl * 4:sl * 4 + 4].unsqueeze(2).to_broadcast([128, 4, 2]), ALU.add),
                     reads=[pm.r, adabT.r], writes=[modT.r])
                j, half = divmod(sl, 2)
                if j in (2, 5, 8):
                    gi = (2, 5, 8).index(j)
                    for r in range(2):
                        pg = k.psb[4 + r]
                        for kc in range(8):
                            P.op("pe", lambda e, sb_=sb_, kc=kc, pg=pg, r=r: e.matmul(pg[0:1, :], scT[:, r * 8 + kc:r * 8 + kc + 1], sb_[:, kc, :], start=(kc == 0), stop=(kc == 7)),
                                 reads=[sb_.r, scT.r], writes=[pg.r])
                        P.op("dve", lambda e, pg=pg, r=r, half=half, gi=gi: e.tensor_tensor(growsb[0:1, r, half, :], pg[0:1, :], gbias[0:1, gi, half * 512:(half + 1) * 512], ALU.add),
                             reads=[pg.r, gbias.r], writes=[growsb.r])
                    if half == 1:
                        for r in range(2):
                            P.dma("act", k.growd[l, r, gi, :].rearrange("(o h n) -> o h n", o=1, h=2), growsb[0:1, r, :, :], reads=[growsb.r], dwrites=[k.growd.r])
                for _ in range(2):
                    th = next(cv_iter, None)
                    if th is not None:
                        th(("act", "dve"))
            mv = modT[:].rearrange("p (j c) r -> p j c r", c=8)
            for r in range(2):
                for s3 in range(3):
                    P.op("dve", lambda e, r=r, s3=s3, l=l: e.scalar_tensor_tensor(k.scl[:, l, r, s3, :], mv[:, 3 * s3 + 1, :, r], 1.0, ngT[:, s3 * 8:(s3 + 1) * 8], ALU.add, ALU.mult),
                         reads=[modT.r, ngT.r], writes=[k.scl.r])
                    P.op("dve", lambda e, r=r, s3=s3, l=l: e.tensor_copy(k.sh[:, l, r, s3, :], mv[:, 3 * s3, :, r]), reads=[modT.r], writes=[k.sh.r])
        for th in cv_iter:
            th(("act", "dve", "pool"))
        P.barrier()


_cv = [0]


def conv_chunks(k, stg, stb, src, dst, R, C):
    P = k.P
    nr_tot = R // 128
    nr = max(1, min(nr_tot, 6144 // C))
    sv = src.rearrange("(c p) n -> p c n", p=128)
    dv = dst.t.rearrange("(c p) n -> p c n", p=128)
    c0 = 0
    while c0 < nr_tot:
        n = min(nr, nr_tot - c0)

        def th(engs, c0=c0, n=n):
            i = _cv[0] % len(stg)
            _cv[0] += 1
            a, b = stg[i], stb[i]
            av = a[:, 0:n * C].rearrange("p (c n) -> p c n", n=C)
            bv = b[:, 0:n * C].rearrange("p (c n) -> p c n", n=C)
            P.dma("sp", av, sv[:, c0:c0 + n, :], writes=[a.r])
            eng = engs[_cv[0] % len(engs)]
            copy_op(k, eng, b[:, 0:n * C], a[:, 0:n * C], [a.r], [b.r])
            P.dma("act", dv[:, c0:c0 + n, :], bv, reads=[b.r], dwrites=[dst.r])
        yield th
        c0 += n


def alloc_pipe(k, st, n_layers, l):
    P, I = k.P, k.I
    k.hb = [sbuf(k, st, "h_t%d" % i, [128, 4, D], F32) for i in range(2)]
    for hb_ in k.hb:
        hb_.rs = [Res("hsub") for _ in range(4)]
    k.h = k.hb[0]
    k.tokbf = sbuf(k, st, "tokbf", [128, 4, D], BF16)
    k.tokr = [Res("tok%d" % i) for i in range(4)]
    k.uT = sbuf(k, st, "uT", [128, 8, 512], BF16)
    k.actT = sbuf(k, st, "actT", [128, NFC, 512], BF16)
    k.wg = [sbuf(k, st, "wg%d" % i, [128, 8, 512], BF16) for i in range(2)]
    k.wu = [sbuf(k, st, "wu%d" % i, [128, 8, 512], BF16) for i in range(2)]
    k.wd = [sbuf(k, st, "wd%d" % i, [128, D], BF16) for i in range(4)]
    k.win = [sbuf(k, st, "win%d" % i, [128, 8, 512], BF16) for i in range(3)]
    k.wout = sbuf(k, st, "wout", [128, 8, D], BF16)
    k.grow = sbuf(k, st, "grow", [128, 2, 4, D], F32)
    k.stage = [sbuf(k, st, "stg%d" % i, [128, 512], F32) for i in range(4)]
    k.tmp = [sbuf(k, st, "tmp%d" % i, [128, 512], F32) for i in range(2)]
    k.ss = sbuf(k, st, "ss", [128, 8], F32)
    k.mhalf = sbuf(k, st, "mhalf", [128, 4], F32)
    P.op("dve", lambda e: e.memset(k.mhalf[:], -0.5), writes=[k.mhalf.r])
    k.fng = sbuf(k, st, "fng", [128, D], F32)
    k.stg_i = 0
    P.dma("sp", k.fng[:], I["final_norm_g"].partition_broadcast(128), writes=[k.fng.r])
    for r in range(2):
        P.dma("sp", k.grow[:, r, 0:3, :].rearrange("p g n -> p (g n)"), k.growd.t[l, r].rearrange("g n -> (g n)").partition_broadcast(128),
              reads=[k.growd.r], writes=[k.grow.r])
        if l + 1 < n_layers:
            P.dma("sp", k.grow[:, r, 3, :], k.growd.t[l + 1, r, 0, :].partition_broadcast(128), reads=[k.growd.r], writes=[k.grow.r])
    for r in range(2):
        for g in (0, 2, 3):
            P.op("dve", lambda e, r=r, g=g: e.tensor_scalar(k.grow[:, r, g, :], k.grow[:, r, g, :], 0.5, None, ALU.mult), reads=[k.grow.r], writes=[k.grow.r])


def pipeline(k, n_layers):
    P, I, S, nc = k.P, k.I, k.S, k.nc
    tiles = [(0, TC, 1)] + [(TC + i * 512, 512, 0) for i in range(8)]
    with contextlib.ExitStack() as st:
        alloc_pipe(k, st, n_layers, 0)
        load_h(k, k.hb[0], *tiles[0], first=True)
        for i, (r0, nt, ic) in enumerate(tiles):
            k.h = k.hb[i % 2]
            if i + 1 < len(tiles):
                load_h(k, k.hb[(i + 1) % 2], *tiles[i + 1], first=True)
            ffn(k, 0, 0, nt, ic, 0)
            proj_in(k, 0, r0, nt, ic)
            store_h(k, k.h, r0, nt)
        P.barrier()
    if k.stop == "A0":
        return
    for l in range(n_layers):
        mixers(k, l)
        if k.stop == "M%d" % l:
            return
        last = (l == n_layers - 1)
        with contextlib.ExitStack() as st:
            alloc_pipe(k, st, n_layers, l)
            P.dma("sp", k.wout[:], S["wout%d" % l].t.rearrange("(c p) n -> p c n", p=128), reads=[S["wout%d" % l].r], writes=[k.wout.r])
            tl = [t_ for t_ in tiles if not (last and t_[2])]
            load_h(k, k.hb[0], *tl[0], first=False)
            for i, (r0, nt, ic) in enumerate(tl):
                k.h = k.hb[i % 2]
                if i + 1 < len(tl):
                    load_h(k, k.hb[(i + 1) % 2], *tl[i + 1], first=False)
                proj_out(k, l, r0, nt, ic)
                ffn(k, l, 1, nt, ic, 2)
                if last:
                    final_norm(k, r0, nt)
                else:
                    ffn(k, l + 1, 0, nt, ic, 3)
                    proj_in(k, l + 1, r0, nt, ic)
                    store_h(k, k.h, r0, nt)
            P.barrier()


def load_h(k, h, r0, nt, ic, first):
    P, I, S = k.P, k.I, k.S
    ns = nt // 128
    if first:
        src = I["ctx"] if ic else I["x"][r0 - TC:r0 - TC + nt, :]
        P.dma("sp", h[:, 0:ns, :], src.rearrange("(s p) n -> p s n", p=128), writes=list(h.rs[0:ns]))
    else:
        P.dma("sp", h[:, 0:ns, :], S["hbuf"].t[r0:r0 + nt, :].rearrange("(s p) n -> p s n", p=128), reads=[S["hbuf"].r], writes=list(h.rs[0:ns]))


def store_h(k, h, r0, nt):
    P, S = k.P, k.S
    ns = nt // 128
    P.dma("act", S["hbuf"].t[r0:r0 + nt, :].rearrange("(s p) n -> p s n", p=128), h[:, 0:ns, :], reads=list(h.rs[0:ns]), dwrites=[S["hbuf"].r])


def norm_T(k, l, s3, nt, ic):
    P = k.P
    h = k.h
    ns = nt // 128
    for s in range(ns):
        P.op("act", lambda e, s=s: e.activation(k.tokbf[:, s, :], h[:, s, :], AF.Square, accum_out=k.ss[:, s:s + 1]), reads=[h.rs[s]], writes=[k.tokr[s], k.ss.r])
    P.op("dve", lambda e: e.tensor_scalar(k.ss[:, 4:4 + ns], k.ss[:, 0:ns], 1.0 / D, EPS, ALU.mult, ALU.add), reads=[k.ss.r], writes=[k.ss.r])
    P.op("pool", lambda e: e.tensor_tensor(k.ss[:, 4:4 + ns], k.ss[:, 4:4 + ns], k.mhalf[:, 0:ns], ALU.pow), reads=[k.ss.r, k.mhalf.r], writes=[k.ss.r])
    for s in range(ns):
        if s % 2 == 0:
            P.op("act", lambda e, s=s: e.activation(k.tokbf[:, s, :], h[:, s, :], AF.Copy, scale=k.ss[:, 4 + s:5 + s]), reads=[h.rs[s], k.ss.r], writes=[k.tokr[s]])
        else:
            P.op("dve", lambda e, s=s: e.tensor_scalar(k.tokbf[:, s, :], h[:, s, :], k.ss[:, 4 + s:5 + s], None, ALU.mult), reads=[h.rs[s], k.ss.r], writes=[k.tokr[s]])
    transpose_T(k, nt, (k.scl[:, l, ic, s3, :], k.sh[:, l, ic, s3, :]))


def transpose_T(k, nt, mod):
    P = k.P
    ns = nt // 128
    for kc in range(8):
        pb = k.psb[kc % 4]
        pv = pb[:].bitcast(BF16)
        for s in range(ns):
            P.op("pe", lambda e, s=s, kc=kc, pv=pv: e.transpose(pv[:, s * 128:(s + 1) * 128], k.tokbf[:, s, kc * 128:(kc + 1) * 128], k.idb[:]),
                 reads=[k.tokr[s], k.idb.r], writes=[pb.r])
        eng = ("act", "dve")[kc % 2]
        if mod is None:
            copy_op(k, eng, k.uT[:, kc, 0:nt], pv[:, 0:nt], [pb.r], [k.uT.r])
        else:
            scl, sh = mod
            if eng == "act":
                P.op("act", lambda e, kc=kc, pv=pv: e.activation(k.uT[:, kc, 0:nt], pv[:, 0:nt], AF.Identity, bias=sh[:, kc:kc + 1], scale=scl[:, kc:kc + 1]),
                     reads=[pb.r, k.scl.r, k.sh.r], writes=[k.uT.r])
            else:
                P.op("dve", lambda e, kc=kc, pv=pv: e.tensor_scalar(k.uT[:, kc, 0:nt], pv[:, 0:nt], scl[:, kc:kc + 1], sh[:, kc:kc + 1], ALU.mult, ALU.add),
                     reads=[pb.r, k.scl.r, k.sh.r], writes=[k.uT.r])


def ffn(k, l, j, nt, ic, gslot):
    P, S = k.P, k.S
    ns = nt // 128
    s3 = 0 if j == 0 else 2
    norm_T(k, l, s3, nt, ic)
    wgs, wus, wds = S["wg%d%d" % (l, j)], S["wu%d%d" % (l, j)], S["wd%d%d" % (l, j)]
    wgv = wgs.t.rearrange("(c p) n -> p c n", p=128)
    wuv = wus.t.rearrange("(c p) n -> p c n", p=128)
    nsl = 6
    for sl in range(nsl):
        c0 = sl * 512
        ncol = min(512, FF - c0)
        wg, wu = k.wg[sl % 2], k.wu[sl % 2]
        P.dma("sp", wg[:, :, 0:ncol], wgv[:, :, c0:c0 + ncol], reads=[wgs.r], writes=[wg.r])
        P.dma("sp", wu[:, :, 0:ncol], wuv[:, :, c0:c0 + ncol], reads=[wus.r], writes=[wu.r])
        for f4 in range(ncol // 128):
            fc = sl * 4 + f4
            pg, pu = k.psb[4 + 2 * (fc % 2)], k.psb[5 + 2 * (fc % 2)]
            for kc in range(8):
                P.op("pe", lambda e, wg=wg, kc=kc, f4=f4, pg=pg: e.matmul(pg[:, 0:nt], wg[:, kc, f4 * 128:(f4 + 1) * 128], k.uT[:, kc, 0:nt], start=(kc == 0), stop=(kc == 7)),
                     reads=[wg.r, k.uT.r], writes=[pg.r])
            for kc in range(8):
                P.op("pe", lambda e, wu=wu, kc=kc, f4=f4, pu=pu: e.matmul(pu[:, 0:nt], wu[:, kc, f4 * 128:(f4 + 1) * 128], k.uT[:, kc, 0:nt], start=(kc == 0), stop=(kc == 7)),
                     reads=[wu.r, k.uT.r], writes=[pu.r])
            tm = k.tmp[fc % 2]
            P.op("act", lambda e, pg=pg, tm=tm: e.activation(tm[:, 0:nt], pg[:, 0:nt], AF.Silu), reads=[pg.r], writes=[tm.r])
            P.op("dve", lambda e, pu=pu, tm=tm, fc=fc: e.tensor_tensor(k.actT[:, fc, 0:nt], tm[:, 0:nt], pu[:, 0:nt], ALU.mult), reads=[pu.r, tm.r], writes=[k.actT.r])
    for fc in range(NFC):
        wd = k.wd[fc % 4]
        P.dma("sp", wd[:], wds.t[fc * 128:(fc + 1) * 128, :], reads=[wds.r], writes=[wd.r])
        for s in range(ns):
            for hf in range(2):
                pb = k.psb[s * 2 + hf]
                P.op("pe", lambda e, wd=wd, s=s, hf=hf, fc=fc, pb=pb: e.matmul(pb[:], k.actT[:, fc, s * 128:(s + 1) * 128], wd[:, hf * 512:(hf + 1) * 512], start=(fc == 0), stop=(fc == NFC - 1)),
                     reads=[k.actT.r, wd.r], writes=[pb.r])
    residual(k, ns, ic, gslot)


def residual(k, ns, ic, gi):
    P = k.P
    h = k.h
    for s in range(ns):
        for hf in range(2):
            pb = k.psb[s * 2 + hf]
            tm = k.tmp[(s * 2 + hf) % 2]
            P.op("dve", lambda e, pb=pb, tm=tm, hf=hf: e.tensor_tensor(tm[:], pb[:], k.grow[:, ic, gi, hf * 512:(hf + 1) * 512], ALU.mult), reads=[pb.r, k.grow.r], writes=[tm.r])
            P.op(("pool", "dve")[(s * 2 + hf) % 2], lambda e, tm=tm, s=s, hf=hf: e.tensor_tensor(h[:, s, hf * 512:(hf + 1) * 512], h[:, s, hf * 512:(hf + 1) * 512], tm[:], ALU.add), reads=[tm.r, h.rs[s]], writes=[h.rs[s]])


def proj_in(k, l, r0, nt, ic):
    P, S = k.P, k.S
    ns = nt // 128
    norm_T(k, l, 1, nt, ic)
    ws = S["win%d" % l]
    wv = ws.t.rearrange("(c p) n -> p c n", p=128)
    nb = 0
    for sl in range(6):
        c0 = sl * 512
        ncol = min(512, INW - c0)
        w = k.win[sl % 3]
        P.dma("sp", w[:, :, 0:ncol], wv[:, :, c0:c0 + ncol], reads=[ws.r], writes=[w.r])
        for s in range(ns):
            pb = k.psb[nb % 8]
            nb += 1
            for kc in range(8):
                P.op("pe", lambda e, w=w, kc=kc, s=s, pb=pb, ncol=ncol: e.matmul(pb[:, 0:ncol], k.uT[:, kc, s * 128:(s + 1) * 128], w[:, kc, 0:ncol], start=(kc == 0), stop=(kc == 7)),
                     reads=[w.r, k.uT.r], writes=[pb.r])
            sg = k.stage[k.stg_i % 4]
            k.stg_i += 1
            copy_op(k, "act", sg[:, 0:ncol], pb[:, 0:ncol], [pb.r], [sg.r])
            P.dma("act", S["tm"].t[r0 + s * 128:r0 + (s + 1) * 128, c0:c0 + ncol], sg[:, 0:ncol], reads=[sg.r], dwrites=[S["tm"].r])
        fml = []
        if sl == 0:
            fml = [(0, 128, 0), (128, 128, 128), (256, 128, 256)]
        elif sl == 2:
            fml = [(GL0 - 1024, 32, 384)]
        for (cs, n, fr) in fml:
            pb = k.psb[nb % 8]
            nb += 1
            for kc in range(8):
                P.op("pe", lambda e, w=w, kc=kc, pb=pb, cs=cs, n=n: e.matmul(pb[0:n, 0:nt], w[:, kc, cs:cs + n], k.uT[:, kc, 0:nt], start=(kc == 0), stop=(kc == 7)),
                     reads=[w.r, k.uT.r], writes=[pb.r])
            sg = k.stage[k.stg_i % 4]
            k.stg_i += 1
            copy_op(k, "act", sg[0:n, 0:nt], pb[0:n, 0:nt], [pb.r], [sg.r])
            P.dma("act", S["fm"].t[fr:fr + n, r0:r0 + nt], sg[0:n, 0:nt], reads=[sg.r], dwrites=[S["fm"].r])


def proj_out(k, l, r0, nt, ic):
    P, S = k.P, k.S
    ns = nt // 128
    P.dma("sp", k.tokbf[:, 0:ns, :], S["mix"].t[r0:r0 + nt, :].rearrange("(s p) n -> p s n", p=128), reads=[S["mix"].r], writes=list(k.tokr[0:ns]))
    transpose_T(k, nt, None)
    for s in range(ns):
        for hf in range(2):
            pb = k.psb[s * 2 + hf]
            for kc in range(8):
                P.op("pe", lambda e, s=s, hf=hf, kc=kc, pb=pb: e.matmul(pb[:], k.uT[:, kc, s * 128:(s + 1) * 128], k.wout[:, kc, hf * 512:(hf + 1) * 512], start=(kc == 0), stop=(kc == 7)),
                     reads=[k.uT.r, k.wout.r], writes=[pb.r])
    residual(k, ns, ic, 1)


def final_norm(k, r0, nt):
    P = k.P
    h = k.h
    ns = nt // 128
    for s in range(ns):
        P.op("act", lambda e, s=s: e.activation(k.tokbf[:, s, :], h[:, s, :], AF.Square, accum_out=k.ss[:, s:s + 1]), reads=[h.rs[s]], writes=[k.tokr[s], k.ss.r])
    P.op("dve", lambda e: e.tensor_scalar(k.ss[:, 4:4 + ns], k.ss[:, 0:ns], 1.0 / D, EPS, ALU.mult, ALU.add), reads=[k.ss.r], writes=[k.ss.r])
    P.op("pool", lambda e: e.tensor_tensor(k.ss[:, 4:4 + ns], k.ss[:, 4:4 + ns], k.mhalf[:, 0:ns], ALU.pow), reads=[k.ss.r, k.mhalf.r], writes=[k.ss.r])
    for s in range(ns):
        eng = "dve"
        P.op(eng, lambda e, s=s: e.scalar_tensor_tensor(h[:, s, :], h[:, s, :], k.ss[:, 4 + s:5 + s], k.fng[:], ALU.mult, ALU.mult), reads=[h.rs[s], k.ss.r, k.fng.r], writes=[h.rs[s]])
    P.dma("act", k.out[r0 - TC:r0 - TC + nt, :].rearrange("(s p) n -> p s n", p=128), h[:, 0:ns, :], reads=list(h.rs[0:ns]))


def dbg_dump(k, name, tb, shape, dt=F32):
    if name not in k.debug:
        return
    d = k.nc.dram_tensor(name, list(shape), dt, kind="ExternalOutput").ap()
    flat = "p " + " ".join("a%d" % i for i in range(len(shape) - 1)) + " -> p (" + " ".join("a%d" % i for i in range(len(shape) - 1)) + ")"
    k.P.dma("sp", d.rearrange(flat) if len(shape) > 2 else d, tb[:].rearrange(flat) if len(shape) > 2 else tb[:], reads=[tb.r])


def mixers(k, l):
    with contextlib.ExitStack() as st:
        gens = []
        if "gla" in k.mix_on:
            gens.append(gla_gen(k, l, st))
        if "ssd" in k.mix_on:
            gens.append(ssd_gen(k, l, st))
        while gens:
            for g_ in list(gens):
                try:
                    next(g_)
                except StopIteration:
                    gens.remove(g_)
        k.P.barrier()
    if "s5" in k.mix_on:
        s5(k, l)


C_MASKF, C_MASKR, C_INCLF, C_INCLR, C_SUFX, C_PREX = 0, 1, 2, 3, 4, 5
NB = T // 128


def gla_gen(k, l, st):
    P, I, S, nc = k.P, k.I, k.S, k.nc
    if True:
        cm = sbuf(k, st, "gcm", [128, 6, 128], F32)
        P.dma("sp", cm[:], I["cst_m"][0:6].rearrange("c p n -> p c n"), writes=[cm.r])
        wgp = sbuf(k, st, "wgp", [33, 512], F32)
        P.op("dve", lambda e: e.memset(wgp[:], 0.0), writes=[wgp.r])
        for d in range(2):
            P.dma("sp", wgp[d * 16:(d + 1) * 16, d * 256:(d + 1) * 256].rearrange("p (h c) -> p h c", c=64)[:, :, 0:48],
                  I["gla_w_gate"][l, d].rearrange("p (h c) -> p h c", c=48), writes=[wgp.r])
            P.dma("sp", wgp[32:33, d * 256:(d + 1) * 256].rearrange("p (h c) -> p h c", c=64)[:, :, 0:48],
                  I["gla_b_gate"][l, d].rearrange("(o h c) -> o h c", o=1, c=48), writes=[wgp.r])
        gng = sbuf(k, st, "gng", [128, 384], F32)
        mhalf = sbuf(k, st, "gmhalf", [128, 4], F32)
        P.op("dve", lambda e: e.memset(mhalf[:], -0.5), writes=[mhalf.r])
        P.dma("sp", gng[:], I["gla_norm_g"][l].partition_broadcast(128), writes=[gng.r])
        o_res = [Res("glao%d" % b) for b in range(NB)]
        X = []
        for d in range(2):
            x = K()
            x.qT = [sbuf(k, st, "qT%d%d" % (d, i), [64, 4, 128], F32) for i in range(1)] * 2
            x.kT = [sbuf(k, st, "kT%d%d" % (d, i), [64, 4, 128], F32) for i in range(1)] * 2
            x.glr = [sbuf(k, st, "glr%d%d" % (d, i), [33, 128], F32) for i in range(1)] * 2
            x.tok = [sbuf(k, st, "tok%d%d" % (d, i), [128, 576], F32) for i in range(1)] * 2
            x.rt = sbuf(k, st, "rt%d" % d, [128, 384], F32)
            for i in range(2):
                P.op("pool", lambda e, t=x.qT[i]: e.memset(t[:], 0.0), writes=[x.qT[i].r])
                P.op("pool", lambda e, t=x.kT[i]: e.memset(t[:], 0.0), writes=[x.kT[i].r])
                P.op("pool", lambda e, t=x.glr[i]: e.memset(t[:], 1.0), writes=[x.glr[i].r])
            x.ex = sbuf(k, st, "ex%d" % d, [128, 256], F32)
            x.la = sbuf(k, st, "la%d" % d, [128, 256], F32)
            x.ecum = sbuf(k, st, "ecum%d" % d, [64, 4, 128], F32)
            x.eneg = sbuf(k, st, "eneg%d" % d, [64, 4, 128], F32)
            x.ekend = sbuf(k, st, "ekend%d" % d, [128, 256], F32)
            x.att = sbuf(k, st, "att%d" % d, [128, 4, 128], BF16)
            x.S = sbuf(k, st, "S%d" % d, [64, 4, 96], F32)
            x.Sbf = sbuf(k, st, "Sbf%d" % d, [64, 4, 96], BF16)
            x.pp = []
            for i in range(2):
                y = K()
                y.qdec = sbuf(k, st, "qdec%d%d" % (d, i), [64, 4, 128], BF16)
                y.kinv = sbuf(k, st, "kinv%d%d" % (d, i), [64, 4, 128], BF16)
                y.kend = sbuf(k, st, "kend%d%d" % (d, i), [128, 4, 48], BF16)
                y.vbf = sbuf(k, st, "vbf%d%d" % (d, i), [128, 384], BF16)
                y.dec = sbuf(k, st, "dec%d%d" % (d, i), [64, 4], F32)
                x.pp.append(y)
            x.ost = sbuf(k, st, "ost%d" % d, [128, 384], F32)
            x.oprev = sbuf(k, st, "oprev%d" % d, [128, 384], F32)
            x.sq = sbuf(k, st, "sq%d" % d, [128, 96], F32)
            x.ss = sbuf(k, st, "gss%d" % d, [128, 8], F32)
            x.sr = sbuf(k, st, "sr%d" % d, [128, 384], F32)
            x.y = sbuf(k, st, "gy%d" % d, [128, 384], BF16)
            P.op("dve", lambda e, x=x: e.memset(x.S[:], 0.0), writes=[x.S.r])
            P.op("dve", lambda e, x=x: e.memset(x.Sbf[:], 0.0), writes=[x.Sbf.r])
            x.pA, x.pB, x.pC, x.pD = [k.psb[d * 4 + i] for i in range(4)]
            X.append(x)
        fseq = list(range(NB))
        rseq = [1, 0] + list(range(NB - 1, 1, -1))
        visited = set()
        cv_iter = None
        if k.conv_pending:
            cstg = [sbuf(k, st, "gcvs%d" % i, [128, 6144], F32) for i in range(2)]
            cstb = [sbuf(k, st, "gcvb%d" % i, [128, 6144], BF16) for i in range(2)]
            pend, k.conv_pending = k.conv_pending, []
            cv_iter = (th for jb in pend for th in conv_chunks(k, cstg, cstb, *jb))
        def prep(d, step):
            b = (fseq, rseq)[d][step]
            x = X[d]
            par = step % 2
            y = x.pp[par]
            t0 = b * 128
            qT, kT, glr, tok = x.qT[par], x.kT[par], x.glr[par], x.tok[par]
            P.dma("sp", qT[0:48, :, :], S["fm"].t[0:192, t0:t0 + 128].rearrange("(h c) n -> c h n", c=48), reads=[S["fm"].r], writes=[qT.r])
            P.dma("sp", kT[0:48, :, :], S["fm"].t[192:384, t0:t0 + 128].rearrange("(h c) n -> c h n", c=48), reads=[S["fm"].r], writes=[kT.r])
            P.dma("sp", glr[0:32, :], S["fm"].t[384:416, t0:t0 + 128], reads=[S["fm"].r], writes=[glr.r])
            P.dma("sp", tok[:], S["tm"].t[t0:t0 + 128, K0:K0 + 576], reads=[S["tm"].r], writes=[tok.r])
            P.op("pe", lambda e, x=x, y=y, glr=glr, d=d: e.matmul(x.pA[:, 0:256], glr[0:33, :], wgp[0:33, d * 256:(d + 1) * 256], start=True, stop=True),
                 reads=[glr.r, wgp.r], writes=[x.pA.r])
            P.op("act", lambda e, x=x, y=y: e.activation(x.ex[:], x.pA[:, 0:256], AF.Exp, scale=-1.0), reads=[x.pA.r], writes=[x.ex.r])
            P.op("act", lambda e, x=x, y=y: e.activation(x.la[:], x.ex[:], AF.Ln, bias=1.0), reads=[x.ex.r], writes=[x.la.r])
            for h in range(4):
                P.op("pe", lambda e, x=x, y=y, h=h, d=d: e.matmul(x.pB[0:64, h * 128:(h + 1) * 128], x.la[:, h * 64:(h + 1) * 64], cm[:, C_INCLF + d, :], start=True, stop=True),
                     reads=[x.la.r, cm.r], writes=[x.pB.r])
            P.op("pe", lambda e, x=x, y=y, d=d: e.matmul(x.pA[:, 256:512], cm[:, C_SUFX + d, :], x.la[:], start=True, stop=True), reads=[x.la.r, cm.r], writes=[x.pA.r])
            P.op("act", lambda e, x=x, y=y: e.activation(x.ecum[:].rearrange("p g n -> p (g n)"), x.pB[0:64, :], AF.Exp), reads=[x.pB.r], writes=[x.ecum.r])
            P.op("act", lambda e, x=x, y=y: e.activation(x.eneg[:].rearrange("p g n -> p (g n)"), x.pB[0:64, :], AF.Exp, scale=-1.0), reads=[x.pB.r], writes=[x.eneg.r])
            P.op("act", lambda e, x=x, y=y: e.activation(x.ekend[:], x.pA[:, 256:512], AF.Exp), reads=[x.pA.r], writes=[x.ekend.r])
            P.op("dve", lambda e, x=x, y=y, qT=qT: e.scalar_tensor_tensor(y.qdec[:], qT[:], 48.0 ** -0.5, x.ecum[:], ALU.mult, ALU.mult), reads=[qT.r, x.ecum.r], writes=[y.qdec.r])
            P.op("pool", lambda e, x=x, y=y, kT=kT: e.tensor_tensor(y.kinv[:], kT[:], x.eneg[:], ALU.mult), reads=[kT.r, x.eneg.r], writes=[y.kinv.r])
            P.op("dve", lambda e, x=x, y=y, tok=tok: e.tensor_tensor(y.kend[:], tok[:, 0:192].rearrange("p (h c) -> p h c", c=48),
                                                                  x.ekend[:].rearrange("p (h c) -> p h c", c=64)[:, :, 0:48], ALU.mult), reads=[tok.r, x.ekend.r], writes=[y.kend.r])
            P.op("pool", lambda e, x=x, y=y, tok=tok: e.tensor_copy(y.vbf[:], tok[:, 192:576]), reads=[tok.r], writes=[y.vbf.r])
            lastcol = 127 if d == 0 else 0
            P.op("dve", lambda e, x=x, y=y, lastcol=lastcol: e.tensor_copy(y.dec[:], x.ecum[:, :, lastcol]), reads=[x.ecum.r], writes=[y.dec.r])

        def scan(d, step):
            b = (fseq, rseq)[d][step]
            x = X[d]
            par = step % 2
            y = x.pp[par]
            t0 = b * 128
            qT, kT, glr, tok = x.qT[par], x.kT[par], x.glr[par], x.tok[par]
            for h in range(4):
                g, hh = divmod(h, 2)
                P.op("pe", lambda e, x=x, y=y, g=g, hh=hh, h=h: e.matmul(x.pC[:, h * 128:(h + 1) * 128], y.kinv[0:48, h, :], y.qdec[0:48, h, :], start=True, stop=True),
                     reads=[y.kinv.r, y.qdec.r], writes=[x.pC.r])
            P.op("dve", lambda e, x=x, y=y, d=d: e.tensor_tensor(x.att[:], x.pC[:].rearrange("p (h n) -> p h n", n=128), cm[:, C_MASKF + d, :].unsqueeze(1).to_broadcast([128, 4, 128]), ALU.mult),
                 reads=[x.pC.r, cm.r], writes=[x.att.r])
            for h in range(4):
                g, hh = divmod(h, 2)
                P.op("pe", lambda e, x=x, y=y, h=h: e.matmul(x.pD[:, h * 96:(h + 1) * 96], x.att[:, h, :], y.vbf[:, h * 96:(h + 1) * 96], start=True, stop=False),
                     reads=[x.att.r, y.vbf.r], writes=[x.pD.r])
                P.op("pe", lambda e, x=x, y=y, h=h, g=g, hh=hh: e.matmul(x.pD[:, h * 96:(h + 1) * 96], y.qdec[0:48, h, :], x.Sbf[0:48, h, :], start=False, stop=True),
                     reads=[y.qdec.r, x.Sbf.r], writes=[x.pD.r])
            for h in range(4):
                g, hh = divmod(h, 2)
                P.op("pe", lambda e, x=x, y=y, h=h, g=g, hh=hh: e.matmul(x.pC[0:48, h * 96:(h + 1) * 96], y.kend[:, h, :], y.vbf[:, h * 96:(h + 1) * 96], start=True, stop=True),
                     reads=[y.kend.r, y.vbf.r], writes=[x.pC.r])
            for h in range(4):
                P.op("dve", lambda e, x=x, y=y, h=h: e.scalar_tensor_tensor(x.S[0:48, h, :], x.S[0:48, h, :], y.dec[0:48, h:h + 1], x.pC[0:48, h * 96:(h + 1) * 96], ALU.mult, ALU.add),
                     reads=[x.S.r, y.dec.r, x.pC.r], writes=[x.S.r])
            P.op("act", lambda e, x=x, y=y: e.copy(x.Sbf[:], x.S[:]), reads=[x.S.r], writes=[x.Sbf.r])
            if b not in visited:
                visited.add(b)
                P.op("act", lambda e, x=x, y=y: e.copy(x.ost[:], x.pD[:, 0:384]), reads=[x.pD.r], writes=[x.ost.r])
                P.dma("act", S["gla_o"].t[t0:t0 + 128, :], x.ost[:], reads=[x.ost.r], writes=[o_res[b]])
            else:
                if l == DEPTH - 1 and b < 2:
                    return
                P.dma("sp", x.oprev[:], S["gla_o"].t[t0:t0 + 128, :], reads=[o_res[b]], writes=[x.oprev.r])
                P.op("dve", lambda e, x=x, y=y: e.tensor_tensor(x.ost[:], x.pD[:, 0:384], x.oprev[:], ALU.add), reads=[x.pD.r, x.oprev.r], writes=[x.ost.r])
                for h in range(4):
                    P.op("act", lambda e, x=x, y=y, h=h: e.activation(x.sq[:], x.ost[:, h * 96:(h + 1) * 96], AF.Square, accum_out=x.ss[:, h:h + 1]), reads=[x.ost.r], writes=[x.sq.r, x.ss.r])
                P.op("dve", lambda e, x=x, y=y: e.tensor_scalar(x.ss[:, 4:8], x.ss[:, 0:4], 1.0 / 96, EPS, ALU.mult, ALU.add), reads=[x.ss.r], writes=[x.ss.r])
                P.op("pool", lambda e, x=x, y=y: e.tensor_tensor(x.ss[:, 4:8], x.ss[:, 4:8], mhalf[:, 0:4], ALU.pow), reads=[x.ss.r, mhalf.r], writes=[x.ss.r])
                P.dma("sp", x.rt[:], S["tm"].t[t0:t0 + 128, R0:R0 + 384], reads=[S["tm"].r], writes=[x.rt.r])
                P.op("act", lambda e, x=x, y=y: e.activation(x.sr[:], x.rt[:], AF.Exp, scale=-1.0), reads=[x.rt.r], writes=[x.sr.r])
                P.op("pool", lambda e, x=x, y=y: e.tensor_tensor(x.rt[:], x.rt[:], gng[:], ALU.mult), reads=[x.rt.r, gng.r], writes=[x.rt.r])
                P.op("dve", lambda e, x=x, y=y: e.tensor_scalar(x.sr[:], x.sr[:], 1.0, None, ALU.add), reads=[x.sr.r], writes=[x.sr.r])
                P.op("dve", lambda e, x=x, y=y: e.reciprocal(x.sr[:], x.sr[:]), reads=[x.sr.r], writes=[x.sr.r])
                P.op("dve", lambda e, x=x, y=y: e.tensor_tensor(x.sr[:], x.sr[:], x.rt[:], ALU.mult), reads=[x.sr.r, x.rt.r], writes=[x.sr.r])
                P.op("dve", lambda e, x=x, y=y: e.tensor_tensor(x.ost[:].rearrange("p (h c) -> p h c", c=96), x.ost[:].rearrange("p (h c) -> p h c", c=96),
                                                              x.ss[:, 4:8].unsqueeze(2).to_broadcast([128, 4, 96]), ALU.mult), reads=[x.ost.r, x.ss.r], writes=[x.ost.r])
                P.op("dve", lambda e, x=x, y=y: e.tensor_tensor(x.y[:], x.ost[:], x.sr[:], ALU.mult), reads=[x.ost.r, x.sr.r], writes=[x.y.r])
                P.dma("act", S["mix"].t[t0:t0 + 128, 0:384], x.y[:], reads=[x.y.r], dwrites=[S["mix"].r])

        for d in range(2):
            prep(d, 0)
        for step in range(NB):
            for d in range(2):
                if cv_iter is not None:
                    th = next(cv_iter, None)
                    if th is not None:
                        th(("act",))
                if step + 1 < NB:
                    prep(d, step + 1)
                    yield
            for d in range(2):
                scan(d, step)
                yield
        if cv_iter is not None:
            for th in cv_iter:
                th(("act", "dve"))
        yield


C_STRF, C_STRR = 6, 7


def ssd_gen(k, l, st):
    P, I, S, nc = k.P, k.I, k.S, k.nc
    tm = S["tm"]
    tml = tm.t[TC:, :].rearrange("(r c) n -> c r n", c=64)
    mixl = S["mix"].t[TC:, :].rearrange("(r c) n -> c r n", c=64)

    def rows(view_c, view_l, b, c0, n):
        if b < 2:
            return [(0, 128, view_c[b * 128:(b + 1) * 128, c0:c0 + n])]
        bb = b - 2
        return [(0, 64, view_l[2 * bb][:, c0:c0 + n]), (64, 64, view_l[2 * bb + 1][:, c0:c0 + n])]

    if True:
        cm = sbuf(k, st, "scm", [128, 8, 128], F32)
        P.dma("sp", cm[:], I["cst_m"][0:8].rearrange("c p n -> p c n"), writes=[cm.r])
        maskb = sbuf(k, st, "maskb", [128, 2, 128], BF16)
        P.op("dve", lambda e: e.tensor_copy(maskb[:], cm[:, 0:2, :]), reads=[cm.r], writes=[maskb.r])
        cwr = sbuf(k, st, "cwr", [8, 896], F32)
        P.op("dve", lambda e: e.memset(cwr[:], 0.0), writes=[cwr.r])
        P.dma("sp", cwr[0:5, :], I["ssd_conv_w"][l], writes=[cwr.r])
        P.dma("sp", cwr[5:6, :], I["ssd_conv_b"][l].rearrange("(o n) -> o n", o=1), writes=[cwr.r])
        cw = sbuf(k, st, "cw", [128, 7, 8], F32)
        for cc in range(7):
            P.op("pe", lambda e, cc=cc: e.transpose(k.psb[0][:, cc * 8:(cc + 1) * 8], cwr[0:8, cc * 128:(cc + 1) * 128], k.idf[0:8, 0:8]), reads=[cwr.r, k.idf.r], writes=[k.psb[0].r])
        P.op("dve", lambda e: e.tensor_copy(cw[:].rearrange("p c k -> p (c k)"), k.psb[0][:, 0:56]), reads=[k.psb[0].r], writes=[cw.r])
        dtb = sbuf(k, st, "dtb", [128, 12], F32)
        negA = sbuf(k, st, "negA", [128, 12], F32)
        dsk = sbuf(k, st, "dsk", [128, 6], F32)
        sng = sbuf(k, st, "sng", [128, 384], F32)
        smhalf = sbuf(k, st, "smhalf", [128, 4], F32)
        P.op("dve", lambda e: e.memset(smhalf[:], -0.5), writes=[smhalf.r])
        P.dma("sp", dtb[:], I["ssd_dt_bias"][l].rearrange("d h -> (d h)").partition_broadcast(128), writes=[dtb.r])
        P.dma("sp", negA[:], I["ssd_a_log"][l].rearrange("d h -> (d h)").partition_broadcast(128), writes=[negA.r])
        P.dma("sp", dsk[:], I["ssd_d"][l].partition_broadcast(128), writes=[dsk.r])
        P.dma("sp", sng[:], I["ssd_norm_g"][l].partition_broadcast(128), writes=[sng.r])
        P.op("act", lambda e: e.activation(negA[:], negA[:], AF.Exp), reads=[negA.r], writes=[negA.r])
        P.op("dve", lambda e: e.tensor_scalar(negA[:], negA[:], -1.0, None, ALU.mult), reads=[negA.r], writes=[negA.r])
        xs_tm = sbuf(k, st, "xs_tm", [128, NB, 384], BF16)
        bm_tm = sbuf(k, st, "bm_tm", [128, NB, 256], BF16)
        bmT = sbuf(k, st, "bmT", [128, 2, T], BF16)
        cmT = sbuf(k, st, "cmT", [128, 2, T], BF16)
        dt_all = sbuf(k, st, "dt_all", [128, NB, 12], F32)
        la_all = sbuf(k, st, "la_all", [128, NB, 12], F32)
        with contextlib.ExitStack() as st1:
            xT = [sbuf(k, st1, "xTs%d" % i, [128, 7, 516], F32) for i in range(2)]
            xld = [sbuf(k, st1, "xld%d" % i, [128, 908], F32) for i in range(2)]
            acc = [sbuf(k, st1, "cacc%d" % i, [128, 512], F32) for i in range(2)]
            xsT = [sbuf(k, st1, "xsT%d" % i, [128, 512], F32) for i in range(2)]
            dte = sbuf(k, st1, "dte", [128, 12], F32)
            segs = [[0, 1]] + [[2 + 4 * s_ + j for j in range(4)] for s_ in range(8)]
            nld = [0]

            def stageA(si):
                seg = segs[si]
                buf = xT[si % 2]
                first = si in (0, 1)
                if first:
                    P.op("pool", lambda e, buf=buf: e.memset(buf[:, :, 0:2], 0.0), writes=[buf.r])
                else:
                    pb_ = xT[(si - 1) % 2]
                    P.op("pool", lambda e, buf=buf, pb_=pb_: e.tensor_copy(buf[:, :, 0:2], pb_[:, :, 512:514]), reads=[pb_.r], writes=[buf.r])
                for j, b in enumerate(seg):
                    xl = xld[nld[0] % 2]
                    nld[0] += 1
                    for (p0, np_, ap) in rows(tm.t, tml, b, XBC0, 908):
                        P.dma("sp", xl[p0:p0 + np_, :], ap, reads=[tm.r], writes=[xl.r])
                    P.op("dve", lambda e, xl=xl: e.tensor_tensor(dte[:], xl[:, 896:908], dtb[:], ALU.add), reads=[xl.r, dtb.r], writes=[dte.r])
                    P.op("act", lambda e: e.activation(dte[:], dte[:], AF.Exp), reads=[dte.r], writes=[dte.r])
                    P.op("act", lambda e, b=b: e.activation(dt_all[:, b, :], dte[:], AF.Ln, bias=1.0), reads=[dte.r], writes=[dt_all.r])
                    P.op("dve", lambda e, b=b: e.tensor_tensor(la_all[:, b, :], dt_all[:, b, :], negA[:], ALU.mult), reads=[dt_all.r, negA.r], writes=[la_all.r])
                    for cc in range(7):
                        pb = k.psb[1 + (cc % 2)]
                        P.op("pe", lambda e, xl=xl, cc=cc, pb=pb: e.transpose(pb[:, 0:128], xl[:, cc * 128:(cc + 1) * 128], k.idf[:]), reads=[xl.r, k.idf.r], writes=[pb.r])
                        copy_op(k, ("act", "dve")[cc % 2], buf[:, cc, 2 + j * 128:2 + (j + 1) * 128], pb[:, 0:128], [pb.r], [buf.r])

            def stageB(si):
                seg = segs[si]
                buf = xT[si % 2]
                N = 128 * len(seg)
                last = si in (0, 8)
                if last:
                    P.op("pool", lambda e, buf=buf, N=N: e.memset(buf[:, :, 2 + N:4 + N], 0.0), writes=[buf.r])
                else:
                    nb_ = xT[(si + 1) % 2]
                    P.op("pool", lambda e, buf=buf, nb_=nb_, N=N: e.tensor_copy(buf[:, :, 2 + N:4 + N], nb_[:, :, 2:4]), reads=[nb_.r], writes=[buf.r])
                t0 = seg[0] * 128
                for cc2 in range(0, 7, 2):
                    ccs = [c_ for c_ in (cc2, cc2 + 1) if c_ < 7]
                    for cc in ccs:
                        ac = acc[cc % 2]
                        P.op("dve", lambda e, ac=ac, buf=buf, cc=cc, N=N: e.tensor_scalar(ac[:, 0:N], buf[:, cc, 0:N], cw[:, cc, 0:1], None, ALU.mult), reads=[buf.r, cw.r], writes=[ac.r])
                    for kk in range(1, 5):
                        for cc in ccs:
                            ac = acc[cc % 2]
                            P.op("dve", lambda e, ac=ac, buf=buf, cc=cc, N=N, kk=kk: e.scalar_tensor_tensor(ac[:, 0:N], buf[:, cc, kk:kk + N], cw[:, cc, kk:kk + 1], ac[:, 0:N], ALU.mult, ALU.add),
                                 reads=[buf.r, cw.r, ac.r], writes=[ac.r])
                    for cc in ccs:
                        ac = acc[cc % 2]
                        if cc < 3:
                            xo = xsT[cc % 2]
                            P.op("act", lambda e, ac=ac, xo=xo, cc=cc, N=N: e.activation(xo[:, 0:N], ac[:, 0:N], AF.Silu, bias=cw[:, cc, 5:6]), reads=[ac.r, cw.r], writes=[xo.r])
                            for j, b in enumerate(seg):
                                pb = k.psb[3 + (j % 2)]
                                P.op("pe", lambda e, xo=xo, j=j, pb=pb: e.transpose(pb[:, 0:128], xo[:, j * 128:(j + 1) * 128], k.idf[:]), reads=[xo.r, k.idf.r], writes=[pb.r])
                                copy_op(k, ("act", "dve")[j % 2], xs_tm[:, b, cc * 128:(cc + 1) * 128], pb[:, 0:128], [pb.r], [xs_tm.r])
                        elif cc < 5:
                            g = cc - 3
                            P.op("act", lambda e, ac=ac, g=g, cc=cc, N=N, t0=t0: e.activation(bmT[:, g, t0:t0 + N], ac[:, 0:N], AF.Silu, bias=cw[:, cc, 5:6]), reads=[ac.r, cw.r], writes=[bmT.r])
                            for j, b in enumerate(seg):
                                pb = k.psb[5 + (j % 2)]
                                pv = pb[:].bitcast(BF16)
                                P.op("pe", lambda e, g=g, b=b, pv=pv: e.transpose(pv[:, 0:128], bmT[:, g, b * 128:(b + 1) * 128], k.idb[:]), reads=[bmT.r, k.idb.r], writes=[pb.r])
                                copy_op(k, ("act", "dve")[j % 2], bm_tm[:, b, g * 128:(g + 1) * 128], pv[:, 0:128], [pb.r], [bm_tm.r])
                        else:
                            g = cc - 5
                            P.op("act", lambda e, ac=ac, g=g, cc=cc, N=N, t0=t0: e.activation(cmT[:, g, t0:t0 + N], ac[:, 0:N], AF.Silu, bias=cw[:, cc, 5:6]), reads=[ac.r, cw.r], writes=[cmT.r])

            stageA(0)
            stageB(0)
            yield
            stageA(1)
            for si in range(2, 9):
                stageA(si)
                stageB(si - 1)
                yield
            stageB(8)
            P.barrier()
        y_res = [Res("ssdy%d" % b) for b in range(NB)]
        X = []
        pY, pYo, pS, pSm = k.psb[4], k.psb[5], k.psb[6], k.psb[7]
        for d in range(2):
            x = K()
            x.lhs = sbuf(k, st, "slhs%d" % d, [128, 6, 128], F32)
            x.E = sbuf(k, st, "sE%d" % d, [128, 6, 128], BF16)
            x.sm = sbuf(k, st, "ssm%d" % d, [128, 2, 128], BF16)
            x.pp = []
            for i in range(2):
                y = K()
                y.att = sbuf(k, st, "satt%d%d" % (d, i), [128, 6, 128], BF16)
                y.xin = sbuf(k, st, "xin%d%d" % (d, i), [128, 6, 64], BF16)
                y.xw = sbuf(k, st, "xw%d%d" % (d, i), [128, 6, 64], BF16)
                y.sc = sbuf(k, st, "ssc%d%d" % (d, i), [128, 24], F32)
                x.pp.append(y)
            x.S = sbuf(k, st, "sS%d" % d, [128, 6, 64], F32)
            x.Sbf = sbuf(k, st, "sSbf%d" % d, [128, 6, 64], BF16)
            x.yo = sbuf(k, st, "syo%d" % d, [128, 6, 64], F32)
            x.ys = sbuf(k, st, "sys%d" % d, [128, 384], F32)
            x.yp = sbuf(k, st, "syp%d" % d, [128, 384], F32)
            x.zt = sbuf(k, st, "szt%d" % d, [128, 384], F32)
            x.jk = sbuf(k, st, "sjk%d" % d, [128, 384], F32)
            x.ss = sbuf(k, st, "sss%d" % d, [128, 4], F32)
            P.op("dve", lambda e, x=x: e.memset(x.ss[:], 1.0), writes=[x.ss.r])
            x.yb = sbuf(k, st, "syb%d" % d, [128, 384], BF16)
            P.op("dve", lambda e, x=x: e.memset(x.S[:], 0.0), writes=[x.S.r])
            P.op("dve", lambda e, x=x: e.memset(x.Sbf[:], 0.0), writes=[x.Sbf.r])
            x.bX, x.bY = k.psb[2 * d], k.psb[2 * d + 1]
            X.append(x)
        fseq = list(range(NB))
        rseq = [1, 0] + list(range(NB - 1, 1, -1))
        visited = set()
        def prep(d, step):
            b = (fseq, rseq)[d][step]
            x = X[d]
            t0 = b * 128
            la_d = la_all[:, b, d * 6:(d + 1) * 6]
            dt_d = dt_all[:, b, d * 6:(d + 1) * 6]
            y = x.pp[step % 2]
            xsv = xs_tm[:, b, :].rearrange("p (h c) -> p h c", c=64)
            P.op("pe", lambda e, d=d, la_d=la_d: e.matmul(pSm[:, 0:6], cm[:, C_MASKF + d, :], la_d, start=True, stop=True), reads=[cm.r, la_all.r], writes=[pSm.r])
            P.op("pe", lambda e, d=d, la_d=la_d: e.matmul(pSm[:, 6:12], cm[:, C_STRF + d, :], la_d, start=True, stop=True), reads=[cm.r, la_all.r], writes=[pSm.r])
            P.op("act", lambda e, x=x, y=y: e.activation(y.sc[:, 0:12], pSm[:, 0:12], AF.Exp), reads=[pSm.r], writes=[y.sc.r])
            P.op("dve", lambda e, x=x, y=y: e.tensor_tensor(y.sc[:, 12:18], y.sc[:, 0:6], y.sc[:, 6:12], ALU.mult), reads=[y.sc.r], writes=[y.sc.r])
            P.op("dve", lambda e, x=x, y=y, dt_d=dt_d: e.tensor_tensor(y.sc[:, 18:24], y.sc[:, 6:12], dt_d, ALU.mult), reads=[y.sc.r, dt_all.r], writes=[y.sc.r])
            P.op("pool", lambda e, x=x, y=y, d=d, la_d=la_d: e.tensor_tensor(x.lhs[:], la_d.unsqueeze(2).to_broadcast([128, 6, 128]), cm[:, C_STRF + d, :].unsqueeze(1).to_broadcast([128, 6, 128]), ALU.mult),
                 reads=[la_all.r, cm.r], writes=[x.lhs.r])
            for h in range(6):
                pb, c0 = (x.bX, h * 128) if h < 4 else (x.bY, (h - 4) * 128)
                P.op("pe", lambda e, x=x, y=y, h=h, pb=pb, c0=c0, d=d: e.matmul(pb[:, c0:c0 + 128], x.lhs[:, h, :], cm[:, C_MASKF + d, :], start=True, stop=True),
                     reads=[x.lhs.r, cm.r], writes=[pb.r])
            P.op("act", lambda e, x=x, y=y: e.activation(x.E[:, 0:4, :].rearrange("p h n -> p (h n)"), x.bX[:], AF.Exp), reads=[x.bX.r], writes=[x.E.r])
            P.op("act", lambda e, x=x, y=y: e.activation(x.E[:, 4:6, :].rearrange("p h n -> p (h n)"), x.bY[:, 0:256], AF.Exp), reads=[x.bY.r], writes=[x.E.r])
            for g in range(2):
                P.op("pe", lambda e, x=x, y=y, g=g, t0=t0: e.matmul(x.bY[:, 256 + g * 128:256 + (g + 1) * 128], bmT[:, g, t0:t0 + 128], cmT[:, g, t0:t0 + 128], start=True, stop=True),
                     reads=[bmT.r, cmT.r], writes=[x.bY.r])
            P.op("dve", lambda e, x=x, y=y, d=d: e.tensor_tensor(x.sm[:], x.bY[:, 256:512].rearrange("p (g n) -> p g n", g=2), maskb[:, d, :].unsqueeze(1).to_broadcast([128, 2, 128]), ALU.mult),
                 reads=[x.bY.r, maskb.r], writes=[x.sm.r])
            for g in range(2):
                P.op(("dve", "pool")[g], lambda e, x=x, y=y, g=g: e.tensor_tensor(y.att[:, 3 * g:3 * g + 3, :], x.E[:, 3 * g:3 * g + 3, :], x.sm[:, g, :].unsqueeze(1).to_broadcast([128, 3, 128]), ALU.mult),
                     reads=[x.E.r, x.sm.r], writes=[y.att.r])
            P.op("dve", lambda e, x=x, y=y, xsv=xsv, dt_d=dt_d: e.tensor_tensor(y.xin[:], xsv, dt_d.unsqueeze(2).to_broadcast([128, 6, 64]), ALU.mult), reads=[xs_tm.r, dt_all.r], writes=[y.xin.r])
            P.op("pool", lambda e, x=x, y=y, xsv=xsv: e.tensor_tensor(y.xw[:], xsv, y.sc[:, 18:24].unsqueeze(2).to_broadcast([128, 6, 64]), ALU.mult), reads=[xs_tm.r, y.sc.r], writes=[y.xw.r])

        def scan(d, step):
            b = (fseq, rseq)[d][step]
            x = X[d]
            t0 = b * 128
            la_d = la_all[:, b, d * 6:(d + 1) * 6]
            dt_d = dt_all[:, b, d * 6:(d + 1) * 6]
            y = x.pp[step % 2]
            xsv = xs_tm[:, b, :].rearrange("p (h c) -> p h c", c=64)
            for h in range(6):
                P.op("pe", lambda e, x=x, y=y, h=h: e.matmul(pY[:, h * 64:(h + 1) * 64], y.att[:, h, :], y.xin[:, h, :], start=True, stop=True), reads=[y.att.r, y.xin.r], writes=[pY.r])
            for h in range(6):
                g = h // 3
                P.op("pe", lambda e, x=x, y=y, h=h, g=g, t0=t0: e.matmul(pYo[:, h * 64:(h + 1) * 64], cmT[:, g, t0:t0 + 128], x.Sbf[:, h, :], start=True, stop=True), reads=[cmT.r, x.Sbf.r], writes=[pYo.r])
            P.op("dve", lambda e, x=x, y=y: e.tensor_tensor(x.yo[:], pYo[:, 0:384].rearrange("p (h c) -> p h c", c=64), y.sc[:, 0:6].unsqueeze(2).to_broadcast([128, 6, 64]), ALU.mult),
                 reads=[pYo.r, y.sc.r], writes=[x.yo.r])
            P.op("dve", lambda e, x=x, y=y: e.tensor_tensor(x.ys[:], pY[:, 0:384], x.yo[:].rearrange("p h c -> p (h c)"), ALU.add), reads=[pY.r, x.yo.r], writes=[x.ys.r])
            for h in range(6):
                g = h // 3
                P.op("pe", lambda e, x=x, y=y, h=h, g=g, b=b: e.matmul(pS[:, h * 64:(h + 1) * 64], bm_tm[:, b, g * 128:(g + 1) * 128], y.xw[:, h, :], start=True, stop=True), reads=[bm_tm.r, y.xw.r], writes=[pS.r])
            P.op("pool", lambda e, x=x, y=y: e.tensor_tensor(x.S[:], x.S[:], y.sc[:, 12:18].unsqueeze(2).to_broadcast([128, 6, 64]), ALU.mult), reads=[x.S.r, y.sc.r], writes=[x.S.r])
            P.op("dve", lambda e, x=x, y=y: e.tensor_tensor(x.S[:], x.S[:], pS[:, 0:384].rearrange("p (h c) -> p h c", c=64), ALU.add), reads=[x.S.r, pS.r], writes=[x.S.r])
            P.op("act", lambda e, x=x, y=y: e.copy(x.Sbf[:], x.S[:]), reads=[x.S.r], writes=[x.Sbf.r])
            if b not in visited:
                visited.add(b)
                P.op("pool", lambda e, x=x, y=y, xsv=xsv: e.tensor_tensor(x.yo[:], xsv, dsk[:].unsqueeze(2).to_broadcast([128, 6, 64]), ALU.mult), reads=[xs_tm.r, dsk.r, x.yo.r], writes=[x.yo.r])
                P.op("pool", lambda e, x=x, y=y: e.tensor_tensor(x.ys[:], x.ys[:], x.yo[:].rearrange("p h c -> p (h c)"), ALU.add), reads=[x.ys.r, x.yo.r], writes=[x.ys.r])
                P.dma("act", S["ssd_y"].t[t0:t0 + 128, :], x.ys[:], reads=[x.ys.r], writes=[y_res[b]])
            else:
                if l == DEPTH - 1 and b < 2:
                    return
                P.dma("sp", x.yp[:], S["ssd_y"].t[t0:t0 + 128, :], reads=[y_res[b]], writes=[x.yp.r])
                for (p0, np_, ap) in rows(tm.t, tml, b, Z0, 384):
                    P.dma("sp", x.zt[p0:p0 + np_, :], ap, reads=[tm.r], writes=[x.zt.r])
                P.op("dve", lambda e, x=x, y=y: e.tensor_tensor(x.ys[:], x.ys[:], x.yp[:], ALU.add), reads=[x.ys.r, x.yp.r], writes=[x.ys.r])
                P.op("act", lambda e, x=x, y=y: e.activation(x.jk[:], x.zt[:], AF.Exp, scale=-1.0), reads=[x.zt.r], writes=[x.jk.r])
                P.op("pool", lambda e, x=x, y=y: e.tensor_tensor(x.ys[:], x.ys[:], x.zt[:], ALU.mult), reads=[x.ys.r, x.zt.r], writes=[x.ys.r])
                P.op("dve", lambda e, x=x, y=y: e.tensor_scalar(x.jk[:], x.jk[:], 1.0, None, ALU.add), reads=[x.jk.r], writes=[x.jk.r])
                P.op("dve", lambda e, x=x, y=y: e.reciprocal(x.jk[:], x.jk[:]), reads=[x.jk.r], writes=[x.jk.r])
                P.op("dve", lambda e, x=x, y=y: e.tensor_tensor(x.ys[:], x.ys[:], x.jk[:], ALU.mult), reads=[x.ys.r, x.jk.r], writes=[x.ys.r])
                P.op("act", lambda e, x=x, y=y: e.activation(x.jk[:], x.ys[:], AF.Square, accum_out=x.ss[:, 0:1]), reads=[x.ys.r], writes=[x.jk.r, x.ss.r])
                P.op("dve", lambda e, x=x, y=y: e.tensor_scalar(x.ss[:, 1:2], x.ss[:, 0:1], 1.0 / 384, EPS, ALU.mult, ALU.add), reads=[x.ss.r], writes=[x.ss.r])
                P.op("pool", lambda e, x=x, y=y: e.tensor_tensor(x.ss[:, 0:4], x.ss[:, 0:4], smhalf[:, 0:4], ALU.pow), reads=[x.ss.r, smhalf.r], writes=[x.ss.r])
                P.op("dve", lambda e, x=x, y=y: e.scalar_tensor_tensor(x.yb[:], x.ys[:], x.ss[:, 1:2], sng[:], ALU.mult, ALU.mult), reads=[x.ys.r, x.ss.r, sng.r], writes=[x.yb.r])
                for (p0, np_, ap) in rows(S["mix"].t, mixl, b, 640, 384):
                    P.dma("act", ap, x.yb[p0:p0 + np_, :], reads=[x.yb.r], dwrites=[S["mix"].r])

        for d in range(2):
            prep(d, 0)
        for step in range(NB):
            for d in range(2):
                if step + 1 < NB:
                    prep(d, step + 1)
                    yield
            for d in range(2):
                scan(d, step)
                yield
        yield


def s5(k, l):
    P, I, S, nc = k.P, k.I, k.S, k.nc
    tm = S["tm"]
    NCH = T // 16
    cbs = [(0, 128), (128, 128), (256, 16)]
    TWO_PI = 6.283185307179586

    def tt(eng, out, a, b, op, reads, writes):
        P.op(eng, lambda e: e.tensor_tensor(out, a, b, op), reads=reads, writes=writes)

    with contextlib.ExitStack() as st:
        cs = sbuf(k, st, "s5cs", [128, 5, 256], F32)
        P.dma("sp", cs[:], I["cst_s5"].rearrange("c p n -> p c n"), writes=[cs.r])
        Yacc = sbuf(k, st, "Yacc", [128, 2, 16, NCH], F32)
        U = sbuf(k, st, "s5U", [128, 16, 2, NCH], BF16)
        wglu = sbuf(k, st, "wglu", [128, 2, 256], BF16)
        bglu = sbuf(k, st, "bglu", [128, 256], F32)
        dsk = sbuf(k, st, "s5dsk", [128, 256], F32)
        P.dma("sp", bglu[:], I["s5_b_glu"][l].partition_broadcast(128), writes=[bglu.r])
        P.dma("sp", dsk[:], I["s5_d"][l].partition_broadcast(128), writes=[dsk.r])
        with contextlib.ExitStack() as su:
            utm = sbuf(k, su, "utm", [128, 16, 256], F32)
            utg = sbuf(k, su, "utg", [128, 4096], F32)
            P.dma("sp", utm[:, 0:2, :], I["s5_w_glu"][l].rearrange("(c p) n -> p c n", p=128), writes=[utm.r])
            P.op("dve", lambda e: e.tensor_copy(wglu[:], utm[:, 0:2, :]), reads=[utm.r], writes=[wglu.r])
            for (c0, n) in cbs:
                for sh_ in range(4):
                    P.dma("sp", utm[0:n, sh_ * 4:(sh_ + 1) * 4, :], tm.t[c0 * 16:(c0 + n) * 16, S50:S50 + 256].rearrange("(c s) n -> c s n", s=16)[:, sh_ * 4:(sh_ + 1) * 4, :], reads=[tm.r], writes=[utm.r])
                for g in range(16):
                    copy_op(k, ("dve", "act")[g % 2], utg[0:n, g * 256:(g + 1) * 256].rearrange("p (s h) -> p s h", h=16), utm[0:n, :, g * 16:(g + 1) * 16], [utm.r], [utg.r])
                for g in range(16):
                    pb = k.psb[g % 4]
                    for kc in range(2):
                        P.op("pe", lambda e, g=g, kc=kc, pb=pb, n=n: e.transpose(pb[:, kc * 128:kc * 128 + n], utg[0:n, g * 256 + kc * 128:g * 256 + (kc + 1) * 128], k.idf[0:n, 0:n]),
                             reads=[utg.r, k.idf.r], writes=[pb.r])
                    copy_op(k, ("act", "dve")[g % 2], U[:, g, :, c0:c0 + n], pb[:, 0:256].rearrange("p (a b) -> p a b", a=2)[:, :, 0:n], [pb.r], [U.r])
            P.barrier()
        pr = sbuf(k, st, "s5pr", [128, 16, 32], F32)
        AR, AI, DT, MAG, CO, SI, LR, LI, FR, FI, QR, QI, T1, T2, T3 = range(15)
        a0 = sbuf(k, st, "s5a0", [32, 2, 128], F32)
        for j, nm in enumerate(("s5_a_re", "s5_a_im")):
            for hcol in range(2):
                P.dma("sp", a0[:, j, hcol * 64:(hcol + 1) * 64], I[nm][l].rearrange("d g p -> (d g) p"), writes=[a0.r])
            P.op("pe", lambda e, j=j: e.transpose(k.psb[4][:, j * 32:(j + 1) * 32], a0[:, j, :], k.idf[0:32, 0:32]), reads=[a0.r, k.idf.r], writes=[k.psb[4].r])
        P.op("dve", lambda e: e.tensor_copy(pr[:, AR:AI + 1, :].rearrange("p a b -> p (a b)"), k.psb[4][:, 0:64]), reads=[k.psb[4].r], writes=[pr.r])
        P.dma("sp", pr[:, DT, :], I["s5_log_dt"][l].rearrange("d g -> (d g)").partition_broadcast(128), writes=[pr.r])
        R = [pr.r]
        P.op("act", lambda e: e.activation(pr[:, DT, :], pr[:, DT, :], AF.Exp), reads=R, writes=R)
        tt("dve", pr[:, T1, :], pr[:, DT, :], pr[:, AR, :], ALU.mult, R, R)
        P.op("act", lambda e: e.activation(pr[:, MAG, :], pr[:, T1, :], AF.Exp), reads=R, writes=R)
        tt("dve", pr[:, T1, :], pr[:, DT, :], pr[:, AI, :], ALU.mult, R, R)
        P.op("act", lambda e: e.activation(pr[:, SI, :], pr[:, T1, :], AF.Sin, scale=1.0 / 16), reads=R, writes=R)
        P.op("dve", lambda e: e.tensor_scalar(pr[:, T2, :], pr[:, T1, :], 1.0 / 16, 1.5707963267948966, ALU.mult, ALU.add), reads=R, writes=R)
        P.op("act", lambda e: e.activation(pr[:, CO, :], pr[:, T2, :], AF.Sin), reads=R, writes=R)
        for _ in range(4):
            tt("dve", pr[:, T1, :], pr[:, CO, :], pr[:, CO, :], ALU.mult, R, R)
            tt("dve", pr[:, T2, :], pr[:, SI, :], pr[:, SI, :], ALU.mult, R, R)
            P.op("dve", lambda e: e.scalar_tensor_tensor(pr[:, SI, :], pr[:, CO, :], 2.0, pr[:, SI, :], ALU.mult, ALU.mult), reads=R, writes=R)
            tt("dve", pr[:, CO, :], pr[:, T1, :], pr[:, T2, :], ALU.subtract, R, R)
        tt("dve", pr[:, LR, :], pr[:, MAG, :], pr[:, CO, :], ALU.mult, R, R)
        tt("dve", pr[:, LI, :], pr[:, MAG, :], pr[:, SI, :], ALU.mult, R, R)
        tt("dve", pr[:, T1, :], pr[:, AR, :], pr[:, AR, :], ALU.mult, R, R)
        tt("dve", pr[:, T2, :], pr[:, AI, :], pr[:, AI, :], ALU.mult, R, R)
        tt("dve", pr[:, T1, :], pr[:, T1, :], pr[:, T2, :], ALU.add, R, R)
        P.op("dve", lambda e: e.reciprocal(pr[:, T3, :], pr[:, T1, :]), reads=R, writes=R)
        P.op("dve", lambda e: e.tensor_scalar(pr[:, T1, :], pr[:, LR, :], -1.0, None, ALU.add), reads=R, writes=R)
        tt("dve", pr[:, FR, :], pr[:, T1, :], pr[:, AR, :], ALU.mult, R, R)
        tt("dve", pr[:, T2, :], pr[:, LI, :], pr[:, AI, :], ALU.mult, R, R)
        tt("dve", pr[:, FR, :], pr[:, FR, :], pr[:, T2, :], ALU.add, R, R)
        tt("dve", pr[:, FR, :], pr[:, FR, :], pr[:, T3, :], ALU.mult, R, R)
        tt("dve", pr[:, FI, :], pr[:, LI, :], pr[:, AR, :], ALU.mult, R, R)
        tt("dve", pr[:, T2, :], pr[:, T1, :], pr[:, AI, :], ALU.mult, R, R)
        tt("dve", pr[:, FI, :], pr[:, FI, :], pr[:, T2, :], ALU.subtract, R, R)
        tt("dve", pr[:, FI, :], pr[:, FI, :], pr[:, T3, :], ALU.mult, R, R)
        tt("dve", pr[:, T1, :], pr[:, MAG, :], pr[:, MAG, :], ALU.mult, R, R)
        P.op("dve", lambda e: e.reciprocal(pr[:, T1, :], pr[:, T1, :]), reads=R, writes=R)
        tt("dve", pr[:, QR, :], pr[:, LR, :], pr[:, T1, :], ALU.mult, R, R)
        P.op("dve", lambda e: e.scalar_tensor_tensor(pr[:, QI, :], pr[:, LI, :], -1.0, pr[:, T1, :], ALU.mult, ALU.mult), reads=R, writes=R)
        PW = sbuf(k, st, "s5pw", [128, 4, 2, 32, 17], F32)
        W = [PW.r]
        ptmp = sbuf(k, st, "s5ptmp", [128, 2, 32, 8], F32)
        for tb, (br, bi) in ((0, (LR, LI)), (2, (QR, QI))):
            P.op("dve", lambda e, tb=tb: e.memset(PW[:, tb, 0, :, 0:1], 1.0), writes=W)
            P.op("dve", lambda e, tb=tb: e.memset(PW[:, tb, 1, :, 0:1], 0.0), writes=W)
            P.op("dve", lambda e, tb=tb, br=br: e.tensor_copy(PW[:, tb, 0, :, 1], pr[:, br, :]), reads=R, writes=W)
            P.op("dve", lambda e, tb=tb, bi=bi: e.tensor_copy(PW[:, tb, 1, :, 1], pr[:, bi, :]), reads=R, writes=W)
            m = 1
            while m < 16:
                Ar, Ai = PW[:, tb, 0, :, 1:m + 1], PW[:, tb, 1, :, 1:m + 1]
                brr = PW[:, tb, 0, :, m:m + 1].to_broadcast([128, 32, m])
                bii = PW[:, tb, 1, :, m:m + 1].to_broadcast([128, 32, m])
                t1, t2 = ptmp[:, 0, :, 0:m], ptmp[:, 1, :, 0:m]
                RW = [PW.r, ptmp.r]
                tt("dve", t1, Ar, brr, ALU.mult, RW, RW)
                tt("dve", t2, Ai, bii, ALU.mult, RW, RW)
                tt("dve", PW[:, tb, 0, :, m + 1:2 * m + 1], t1, t2, ALU.subtract, RW, RW)
                tt("dve", t1, Ar, bii, ALU.mult, RW, RW)
                tt("dve", t2, Ai, brr, ALU.mult, RW, RW)
                tt("dve", PW[:, tb, 1, :, m + 1:2 * m + 1], t1, t2, ALU.add, RW, RW)
                m *= 2
        for j in range(17):
            P.op("pool", lambda e, j=j: e.tensor_copy(PW[:, 1, :, :, j], PW[:, 0, :, :, 16 - j]), reads=W, writes=W)
        for j in range(16):
            P.op("pool", lambda e, j=j: e.tensor_copy(PW[:, 3, :, :, j], PW[:, 2, :, :, 15 - j]), reads=W, writes=W)
        Bz = sbuf(k, st, "s5B", [128, 2, 16, 16], F32)
        Bb = sbuf(k, st, "s5Bb", [128, 2, 2, 16, 16], F32)
        P.dma("sp", Bz[0:64, 0], I["s5_b_re"][l].rearrange("g p h -> p g h"), writes=[Bz.r])
        P.dma("sp", Bz[64:128, 0], I["s5_b_im"][l].rearrange("g p h -> p g h"), writes=[Bz.r])
        P.dma("sp", Bz[0:64, 1], I["s5_b_im"][l].rearrange("g p h -> p g h"), writes=[Bz.r])
        P.dma("sp", Bz[64:128, 1], I["s5_b_re"][l].rearrange("g p h -> p g h"), writes=[Bz.r])
        P.op("dve", lambda e: e.tensor_scalar(Bz[0:64, 1], Bz[0:64, 1], -1.0, None, ALU.mult), reads=[Bz.r], writes=[Bz.r])
        btmp = sbuf(k, st, "s5btmp", [128, 2, 16, 16], F32)
        frv = pr[:, FR, :].rearrange("p (d g) -> p d g", d=2).unsqueeze(3).to_broadcast([128, 2, 16, 16])
        fiv = pr[:, FI, :].rearrange("p (d g) -> p d g", d=2).unsqueeze(3).to_broadcast([128, 2, 16, 16])
        bst = Bz[:, 0].unsqueeze(1).to_broadcast([128, 2, 16, 16])
        bsw = Bz[:, 1].unsqueeze(1).to_broadcast([128, 2, 16, 16])
        RB = [Bz.r, Bb.r, btmp.r, pr.r]
        tt("dve", Bb[:, 0], frv, bst, ALU.mult, RB, RB)
        tt("dve", btmp[:], fiv, bsw, ALU.mult, RB, RB)
        tt("dve", Bb[:, 0], Bb[:, 0], btmp[:], ALU.add, RB, RB)
        tt("dve", Bb[:, 1], frv, bsw, ALU.mult, RB, RB)
        tt("dve", btmp[:], fiv, bst, ALU.mult, RB, RB)
        tt("dve", Bb[:, 1], Bb[:, 1], btmp[:], ALU.subtract, RB, RB)
        Cz = sbuf(k, st, "s5C", [128, 2, 16, 16], F32)
        cin = sbuf(k, st, "s5cin", [128, 2, 2, 128], F32)
        cre = I["s5_c_re"][l].rearrange("g h p -> (g h) p")
        cim = I["s5_c_im"][l].rearrange("g h p -> (g h) p")
        for hf in range(2):
            P.dma("sp", cin[:, hf, 0, 0:64], cre[hf * 128:(hf + 1) * 128, :], writes=[cin.r])
            P.dma("sp", cin[:, hf, 0, 64:128], cim[hf * 128:(hf + 1) * 128, :], writes=[cin.r])
            P.dma("sp", cin[:, hf, 1, 0:64], cim[hf * 128:(hf + 1) * 128, :], writes=[cin.r])
            P.dma("sp", cin[:, hf, 1, 64:128], cre[hf * 128:(hf + 1) * 128, :], writes=[cin.r])
        for v in range(2):
            for hf in range(2):
                P.op("pe", lambda e, v=v, hf=hf: e.transpose(k.psb[5][:, (v * 2 + hf) * 128:(v * 2 + hf + 1) * 128], cin[:, hf, v, :], k.idf[:]), reads=[cin.r, k.idf.r], writes=[k.psb[5].r])
        P.op("dve", lambda e: e.tensor_copy(Cz[:].rearrange("p v g h -> p (v g h)"), k.psb[5][:]), reads=[k.psb[5].r], writes=[Cz.r])
        P.op("dve", lambda e: e.tensor_scalar(Cz[64:128, 0], Cz[64:128, 0], -1.0, None, ALU.mult), reads=[Cz.r], writes=[Cz.r])
        P.op("dve", lambda e: e.tensor_scalar(Cz[:, 1], Cz[:, 1], -1.0, None, ALU.mult), reads=[Cz.r], writes=[Cz.r])
        def s5_dir(d):
            with contextlib.ExitStack() as sd:
                Mm = sbuf(k, sd, "s5M", [128, 16, 2, 256], BF16)
                TGb = sbuf(k, sd, "s5G", [128, 16, 256], BF16)
                mu = sbuf(k, sd, "s5mu", [128, 2, 2, 16], F32)
                mp = sbuf(k, sd, "s5mp", [128, 3, 3, 16], F32)
                E2 = sbuf(k, sd, "s5E", [128, NCH, 2, 16], F32)
                gs = slice(d * 16, (d + 1) * 16)
                RM = [mp.r]
                P.op("dve", lambda e: e.tensor_copy(mp[:, 0, 0, :], PW[:, 0, 0, gs, 16]), reads=W, writes=RM)
                P.op("dve", lambda e: e.tensor_copy(mp[:, 0, 1, :], PW[:, 0, 1, gs, 16]), reads=W, writes=RM)
                for lv in range(1, 3):
                    tt("dve", mp[:, lv, 0, :], mp[:, lv - 1, 0, :], mp[:, lv - 1, 0, :], ALU.mult, RM, RM)
                    tt("dve", mp[:, lv, 2, :], mp[:, lv - 1, 1, :], mp[:, lv - 1, 1, :], ALU.mult, RM, RM)
                    tt("dve", mp[:, lv, 0, :], mp[:, lv, 0, :], mp[:, lv, 2, :], ALU.subtract, RM, RM)
                    P.op("dve", lambda e, lv=lv: e.scalar_tensor_tensor(mp[:, lv, 1, :], mp[:, lv - 1, 0, :], 2.0, mp[:, lv - 1, 1, :], ALU.mult, ALU.mult), reads=RM, writes=RM)
                for lv in range(3):
                    P.op("dve", lambda e, lv=lv: e.tensor_scalar(mp[:, lv, 2, :], mp[:, lv, 1, :], -1.0, None, ALU.mult), reads=RM, writes=RM)
                for sl_ in range(2):
                    P.op("dve", lambda e, sl_=sl_: e.tensor_copy(mu[:, 0, sl_, :], mp[:, 2, 0, :]), reads=RM, writes=[mu.r])
                P.op("dve", lambda e: e.tensor_copy(mu[:, 1, 0, :], mp[:, 2, 1, :]), reads=RM, writes=[mu.r])
                P.op("dve", lambda e: e.tensor_copy(mu[:, 1, 1, :], mp[:, 2, 2, :]), reads=RM, writes=[mu.r])
                with contextlib.ExitStack() as stt:
                    TA = sbuf(k, stt, "s5TA", [128, 16, 16, 16], F32)
                    TBt = sbuf(k, stt, "s5TB", [128, 16, 16, 16], F32)
                    TT = sbuf(k, stt, "s5TT", [128, 8, 16, 16], F32)
                    H2 = sbuf(k, stt, "s5H", [128, 8, 2, 256], BF16)

                    def table(dst, tb, s0, Z, zi):
                        RT = [dst.r, TT.r, PW.r, zi]
                        for gh_ in range(2):
                            g0 = d * 16 + gh_ * 8
                            hs = slice(gh_ * 8, (gh_ + 1) * 8)
                            xr = PW[:, tb, 0, g0:g0 + 8, s0:s0 + 16].unsqueeze(3).to_broadcast([128, 8, 16, 16])
                            xi = PW[:, tb, 1, g0:g0 + 8, s0:s0 + 16].unsqueeze(3).to_broadcast([128, 8, 16, 16])
                            zst = Z[0][:, hs].unsqueeze(2).to_broadcast([128, 8, 16, 16])
                            zsw = Z[1][:, hs].unsqueeze(2).to_broadcast([128, 8, 16, 16])
                            tt("dve", dst[:, hs], xr, zst, ALU.mult, RT, RT)
                            tt("dve", TT[:], xi, zsw, ALU.mult, [PW.r, zi, TT.r], [TT.r])
                            tt("dve", dst[:, hs], dst[:, hs], TT[:], ALU.add, RT, RT)

                    ZB = (Bb[:, 0, d], Bb[:, 1, d])
                    ZC = (Cz[:, 0], Cz[:, 1])
                    table(TA, 2 if d == 0 else 3, 0, ZB, Bb.r)
                    table(TBt, 0 if d == 0 else 1, 0 if d == 0 else 1, ZC, Cz.r)
                    for g in range(16):
                        for kc in range(2):
                            pb = k.psb[(g * 2 + kc) % 4]
                            P.op("pe", lambda e, g=g, kc=kc, pb=pb: e.matmul(pb[:, 0:256], TA[:, g].rearrange("p j x -> p (j x)")[:, kc * 128:(kc + 1) * 128], TBt[:, g].rearrange("p j x -> p (j x)"), start=True, stop=True),
                                 reads=[TA.r, TBt.r], writes=[pb.r])
                            P.op("dve", lambda e, g=g, kc=kc, pb=pb: e.tensor_tensor(Mm[:, g, kc, :], pb[:, 0:256], cs[:, 1 + d * 2 + kc, :], ALU.mult), reads=[pb.r, cs.r], writes=[Mm.r])
                    table(TBt, 0 if d == 0 else 1, 1 if d == 0 else 0, ZC, Cz.r)
                    P.op("act", lambda e: e.copy(TGb[:].rearrange("p g n -> p (g n)"), TBt[:].rearrange("p g j x -> p (g j x)")), reads=[TBt.r], writes=[TGb.r])
                    table(TA, 1 if d == 0 else 0, 1 if d == 0 else 0, ZB, Bb.r)
                    for gh_ in range(2):
                        for g8 in range(8):
                            g = gh_ * 8 + g8
                            for kc in range(2):
                                pb = k.psb[(g * 2 + kc) % 4]
                                P.op("pe", lambda e, g=g, kc=kc, pb=pb: e.matmul(pb[:, 0:256], TA[:, g].rearrange("p j x -> p (j x)")[:, kc * 128:(kc + 1) * 128], cs[:, 0, :], start=True, stop=True),
                                     reads=[TA.r, cs.r], writes=[pb.r])
                                copy_op(k, ("act", "dve")[kc], H2[:, g8, kc, :], pb[:, 0:256], [pb.r], [H2.r])
                        for g8 in range(8):
                            g = gh_ * 8 + g8
                            for hf in range(2):
                                pb = k.psb[4 + (g * 2 + hf) % 4]
                                for kc in range(2):
                                    P.op("pe", lambda e, g=g, g8=g8, hf=hf, kc=kc, pb=pb: e.matmul(pb[:, 0:NCH], H2[:, g8, kc, hf * 128:(hf + 1) * 128], U[:, g, kc, :], start=(kc == 0), stop=(kc == 1)),
                                         reads=[H2.r, U.r], writes=[pb.r])
                                copy_op(k, ("act", "dve")[hf], E2[:, :, hf, g], pb[:, 0:NCH], [pb.r], [E2.r])
                    P.barrier()
                Xbf = sbuf(k, sd, "s5X", [128, 16, NCH], BF16)
                E1 = sbuf(k, sd, "s5E1", [128, NCH // 2, 2, 16], F32)
                Eq = sbuf(k, sd, "s5Eq", [128, NCH // 4, 2, 16], F32)
                Qp = sbuf(k, sd, "s5Qp", [128, NCH // 4, 2, 16], F32)
                P1 = sbuf(k, sd, "s5P1", [128, NCH // 2, 2, 8], F32)
                Zb = [sbuf(k, sd, "s5Z%d" % i, [128, 3, 16], F32) for i in range(4)]
                zt_ = [sbuf(k, sd, "s5zt%d" % i, [128, 2, 16], F32) for i in range(2)]
                fi, si = (0, 1) if d == 0 else (1, 0)

                def pairs(ap, which):
                    return ap.rearrange("p (m two) s g -> p m two s g", two=2)[:, :, which]

                eqv = Eq[:].rearrange("p q s g -> p (q s g)").rearrange("p (m s g) -> p m s g", s=2, g=8)

                def cmadd(out, a, lv, gsl, bterm, n, res_r, res_w, tfull, tres):
                    mr = mp[:, lv, 0, gsl].unsqueeze(1).unsqueeze(1).to_broadcast([128, n, 2, 8])
                    mi = mp[:, lv, 1, gsl].unsqueeze(1).to_broadcast([128, n, 8])
                    nmi = mp[:, lv, 2, gsl].unsqueeze(1).to_broadcast([128, n, 8])
                    tv = tfull[:, 0:n]
                    rr_ = res_r + [mp.r, tres]
                    tt("dve", out, a, mr, ALU.mult, rr_, res_w)
                    tt("dve", tv[:, :, 0, :], a[:, :, 1, :], mi, ALU.mult, rr_, [tres])
                    tt("dve", tv[:, :, 1, :], a[:, :, 0, :], nmi, ALU.mult, rr_, [tres])
                    tt("dve", out, out, tv, ALU.add, rr_ + res_w, res_w)
                    tt("dve", out, out, bterm, ALU.add, rr_ + res_w, res_w)

                for gh in range(2):
                    gsl = slice(gh * 8, (gh + 1) * 8)
                    cmadd(E1[:, :, :, gsl], pairs(E2[:], fi)[:, :, :, gsl], 0, gsl, pairs(E2[:], si)[:, :, :, gsl], NCH // 2, [E2.r], [E1.r], P1[:], P1.r)
                for gh in range(2):
                    gsl = slice(gh * 8, (gh + 1) * 8)
                    cmadd(Eq[:, :, :, gsl], pairs(E1[:], fi)[:, :, :, gsl], 1, gsl, pairs(E1[:], si)[:, :, :, gsl], NCH // 4, [E1.r], [Eq.r], P1[:], P1.r)
                eng = "dve"
                P.op(eng, lambda e: e.memset(Zb[0][:], 0.0), writes=[Zb[0].r])
                NQ = NCH // 4
                qorder = list(range(NQ)) if d == 0 else [3, 2, 1, 0] + list(range(NQ - 1, 3, -1))
                for i, q in enumerate(qorder):
                    zo, zn = Zb[i % 4], Zb[(i + 1) % 4]
                    P.op("pool", lambda e, zo=zo, q=q: e.tensor_copy(Qp[:, q, :, :], zo[:, 0:2, :]), reads=[zo.r], writes=[Qp.r])
                    t1, t2 = zt_[0], zt_[1]
                    tt(eng, t1[:], zo[:, 0:2, :], mu[:, 0], ALU.mult, [zo.r, mu.r], [t1.r])
                    tt(eng, t2[:], zo[:, 1:3, :], mu[:, 1], ALU.mult, [zo.r, mu.r], [t2.r])
                    tt(eng, t1[:], t1[:], t2[:], ALU.add, [t1.r, t2.r], [t1.r])
                    tt(eng, zn[:, 0:2, :], t1[:], Eq[:, q, :, :], ALU.add, [t1.r, Eq.r], [zn.r])
                    P.op(eng, lambda e, zn=zn: e.tensor_copy(zn[:, 2, :], zn[:, 0, :]), reads=[zn.r], writes=[zn.r])
                for gh in range(2):
                    gsl = slice(gh * 8, (gh + 1) * 8)
                    p1f = pairs(P1[:], fi)
                    p1s = pairs(P1[:], si)
                    P.op("dve", lambda e, p1f=p1f, gsl=gsl: e.tensor_copy(p1f, Qp[:, :, :, gsl]), reads=[Qp.r], writes=[P1.r])
                    cmadd(p1s, Qp[:, :, :, gsl], 1, gsl, pairs(E1[:], fi)[:, :, :, gsl], NCH // 4, [Qp.r, E1.r], [P1.r], eqv, Eq.r)
                    xv = Xbf[:, gsl, :].rearrange("p g (m two) -> p g m two", two=2)
                    P.op("act", lambda e, xv=xv: e.copy(xv[:, :, :, fi], P1[:, :, 0, :].rearrange("p m g -> p g m")), reads=[P1.r], writes=[Xbf.r])
                    n2 = NCH // 2
                    mr = mp[:, 0, 0, gsl].unsqueeze(1).to_broadcast([128, n2, 8])
                    mi = mp[:, 0, 1, gsl].unsqueeze(1).to_broadcast([128, n2, 8])
                    RX = [P1.r, Eq.r, mp.r, E2.r]
                    tt("dve", eqv[:, :, 0, :], P1[:, :, 0, :], mr, ALU.mult, RX, [Eq.r])
                    tt("dve", eqv[:, :, 1, :], P1[:, :, 1, :], mi, ALU.mult, RX, [Eq.r])
                    tt("dve", eqv[:, :, 0, :], eqv[:, :, 0, :], eqv[:, :, 1, :], ALU.add, RX, [Eq.r])
                    tt("dve", xv[:, :, :, si], eqv[:, :, 0, :].rearrange("p m g -> p g m"), pairs(E2[:], fi)[:, :, 0, gsl].rearrange("p m g -> p g m"), ALU.add, RX, [Xbf.r])
                for g in range(16):
                    for mc in range(2):
                        pb = k.psb[(g * 2 + mc) % 4]
                        for kc in range(2):
                            P.op("pe", lambda e, g=g, mc=mc, kc=kc, pb=pb: e.matmul(pb[:, 0:NCH], Mm[:, g, kc, mc * 128:(mc + 1) * 128], U[:, g, kc, :], start=(kc == 0), stop=False),
                                 reads=[Mm.r, U.r], writes=[pb.r])
                        P.op("pe", lambda e, g=g, mc=mc, pb=pb: e.matmul(pb[:, 0:NCH], TGb[:, g, mc * 128:(mc + 1) * 128], Xbf[:, g, :], start=False, stop=True),
                             reads=[TGb.r, Xbf.r], writes=[pb.r])
                        if d == 0:
                            copy_op(k, ("act", "dve")[mc], Yacc[:, mc, g, :], pb[:, 0:NCH], [pb.r], [Yacc.r])
                        else:
                            P.op("dve", lambda e, g=g, mc=mc, pb=pb: e.tensor_tensor(Yacc[:, mc, g, :], Yacc[:, mc, g, :], pb[:, 0:NCH], ALU.add), reads=[pb.r, Yacc.r], writes=[Yacc.r])
                P.barrier()
        for d_ in range(2):
            s5_dir(d_)
        dbg_dump(k, "dbgY", Yacc, [128, 2, 16, NCH])
        dbg_dump(k, "dbgPW", PW, [128, 4, 2, 32, 17])
        dbg_dump(k, "dbgpr", pr, [128, 16, 32])
        dbg_dump(k, "dbgBb", Bb, [128, 2, 2, 16, 16])
        dbg_dump(k, "dbgCz", Cz, [128, 2, 16, 16])
        P.barrier()
        with contextlib.ExitStack() as so:
            ytm = sbuf(k, so, "s5ytm", [128, 16, 256], F32)
            utm2 = sbuf(k, so, "utm2", [128, 16, 256], F32)
            tq = sbuf(k, so, "s5tq", [128, 16, 256], F32)
            glb = sbuf(k, so, "s5glb", [128, 16, 256], BF16)
            glT = sbuf(k, so, "s5glT", [128, 2, 128], BF16)
            zz = sbuf(k, so, "s5zz", [128, 256], F32)
            ob = sbuf(k, so, "s5ob", [128, 16, 256], BF16)
            for (c0, n) in cbs:
                if l == DEPTH - 1 and False:
                    continue
                for sh_ in range(4):
                    P.dma("sp", utm2[0:n, sh_ * 4:(sh_ + 1) * 4, :], tm.t[c0 * 16:(c0 + n) * 16, S50:S50 + 256].rearrange("(c s) n -> c s n", s=16)[:, sh_ * 4:(sh_ + 1) * 4, :], reads=[tm.r], writes=[utm2.r])
                for g in range(16):
                    for mc in range(2):
                        pb = k.psb[(g * 2 + mc) % 4]
                        P.op("pe", lambda e, g=g, mc=mc, pb=pb, c0=c0, n=n: e.transpose(pb[0:n, 0:128], Yacc[:, mc, g, c0:c0 + n], k.idf[:]), reads=[Yacc.r, k.idf.r], writes=[pb.r])
                        copy_op(k, ("act", "dve")[mc], ytm[0:n, mc * 8:(mc + 1) * 8, g * 16:(g + 1) * 16], pb[0:n, 0:128].rearrange("p (t h) -> p t h", h=16), [pb.r], [ytm.r])
                dv = dsk[0:n, :].unsqueeze(1).to_broadcast([n, 16, 256])
                tt("pool", tq[0:n], utm2[0:n], dv, ALU.mult, [utm2.r, dsk.r], [tq.r])
                tt("dve", ytm[0:n], ytm[0:n], tq[0:n], ALU.add, [ytm.r, tq.r], [ytm.r])
                tt("pool", tq[0:n], ytm[0:n], ytm[0:n], ALU.mult, [ytm.r], [tq.r])
                P.op("dve", lambda e, n=n: e.tensor_scalar(tq[0:n], tq[0:n], 0.044715, 1.0, ALU.mult, ALU.add), reads=[tq.r], writes=[tq.r])
                tt("pool", tq[0:n], tq[0:n], ytm[0:n], ALU.mult, [tq.r, ytm.r], [tq.r])
                P.op("act", lambda e, n=n: e.activation(tq[0:n], tq[0:n], AF.Sigmoid, scale=1.5957691216057308), reads=[tq.r], writes=[tq.r])
                tt("dve", ytm[0:n], ytm[0:n], tq[0:n], ALU.mult, [ytm.r, tq.r], [ytm.r])
                P.op("act", lambda e, n=n: e.copy(glb[0:n], ytm[0:n]), reads=[ytm.r], writes=[glb.r])
                for s_ in range(16):
                    pt = k.psb[4 + s_ % 2]
                    pv = pt[:].bitcast(BF16)
                    for kc in range(2):
                        P.op("pe", lambda e, s_=s_, kc=kc, pv=pv, n=n: e.transpose(pv[:, kc * 128:kc * 128 + n], glb[0:n, s_, kc * 128:(kc + 1) * 128], k.idb[0:n, 0:n]), reads=[glb.r, k.idb.r], writes=[pt.r])
                    copy_op(k, "act", glT[:, :, 0:n], pv[:, 0:256].rearrange("p (a b) -> p a b", a=2)[:, :, 0:n], [pt.r], [glT.r])
                    pz = k.psb[6 + s_ % 2]
                    for kc in range(2):
                        P.op("pe", lambda e, kc=kc, pz=pz, n=n: e.matmul(pz[0:n, 0:256], glT[:, kc, 0:n], wglu[:, kc, :], start=(kc == 0), stop=(kc == 1)), reads=[glT.r, wglu.r], writes=[pz.r])
                    tt("dve", zz[0:n], pz[0:n, 0:256], bglu[0:n], ALU.add, [pz.r, bglu.r], [zz.r])
                    P.op("act", lambda e, n=n: e.activation(zz[0:n], zz[0:n], AF.Sigmoid), reads=[zz.r], writes=[zz.r])
                    tt("dve", ob[0:n, s_, :], ytm[0:n, s_, :], zz[0:n], ALU.mult, [ytm.r, zz.r], [ob.r])
                for sh_ in range(4):
                    P.dma("act", S["mix"].t[c0 * 16:(c0 + n) * 16, 384:640].rearrange("(c s) n -> c s n", s=16)[:, sh_ * 4:(sh_ + 1) * 4, :], ob[0:n, sh_ * 4:(sh_ + 1) * 4, :], reads=[ob.r], dwrites=[S["mix"].r])
            P.barrier()


def make_consts():
    import ml_dtypes
    idf = np.eye(128, dtype=np.float32)
    idb = idf.astype(ml_dtypes.bfloat16)
    m = np.zeros((NCST, 128, 128), np.float32)
    r = np.arange(128)[:, None]
    c = np.arange(128)[None, :]
    m[0] = (r <= c)
    m[1] = (r >= c)
    m[2] = (r <= c) * (-1.0 / 16.0)
    m[3] = (r >= c) * (-1.0 / 16.0)
    m[4] = (r > c) * (-1.0 / 16.0)
    m[5] = (r < c) * (-1.0 / 16.0)
    m[6] = (r > c)
    m[7] = (r < c)
    c5 = np.zeros((5, 128, 256), np.float32)
    c5[0, :, 0:128] = np.eye(128)
    for p in range(64):
        c5[0, 64 + p, 128 + p] = -1.0
        c5[0, p, 128 + 64 + p] = 1.0
    sl = np.arange(128)[:, None] // 16
    tt_ = np.arange(256)[None, :] // 16
    for kc in range(2):
        c5[1 + kc] = (tt_ >= kc * 8 + sl)
        c5[3 + kc] = (tt_ <= kc * 8 + sl)
    return idb, idf, m, c5


W_KEYS = ["ada_w", "ada_b", "norm_g", "w_in", "w_out", "ff_w_gate", "ff_w_up", "ff_w_down", "gla_w_gate", "gla_b_gate",
          "gla_norm_g", "s5_a_re", "s5_a_im", "s5_log_dt", "s5_b_re", "s5_b_im", "s5_c_re", "s5_c_im", "s5_d", "s5_w_glu",
          "s5_b_glu", "ssd_conv_w", "ssd_conv_b", "ssd_dt_bias", "ssd_a_log", "ssd_d", "ssd_norm_g"]


def make_in_maps(inputs, cores):
    idb, idf, m, c5 = make_consts()
    shared = {kk: np.ascontiguousarray(np.asarray(inputs[kk], dtype=np.float32)) for kk in W_KEYS}
    shared["final_norm_g"] = np.ascontiguousarray(np.asarray(inputs["final_norm_g"], np.float32).reshape(1, D))
    shared["cst_idb"] = idb
    shared["cst_idf"] = idf
    shared["cst_m"] = m
    shared["cst_s5"] = c5
    x = np.asarray(inputs["x"], np.float32)
    c = np.asarray(inputs["c"], np.float32)
    ctx = np.asarray(inputs["ctx"], np.float32)
    c_ctx = np.asarray(inputs["c_ctx"], np.float32)
    maps = []
    for b in cores:
        d = dict(shared)
        d["x"] = np.ascontiguousarray(x[b])
        d["ctx"] = np.ascontiguousarray(ctx[b])
        d["cc"] = np.ascontiguousarray(np.stack([c[b], c_ctx], 0))
        maps.append(d)
    return maps


def kernel(**inputs):
    nc = build()
    maps = make_in_maps(inputs, list(range(8)))
    res = run_bass_kernel_spmd(nc, maps, core_ids=list(range(8)))
    return np.stack([np.asarray(r["out"], np.float32) for r in res.results], 0)
```

```python
import contextlib
import numpy as np
import concourse.bass as bass
import concourse.mybir as mybir
from concourse.bass_utils import run_bass_kernel_spmd

F32 = mybir.dt.float32
BF16 = mybir.dt.bfloat16
AF = mybir.ActivationFunctionType
ALU = mybir.AluOpType

ENGS = ("pe", "act", "dve", "pool", "sp")
NDMA = 20


class Res:
    __slots__ = ("name", "lw", "rd")

    def __init__(self, name=""):
        self.name = name
        self.lw = None
        self.rd = []


SEMBLK = 12000


def _ck(key, eng):
    return key == eng or (isinstance(key, tuple) and key[0] == "c" and key[1] == eng)


class Prog:
    def __init__(self, nc):
        self.nc = nc
        self.ops = {e: [] for e in ENGS}
        self.cnt = {e: 0 for e in ENGS}
        self.known = {e: {} for e in ENGS}
        self.dma_n = {e: 0 for e in ENGS}
        self.dma_ev = {e: [None] * NDMA for e in ENGS}
        self.sems = {}
        self.ckeys = set()
        self.last_ev = {}

    def _need(self, eng, ev, waits):
        if ev is None:
            return
        e2, key, val, vc = ev
        if e2 == eng and eng == "pe" and _ck(key, "pe"):
            return
        kn = self.known[eng]
        if kn.get(key, 0) >= val:
            return
        if waits.get(key, 0) < val:
            waits[key] = val
        if vc:
            for k, v in vc.items():
                if kn.get(k, 0) < v:
                    kn[k] = v
        kn[key] = val

    def _deps(self, eng, reads, writes, waits, dwrites=()):
        for r in reads:
            for lw in (r.lw or ()):
                self._need(eng, lw, waits)
        for w in writes:
            for lw in (w.lw or ()):
                if not (lw[0] == eng and _ck(lw[1], eng)):
                    self._need(eng, lw, waits)
            for ev in w.rd:
                if ev[0] == eng and _ck(ev[1], eng):
                    continue
                self._need(eng, ev, waits)
        for w in dwrites:
            for ev in w.rd:
                if ev[0] == eng and _ck(ev[1], eng):
                    continue
                self._need(eng, ev, waits)

    @staticmethod
    def _compact(lst):
        best = {}
        for e in lst:
            if e[1] not in best or best[e[1]][2] < e[2]:
                best[e[1]] = e
        return list(best.values())

    def _commit(self, ev, reads, writes, dwrites=()):
        for w in dwrites:
            w.lw = (w.lw or []) + [ev]
            if len(w.lw) > 48:
                w.lw = self._compact(w.lw)
            w.rd = []
        for r in reads:
            r.rd.append(ev)
            if len(r.rd) > 40:
                r.rd = self._compact(r.rd)
        for w in writes:
            w.lw = [ev]
            w.rd = []

    def op(self, eng, fn, reads=(), writes=()):
        waits = {}
        self._deps(eng, reads, writes, waits)
        self.cnt[eng] += 1
        blk, off = divmod(self.cnt[eng] - 1, SEMBLK)
        key = eng if blk == 0 else ("c", eng, blk)
        self.ckeys.add(key)
        ev = (eng, key, off + 1, dict(self.known[eng]))
        self.last_ev[eng] = ev
        self.ops[eng].append((fn, list(waits.items()), (key, 1)))
        self._commit(ev, reads, writes)
        return ev

    def dma(self, eng, out, in_, reads=(), writes=(), dwrites=(), **kw):
        waits = {}
        self._deps(eng, reads, writes, waits, dwrites)
        n = self.dma_n[eng]
        self.dma_n[eng] += 1
        slot = n % NDMA
        gen = n // NDMA + 1
        key = ("dma", eng, slot)
        self._need(eng, self.dma_ev[eng][slot], waits)
        ev = (eng, key, 16 * gen, dict(self.known[eng]))
        self.dma_ev[eng][slot] = ev

        def fn(e, out=out, in_=in_, kw=kw):
            return e.dma_start(out=out, in_=in_, **kw)
        self.ops[eng].append((fn, list(waits.items()), (key, 16)))
        self._commit(ev, reads, writes, dwrites)
        return ev

    def barrier(self):
        evs = []
        for e in ENGS:
            if e in self.last_ev:
                evs.append(self.last_ev[e])
            for ev in self.dma_ev[e]:
                if ev is not None:
                    evs.append(ev)
        for e in ENGS:
            waits = {}
            for ev in evs:
                if ev[0] == e and _ck(ev[1], e):
                    continue
                kn = self.known[e]
                if kn.get(ev[1], 0) >= ev[2]:
                    continue
                waits[ev[1]] = max(waits.get(ev[1], 0), ev[2])
                kn[ev[1]] = ev[2]
            if waits:
                self.ops[e].append((None, list(waits.items()), None))

    def emit(self):
        nc = self.nc
        with contextlib.ExitStack() as st:
            keys = list(ENGS) + [kk for kk in self.ckeys if not isinstance(kk, str)]
            for e in ("sp", "act", "pool"):
                for s in range(NDMA):
                    keys.append(("dma", e, s))
            for k in keys:
                nm = k if isinstance(k, str) else "%s_%s_%d" % (k[0], k[1], k[2])
                self.sems[k] = st.enter_context(nc.semaphore("s_" + nm))
            fin = {}
            for e in ENGS:
                if e in self.last_ev:
                    fin[self.last_ev[e][1]] = self.last_ev[e][2]
                for ev in self.dma_ev[e]:
                    if ev is not None:
                        fin[ev[1]] = max(fin.get(ev[1], 0), ev[2])
            block = st.enter_context(nc.Block())
            sems = self.sems

            def run(engobj, name, extra_final=None):
                for fn, waits, inc in self.ops[name]:
                    for k, v in waits:
                        engobj.wait_ge(sems[k], v)
                    if fn is None:
                        continue
                    ins = fn(engobj)
                    ins.then_inc(sems[inc[0]], inc[1])
                if extra_final:
                    for k, v in extra_final.items():
                        engobj.wait_ge(sems[k], v)

            @block.tensor
            def _(e):
                run(e, "pe")

            @block.scalar
            def _(e):
                run(e, "act")

            @block.vector
            def _(e):
                run(e, "dve")

            @block.gpsimd
            def _(e):
                run(e, "pool")

            @block.sync
            def _(e):
                run(e, "sp", fin)


D = 1024
FF = 2816
NFC = 22
TC = 256
TL = 4096
T = TC + TL
DEPTH = 2
INW = 2732
Q0, K0, V0, R0, GL0, S50, Z0, XBC0, DT0 = 0, 192, 384, 768, 1152, 1184, 1440, 1824, 2720
EPS = 1e-6


class TB:
    def __init__(self, t, name=""):
        self.t = t
        self.r = Res(name)

    def __getitem__(self, k):
        return self.t[k]


class K:
    pass


def build(n_layers=DEPTH, debug=None, stop=None, mix_on=("gla", "ssd", "s5")):
    nc = bass.Bass("TRN2", target_bir_lowering=False)
    P = Prog(nc)
    k = K()
    k.nc, k.P = nc, P
    k.debug = debug or ()
    k.stop = stop
    k.mix_on = mix_on

    def din(name, shape, dt=F32):
        return nc.dram_tensor(name, list(shape), dt, kind="ExternalInput").ap()

    I = {}
    I["x"] = din("x", [TL, D])
    I["ctx"] = din("ctx", [TC, D])
    I["cc"] = din("cc", [2, D])
    I["ada_w"] = din("ada_w", [DEPTH, D, 9 * D])
    I["ada_b"] = din("ada_b", [DEPTH, 9 * D])
    I["norm_g"] = din("norm_g", [DEPTH, 3, D])
    I["w_in"] = din("w_in", [DEPTH, D, INW])
    I["w_out"] = din("w_out", [DEPTH, D, D])
    I["ff_w_gate"] = din("ff_w_gate", [DEPTH, 2, D, FF])
    I["ff_w_up"] = din("ff_w_up", [DEPTH, 2, D, FF])
    I["ff_w_down"] = din("ff_w_down", [DEPTH, 2, FF, D])
    I["gla_w_gate"] = din("gla_w_gate", [DEPTH, 2, 16, 192])
    I["gla_b_gate"] = din("gla_b_gate", [DEPTH, 2, 192])
    I["gla_norm_g"] = din("gla_norm_g", [DEPTH, 384])
    I["s5_a_re"] = din("s5_a_re", [DEPTH, 2, 16, 64])
    I["s5_a_im"] = din("s5_a_im", [DEPTH, 2, 16, 64])
    I["s5_log_dt"] = din("s5_log_dt", [DEPTH, 2, 16])
    I["s5_b_re"] = din("s5_b_re", [DEPTH, 16, 64, 16])
    I["s5_b_im"] = din("s5_b_im", [DEPTH, 16, 64, 16])
    I["s5_c_re"] = din("s5_c_re", [DEPTH, 16, 16, 64])
    I["s5_c_im"] = din("s5_c_im", [DEPTH, 16, 16, 64])
    I["s5_d"] = din("s5_d", [DEPTH, 256])
    I["s5_w_glu"] = din("s5_w_glu", [DEPTH, 256, 256])
    I["s5_b_glu"] = din("s5_b_glu", [DEPTH, 256])
    I["ssd_conv_w"] = din("ssd_conv_w", [DEPTH, 5, 896])
    I["ssd_conv_b"] = din("ssd_conv_b", [DEPTH, 896])
    I["ssd_dt_bias"] = din("ssd_dt_bias", [DEPTH, 2, 6])
    I["ssd_a_log"] = din("ssd_a_log", [DEPTH, 2, 6])
    I["ssd_d"] = din("ssd_d", [DEPTH, 6])
    I["ssd_norm_g"] = din("ssd_norm_g", [DEPTH, 384])
    I["final_norm_g"] = din("final_norm_g", [1, D])
    I["cst_idb"] = din("cst_idb", [128, 128], BF16)
    I["cst_idf"] = din("cst_idf", [128, 128])
    I["cst_m"] = din("cst_m", [NCST, 128, 128])
    I["cst_s5"] = din("cst_s5", [5, 128, 256])
    k.I = I
    k.out = nc.dram_tensor("out", [TL, D], F32, kind="ExternalOutput").ap()

    def dscr(name, shape, dt=F32):
        kind = "ExternalOutput" if name in k.debug else "Internal"
        return TB(nc.dram_tensor(name, list(shape), dt, kind=kind).ap(), name)

    S = {}
    S["hbuf"] = dscr("hbuf", [T, D])
    S["tm"] = dscr("tm", [T, INW])
    S["fm"] = dscr("fm", [416, T])
    S["mix"] = dscr("mix", [T, D], BF16)
    for l in range(DEPTH):
        for j in range(2):
            S["wg%d%d" % (l, j)] = dscr("wg%d%d" % (l, j), [D, FF], BF16)
            S["wu%d%d" % (l, j)] = dscr("wu%d%d" % (l, j), [D, FF], BF16)
            S["wd%d%d" % (l, j)] = dscr("wd%d%d" % (l, j), [FF, D], BF16)
        S["win%d" % l] = dscr("win%d" % l, [D, INW], BF16)
        S["wout%d" % l] = dscr("wout%d" % l, [D, D], BF16)
    S["gla_o"] = dscr("gla_o", [T, 384])
    S["ssd_y"] = dscr("ssd_y", [T, 384])
    k.S = S

    with contextlib.ExitStack() as gst:
        k.gst = gst
        k.idb = sbuf(k, gst, "idb", [128, 128], BF16)
        k.idf = sbuf(k, gst, "idf", [128, 128], F32)
        P.dma("sp", k.idb[:], I["cst_idb"], writes=[k.idb.r])
        P.dma("sp", k.idf[:], I["cst_idf"], writes=[k.idf.r])
        k.scl = sbuf(k, gst, "scl", [128, DEPTH, 2, 3, 8], F32)
        k.sh = sbuf(k, gst, "shf", [128, DEPTH, 2, 3, 8], F32)
        k.psb = [psum(k, gst, "psb%d" % i, [128, 512], F32) for i in range(8)]
        prologue(k, n_layers)
        pipeline(k, n_layers)
        P.emit()
    return nc


_uid = [0]


def sbuf(k, st, name, shape, dt):
    _uid[0] += 1
    return TB(st.enter_context(k.nc.sbuf_tensor("sb%d_%s" % (_uid[0], name), list(shape), dt)), name)


def psum(k, st, name, shape, dt):
    _uid[0] += 1
    return TB(st.enter_context(k.nc.psum_tensor("ps%d_%s" % (_uid[0], name), list(shape), dt)), name)


NCST = 8
_rr = [0]


def rr_eng(engs=("act", "dve")):
    _rr[0] += 1
    return engs[_rr[0] % len(engs)]


def copy_op(k, eng, out, in_, reads, writes):
    P = k.P
    if eng == "act":
        P.op("act", lambda e: e.copy(out, in_), reads=reads, writes=writes)
    elif eng == "dve":
        P.op("dve", lambda e: e.tensor_copy(out, in_), reads=reads, writes=writes)
    else:
        P.op("pool", lambda e: e.tensor_copy(out, in_), reads=reads, writes=writes)


def prologue(k, n_layers):
    P, I, S, nc = k.P, k.I, k.S, k.nc
    with contextlib.ExitStack() as st:
        cc = sbuf(k, st, "cc", [16, 128], F32)
        P.dma("sp", cc[:], I["cc"].rearrange("r (c p) -> (r c) p", p=128), writes=[cc.r])
        ccs = sbuf(k, st, "ccs", [16, 128], F32)
        P.op("act", lambda e: e.activation(ccs[:], cc[:], AF.Silu), reads=[cc.r], writes=[ccs.r])
        scT = sbuf(k, st, "scT", [128, 16], F32)
        P.op("pe", lambda e: e.transpose(k.psb[0][:, 0:16], ccs[:], k.idf[0:16, 0:16]), reads=[ccs.r, k.idf.r], writes=[k.psb[0].r])
        P.op("dve", lambda e: e.tensor_copy(scT[:], k.psb[0][:, 0:16]), reads=[k.psb[0].r], writes=[scT.r])
        scR = sbuf(k, st, "scR", [128, 8, 2], F32)
        P.op("dve", lambda e: e.tensor_copy(scR[:], scT[:].rearrange("p (r c) -> p c r", r=2)), reads=[scT.r], writes=[scR.r])
        adab = sbuf(k, st, "adab", [72, 128], F32)
        adabT = sbuf(k, st, "adabT", [128, 72], F32)
        ng = sbuf(k, st, "ng", [24, 128], F32)
        ngT = sbuf(k, st, "ngT", [128, 24], F32)
        modT = sbuf(k, st, "modT", [128, 72, 2], F32)
        slabs = [sbuf(k, st, "adaw%d" % i, [128, 8, 512], F32) for i in range(2)]
        stg = [sbuf(k, st, "cvs%d" % i, [128, 6144], F32) for i in range(3)]
        stb = [sbuf(k, st, "cvb%d" % i, [128, 6144], BF16) for i in range(3)]
        jobs_a, jobs_b = [], []
        for l_ in range(n_layers):
            for j_ in range(2):
                js = [(I["ff_w_gate"][l_, j_], S["wg%d%d" % (l_, j_)], D, FF), (I["ff_w_up"][l_, j_], S["wu%d%d" % (l_, j_)], D, FF),
                      (I["ff_w_down"][l_, j_], S["wd%d%d" % (l_, j_)], FF, D)]
                if j_ == 0:
                    js.append((I["w_in"][l_], S["win%d" % l_], D, INW))
                    js.append((I["w_out"][l_], S["wout%d" % l_], D, D))
                for jb in js:
                    (jobs_a if (l_ == 0 and j_ == 0 and jb[1] is not S["wout0"]) else jobs_b).append(jb)
        cv_iter = (th for jb in jobs_a + jobs_b for th in conv_chunks(k, stg, stb, *jb))
        k.conv_pending = []
        k.growd = TB(nc.dram_tensor("growd", [DEPTH, 2, 3, D], F32, kind="Internal").ap(), "growd")
        growsb = sbuf(k, st, "growsb", [1, 2, 2, 512], F32)
        gbias = sbuf(k, st, "gbias", [1, 3, D], F32)
        for l in range(n_layers):
            P.dma("sp", adab[:], I["ada_b"][l].rearrange("(c p) -> c p", p=128), writes=[adab.r])
            for gi_ in range(3):
                j_ = (2, 5, 8)[gi_]
                P.dma("sp", gbias[0:1, gi_, :], I["ada_b"][l, j_ * D:(j_ + 1) * D].rearrange("(o n) -> o n", o=1), writes=[gbias.r])
            P.op("pe", lambda e: e.transpose(k.psb[1][:, 0:72], adab[:], k.idf[0:72, 0:72]), reads=[adab.r, k.idf.r], writes=[k.psb[1].r])
            P.op("dve", lambda e: e.tensor_copy(adabT[:], k.psb[1][:, 0:72]), reads=[k.psb[1].r], writes=[adabT.r])
            P.dma("sp", ng[:], I["norm_g"][l].rearrange("j (c p) -> (j c) p", p=128), writes=[ng.r])
            P.op("pe", lambda e: e.transpose(k.psb[1][:, 128:152], ng[:], k.idf[0:24, 0:24]), reads=[ng.r, k.idf.r], writes=[k.psb[1].r])
            P.op("dve", lambda e: e.tensor_copy(ngT[:], k.psb[1][:, 128:152]), reads=[k.psb[1].r], writes=[ngT.r])
            for sl in range(18):
                sb_ = slabs[sl % 2]
                P.dma("sp", sb_[:], I["ada_w"][l].rearrange("(c p) n -> p c n", p=128)[:, :, sl * 512:(sl + 1) * 512], writes=[sb_.r])
                pm = k.psb[2 + (sl % 2)]
                for m in range(4):
                    for kc in range(8):
                        P.op("pe", lambda e, sb_=sb_, m=m, kc=kc, pm=pm: e.matmul(pm[:, m * 2:m * 2 + 2], sb_[:, kc, m * 128:(m + 1) * 128], scR[:, kc, :], start=(kc == 0), stop=(kc == 7)),
                             reads=[sb_.r, scR.r], writes=[pm.r])
                P.op("dve", lambda e, pm=pm, sl=sl: e.tensor_tensor(modT[:, sl * 4:sl * 4 + 4, :], pm[:, 0:8].rearrange("p (m r) -> p m r", r=2),
                                                                     adabT[:, sl * 4:sl * 4 + 4].unsqueeze(2).to_broadcast([128, 4, 2]), ALU.add),
                     reads=[pm.r, adabT.r], writes=[modT.r])
                j, half = divmod(sl, 2)
                if j in (2, 5, 8):
                    gi = (2, 5, 8).index(j)
                    for r in range(2):
                        pg = k.psb[4 + r]
                        for kc in range(8):
                            P.op("pe", lambda e, sb_=sb_, kc=kc, pg=pg, r=r: e.matmul(pg[0:1, :], scT[:, r * 8 + kc:r * 8 + kc + 1], sb_[:, kc, :], start=(kc == 0), stop=(kc == 7)),
                                 reads=[sb_.r, scT.r], writes=[pg.r])
                        P.op("dve", lambda e, pg=pg, r=r, half=half, gi=gi: e.tensor_tensor(growsb[0:1, r, half, :], pg[0:1, :], gbias[0:1, gi, half * 512:(half + 1) * 512], ALU.add),
                             reads=[pg.r, gbias.r], writes=[growsb.r])
                    if half == 1:
                        for r in range(2):
                            P.dma("act", k.growd[l, r, gi, :].rearrange("(o h n) -> o h n", o=1, h=2), growsb[0:1, r, :, :], reads=[growsb.r], dwrites=[k.growd.r])
                for _ in range(2):
                    th = next(cv_iter, None)
                    if th is not None:
                        th(("act", "dve"))
            mv = modT[:].rearrange("p (j c) r -> p j c r", c=8)
            for r in range(2):
                for s3 in range(3):
                    P.op("dve", lambda e, r=r, s3=s3, l=l: e.scalar_tensor_tensor(k.scl[:, l, r, s3, :], mv[:, 3 * s3 + 1, :, r], 1.0, ngT[:, s3 * 8:(s3 + 1) * 8], ALU.add, ALU.mult),
                         reads=[modT.r, ngT.r], writes=[k.scl.r])
                    P.op("dve", lambda e, r=r, s3=s3, l=l: e.tensor_copy(k.sh[:, l, r, s3, :], mv[:, 3 * s3, :, r]), reads=[modT.r], writes=[k.sh.r])
        for th in cv_iter:
            th(("act", "dve", "pool"))
        P.barrier()


_cv = [0]


def conv_chunks(k, stg, stb, src, dst, R, C):
    P = k.P
    nr_tot = R // 128
    nr = max(1, min(nr_tot, 6144 // C))
    sv = src.rearrange("(c p) n -> p c n", p=128)
    dv = dst.t.rearrange("(c p) n -> p c n", p=128)
    c0 = 0
    while c0 < nr_tot:
        n = min(nr, nr_tot - c0)

        def th(engs, c0=c0, n=n):
            i = _cv[0] % len(stg)
            _cv[0] += 1
            a, b = stg[i], stb[i]
            av = a[:, 0:n * C].rearrange("p (c n) -> p c n", n=C)
            bv = b[:, 0:n * C].rearrange("p (c n) -> p c n", n=C)
            P.dma("sp", av, sv[:, c0:c0 + n, :], writes=[a.r])
            eng = engs[_cv[0] % len(engs)]
            copy_op(k, eng, b[:, 0:n * C], a[:, 0:n * C], [a.r], [b.r])
            P.dma("act", dv[:, c0:c0 + n, :], bv, reads=[b.r], dwrites=[dst.r])
        yield th
        c0 += n


def alloc_pipe(k, st, n_layers, l):
    P, I = k.P, k.I
    k.hb = [sbuf(k, st, "h_t%d" % i, [128, 4, D], F32) for i in range(2)]
    for hb_ in k.hb:
        hb_.rh = [[Res("hsub0"), Res("hsub1")] for _ in range(4)]
    k.h = k.hb[0]
    k.tokbf = sbuf(k, st, "tokbf", [128, 4, D], BF16)
    k.tokr = [Res("tok%d" % i) for i in range(4)]
    k.uT = sbuf(k, st, "uT", [128, 8, 512], BF16)
    k.actT = sbuf(k, st, "actT", [128, NFC, 512], BF16)
    k.wg = [sbuf(k, st, "wg%d" % i, [128, 8, 512], BF16) for i in range(2)]
    k.wu = [sbuf(k, st, "wu%d" % i, [128, 8, 512], BF16) for i in range(2)]
    k.wd = [sbuf(k, st, "wd%d" % i, [128, D], BF16) for i in range(4)]
    k.win = [sbuf(k, st, "win%d" % i, [128, 8, 512], BF16) for i in range(3)]
    k.wout = sbuf(k, st, "wout", [128, 8, D], BF16)
    k.grow = sbuf(k, st, "grow", [128, 2, 4, D], F32)
    k.stage = [sbuf(k, st, "stg%d" % i, [128, 512], F32) for i in range(4)]
    k.tmp = [sbuf(k, st, "tmp%d" % i, [128, 512], F32) for i in range(2)]
    k.ss = sbuf(k, st, "ss", [128, 8], F32)
    k.mhalf = sbuf(k, st, "mhalf", [128, 4], F32)
    P.op("dve", lambda e: e.memset(k.mhalf[:], -0.5), writes=[k.mhalf.r])
    k.fng = sbuf(k, st, "fng", [128, D], F32)
    k.stg_i = 0
    P.dma("sp", k.fng[:], I["final_norm_g"].partition_broadcast(128), writes=[k.fng.r])
    for r in range(2):
        P.dma("sp", k.grow[:, r, 0:3, :].rearrange("p g n -> p (g n)"), k.growd.t[l, r].rearrange("g n -> (g n)").partition_broadcast(128),
              reads=[k.growd.r], writes=[k.grow.r])
        if l + 1 < n_layers:
            P.dma("sp", k.grow[:, r, 3, :], k.growd.t[l + 1, r, 0, :].partition_broadcast(128), reads=[k.growd.r], writes=[k.grow.r])
    for r in range(2):
        for g in (0, 2, 3):
            P.op("dve", lambda e, r=r, g=g: e.tensor_scalar(k.grow[:, r, g, :], k.grow[:, r, g, :], 0.5, None, ALU.mult), reads=[k.grow.r], writes=[k.grow.r])


def pipeline(k, n_layers):
    P, I, S, nc = k.P, k.I, k.S, k.nc
    tiles = [(0, TC, 1)] + [(TC + i * 512, 512, 0) for i in range(8)]
    with contextlib.ExitStack() as st:
        alloc_pipe(k, st, n_layers, 0)
        load_h(k, k.hb[0], *tiles[0], first=True)
        for i, (r0, nt, ic) in enumerate(tiles):
            k.h = k.hb[i % 2]
            if i + 1 < len(tiles):
                load_h(k, k.hb[(i + 1) % 2], *tiles[i + 1], first=True)
            ffn(k, 0, 0, nt, ic, 0)
            proj_in(k, 0, r0, nt, ic)
            store_h(k, k.h, r0, nt)
        P.barrier()
    if k.stop == "A0":
        return
    for l in range(n_layers):
        mixers(k, l)
        if k.stop == "M%d" % l:
            return
        last = (l == n_layers - 1)
        with contextlib.ExitStack() as st:
            alloc_pipe(k, st, n_layers, l)
            P.dma("sp", k.wout[:], S["wout%d" % l].t.rearrange("(c p) n -> p c n", p=128), reads=[S["wout%d" % l].r], writes=[k.wout.r])
            tl = [t_ for t_ in tiles if not (last and t_[2])]
            load_h(k, k.hb[0], *tl[0], first=False)
            for i, (r0, nt, ic) in enumerate(tl):
                k.h = k.hb[i % 2]
                if i + 1 < len(tl):
                    load_h(k, k.hb[(i + 1) % 2], *tl[i + 1], first=False)
                proj_out(k, l, r0, nt, ic)
                ffn(k, l, 1, nt, ic, 2)
                if last:
                    final_norm(k, r0, nt)
                else:
                    ffn(k, l + 1, 0, nt, ic, 3)
                    proj_in(k, l + 1, r0, nt, ic)
                    store_h(k, k.h, r0, nt)
            P.barrier()


def load_h(k, h, r0, nt, ic, first):
    P, I, S = k.P, k.I, k.S
    ns = nt // 128
    if first:
        src = I["ctx"] if ic else I["x"][r0 - TC:r0 - TC + nt, :]
        P.dma("sp", h[:, 0:ns, :], src.rearrange("(s p) n -> p s n", p=128), writes=[r_ for p_ in h.rh[0:ns] for r_ in p_])
    else:
        P.dma("sp", h[:, 0:ns, :], S["hbuf"].t[r0:r0 + nt, :].rearrange("(s p) n -> p s n", p=128), reads=[S["hbuf"].r], writes=[r_ for p_ in h.rh[0:ns] for r_ in p_])


def store_h(k, h, r0, nt):
    P, S = k.P, k.S
    ns = nt // 128
    P.dma("act", S["hbuf"].t[r0:r0 + nt, :].rearrange("(s p) n -> p s n", p=128), h[:, 0:ns, :], reads=[r_ for p_ in h.rh[0:ns] for r_ in p_], dwrites=[S["hbuf"].r])


def norm_T(k, l, s3, nt, ic):
    P = k.P
    h = k.h
    ns = nt // 128
    for s in range(ns):
        P.op("act", lambda e, s=s: e.activation(k.tokbf[:, s, :], h[:, s, :], AF.Square, accum_out=k.ss[:, s:s + 1]), reads=list(h.rh[s]), writes=[k.tokr[s], k.ss.r])
    P.op("dve", lambda e: e.tensor_scalar(k.ss[:, 4:4 + ns], k.ss[:, 0:ns], 1.0 / D, EPS, ALU.mult, ALU.add), reads=[k.ss.r], writes=[k.ss.r])
    P.op("pool", lambda e: e.tensor_tensor(k.ss[:, 4:4 + ns], k.ss[:, 4:4 + ns], k.mhalf[:, 0:ns], ALU.pow), reads=[k.ss.r, k.mhalf.r], writes=[k.ss.r])
    for s in range(ns):
        if s % 2 == 0:
            P.op("act", lambda e, s=s: e.activation(k.tokbf[:, s, :], h[:, s, :], AF.Copy, scale=k.ss[:, 4 + s:5 + s]), reads=list(h.rh[s]) + [k.ss.r], writes=[k.tokr[s]])
        else:
            P.op("dve", lambda e, s=s: e.tensor_scalar(k.tokbf[:, s, :], h[:, s, :], k.ss[:, 4 + s:5 + s], None, ALU.mult), reads=list(h.rh[s]) + [k.ss.r], writes=[k.tokr[s]])
    transpose_T(k, nt, (k.scl[:, l, ic, s3, :], k.sh[:, l, ic, s3, :]))


def transpose_T(k, nt, mod):
    P = k.P
    ns = nt // 128
    for kc in range(8):
        pb = k.psb[kc % 8]
        pv = pb[:].bitcast(BF16)
        for s in range(ns):
            P.op("pe", lambda e, s=s, kc=kc, pv=pv: e.transpose(pv[:, s * 128:(s + 1) * 128], k.tokbf[:, s, kc * 128:(kc + 1) * 128], k.idb[:]),
                 reads=[k.tokr[s], k.idb.r], writes=[pb.r])
        eng = ("act", "dve")[kc % 2]
        if mod is None:
            copy_op(k, eng, k.uT[:, kc, 0:nt], pv[:, 0:nt], [pb.r], [k.uT.r])
        else:
            scl, sh = mod
            if eng == "act":
                P.op("act", lambda e, kc=kc, pv=pv: e.activation(k.uT[:, kc, 0:nt], pv[:, 0:nt], AF.Identity, bias=sh[:, kc:kc + 1], scale=scl[:, kc:kc + 1]),
                     reads=[pb.r, k.scl.r, k.sh.r], writes=[k.uT.r])
            else:
                P.op("dve", lambda e, kc=kc, pv=pv: e.tensor_scalar(k.uT[:, kc, 0:nt], pv[:, 0:nt], scl[:, kc:kc + 1], sh[:, kc:kc + 1], ALU.mult, ALU.add),
                     reads=[pb.r, k.scl.r, k.sh.r], writes=[k.uT.r])


def ffn(k, l, j, nt, ic, gslot):
    P, S = k.P, k.S
    ns = nt // 128
    s3 = 0 if j == 0 else 2
    norm_T(k, l, s3, nt, ic)
    wgs, wus, wds = S["wg%d%d" % (l, j)], S["wu%d%d" % (l, j)], S["wd%d%d" % (l, j)]
    wgv = wgs.t.rearrange("(c p) n -> p c n", p=128)
    wuv = wus.t.rearrange("(c p) n -> p c n", p=128)
    nsl = 6
    for sl in range(nsl):
        c0 = sl * 512
        ncol = min(512, FF - c0)
        wg, wu = k.wg[sl % 2], k.wu[sl % 2]
        P.dma("sp", wg[:, :, 0:ncol], wgv[:, :, c0:c0 + ncol], reads=[wgs.r], writes=[wg.r])
        P.dma("sp", wu[:, :, 0:ncol], wuv[:, :, c0:c0 + ncol], reads=[wus.r], writes=[wu.r])
        for f4 in range(ncol // 128):
            fc = sl * 4 + f4
            pg, pu = k.psb[4 + 2 * (fc % 2)], k.psb[5 + 2 * (fc % 2)]
            for kc in range(8):
                P.op("pe", lambda e, wg=wg, kc=kc, f4=f4, pg=pg: e.matmul(pg[:, 0:nt], wg[:, kc, f4 * 128:(f4 + 1) * 128], k.uT[:, kc, 0:nt], start=(kc == 0), stop=(kc == 7)),
                     reads=[wg.r, k.uT.r], writes=[pg.r])
            for kc in range(8):
                P.op("pe", lambda e, wu=wu, kc=kc, f4=f4, pu=pu: e.matmul(pu[:, 0:nt], wu[:, kc, f4 * 128:(f4 + 1) * 128], k.uT[:, kc, 0:nt], start=(kc == 0), stop=(kc == 7)),
                     reads=[wu.r, k.uT.r], writes=[pu.r])
            tm = k.tmp[fc % 2]
            P.op("act", lambda e, pg=pg, tm=tm: e.activation(tm[:, 0:nt], pg[:, 0:nt], AF.Silu), reads=[pg.r], writes=[tm.r])
            P.op("dve", lambda e, pu=pu, tm=tm, fc=fc: e.tensor_tensor(k.actT[:, fc, 0:nt], tm[:, 0:nt], pu[:, 0:nt], ALU.mult), reads=[pu.r, tm.r], writes=[k.actT.r])
    for fc in range(NFC):
        wd = k.wd[fc % 4]
        P.dma("sp", wd[:], wds.t[fc * 128:(fc + 1) * 128, :], reads=[wds.r], writes=[wd.r])
        for s in range(ns):
            for hf in range(2):
                pb = k.psb[s * 2 + hf]
                P.op("pe", lambda e, wd=wd, s=s, hf=hf, fc=fc, pb=pb: e.matmul(pb[:], k.actT[:, fc, s * 128:(s + 1) * 128], wd[:, hf * 512:(hf + 1) * 512], start=(fc == 0), stop=(fc == NFC - 1)),
                     reads=[k.actT.r, wd.r], writes=[pb.r])
    residual(k, ns, ic, gslot)


def residual(k, ns, ic, gi):
    P = k.P
    h = k.h
    for s in range(ns):
        for hf in range(2):
            pb = k.psb[s * 2 + hf]
            tm = k.tmp[(s * 2 + hf) % 2]
            P.op("dve", lambda e, pb=pb, tm=tm, hf=hf: e.tensor_tensor(tm[:], pb[:], k.grow[:, ic, gi, hf * 512:(hf + 1) * 512], ALU.mult), reads=[pb.r, k.grow.r], writes=[tm.r])
            P.op(("pool", "dve")[(s * 2 + hf) % 2], lambda e, tm=tm, s=s, hf=hf: e.tensor_tensor(h[:, s, hf * 512:(hf + 1) * 512], h[:, s, hf * 512:(hf + 1) * 512], tm[:], ALU.add), reads=[tm.r, h.rh[s][hf]], writes=[h.rh[s][hf]])


def proj_in(k, l, r0, nt, ic):
    P, S = k.P, k.S
    ns = nt // 128
    norm_T(k, l, 1, nt, ic)
    ws = S["win%d" % l]
    wv = ws.t.rearrange("(c p) n -> p c n", p=128)
    nb = 0
    for sl in range(6):
        c0 = sl * 512
        ncol = min(512, INW - c0)
        w = k.win[sl % 3]
        P.dma("sp", w[:, :, 0:ncol], wv[:, :, c0:c0 + ncol], reads=[ws.r], writes=[w.r])
        for s in range(ns):
            pb = k.psb[nb % 8]
            nb += 1
            for kc in range(8):
                P.op("pe", lambda e, w=w, kc=kc, s=s, pb=pb, ncol=ncol: e.matmul(pb[:, 0:ncol], k.uT[:, kc, s * 128:(s + 1) * 128], w[:, kc, 0:ncol], start=(kc == 0), stop=(kc == 7)),
                     reads=[w.r, k.uT.r], writes=[pb.r])
            sg = k.stage[k.stg_i % 4]
            k.stg_i += 1
            copy_op(k, "act", sg[:, 0:ncol], pb[:, 0:ncol], [pb.r], [sg.r])
            P.dma("act", S["tm"].t[r0 + s * 128:r0 + (s + 1) * 128, c0:c0 + ncol], sg[:, 0:ncol], reads=[sg.r], dwrites=[S["tm"].r])
        fml = []
        if sl == 0:
            fml = [(0, 128, 0), (128, 128, 128), (256, 128, 256)]
        elif sl == 2:
            fml = [(GL0 - 1024, 32, 384)]
        for (cs, n, fr) in fml:
            pb = k.psb[nb % 8]
            nb += 1
            for kc in range(8):
                P.op("pe", lambda e, w=w, kc=kc, pb=pb, cs=cs, n=n: e.matmul(pb[0:n, 0:nt], w[:, kc, cs:cs + n], k.uT[:, kc, 0:nt], start=(kc == 0), stop=(kc == 7)),
                     reads=[w.r, k.uT.r], writes=[pb.r])
            sg = k.stage[k.stg_i % 4]
            k.stg_i += 1
            copy_op(k, "act", sg[0:n, 0:nt], pb[0:n, 0:nt], [pb.r], [sg.r])
            P.dma("act", S["fm"].t[fr:fr + n, r0:r0 + nt], sg[0:n, 0:nt], reads=[sg.r], dwrites=[S["fm"].r])


def proj_out(k, l, r0, nt, ic):
    P, S = k.P, k.S
    ns = nt // 128
    P.dma("sp", k.tokbf[:, 0:ns, :], S["mix"].t[r0:r0 + nt, :].rearrange("(s p) n -> p s n", p=128), reads=[S["mix"].r], writes=list(k.tokr[0:ns]))
    transpose_T(k, nt, None)
    for s in range(ns):
        for hf in range(2):
            pb = k.psb[s * 2 + hf]
            for kc in range(8):
                P.op("pe", lambda e, s=s, hf=hf, kc=kc, pb=pb: e.matmul(pb[:], k.uT[:, kc, s * 128:(s + 1) * 128], k.wout[:, kc, hf * 512:(hf + 1) * 512], start=(kc == 0), stop=(kc == 7)),
                     reads=[k.uT.r, k.wout.r], writes=[pb.r])
    residual(k, ns, ic, 1)


def final_norm(k, r0, nt):
    P = k.P
    h = k.h
    ns = nt // 128
    for s in range(ns):
        P.op("act", lambda e, s=s: e.activation(k.tokbf[:, s, :], h[:, s, :], AF.Square, accum_out=k.ss[:, s:s + 1]), reads=list(h.rh[s]), writes=[k.tokr[s], k.ss.r])
    P.op("dve", lambda e: e.tensor_scalar(k.ss[:, 4:4 + ns], k.ss[:, 0:ns], 1.0 / D, EPS, ALU.mult, ALU.add), reads=[k.ss.r], writes=[k.ss.r])
    P.op("pool", lambda e: e.tensor_tensor(k.ss[:, 4:4 + ns], k.ss[:, 4:4 + ns], k.mhalf[:, 0:ns], ALU.pow), reads=[k.ss.r, k.mhalf.r], writes=[k.ss.r])
    for s in range(ns):
        eng = "dve"
        P.op(eng, lambda e, s=s: e.scalar_tensor_tensor(h[:, s, :], h[:, s, :], k.ss[:, 4 + s:5 + s], k.fng[:], ALU.mult, ALU.mult), reads=list(h.rh[s]) + [k.ss.r, k.fng.r], writes=list(h.rh[s]))
    P.dma("act", k.out[r0 - TC:r0 - TC + nt, :].rearrange("(s p) n -> p s n", p=128), h[:, 0:ns, :], reads=[r_ for p_ in h.rh[0:ns] for r_ in p_])


def dbg_dump(k, name, tb, shape, dt=F32):
    if name not in k.debug:
        return
    d = k.nc.dram_tensor(name, list(shape), dt, kind="ExternalOutput").ap()
    flat = "p " + " ".join("a%d" % i for i in range(len(shape) - 1)) + " -> p (" + " ".join("a%d" % i for i in range(len(shape) - 1)) + ")"
    k.P.dma("sp", d.rearrange(flat) if len(shape) > 2 else d, tb[:].rearrange(flat) if len(shape) > 2 else tb[:], reads=[tb.r])


def mixers(k, l):
    with contextlib.ExitStack() as st:
        gens = []
        if "gla" in k.mix_on:
            gens.append(gla_gen(k, l, st))
        if "ssd" in k.mix_on:
            gens.append(ssd_gen(k, l, st))
        while gens:
            for g_ in list(gens):
                try:
                    next(g_)
                except StopIteration:
                    gens.remove(g_)
        k.P.barrier()
    if "s5" in k.mix_on:
        s5(k, l)


C_MASKF, C_MASKR, C_INCLF, C_INCLR, C_SUFX, C_PREX = 0, 1, 2, 3, 4, 5
NB = T // 128


def gla_gen(k, l, st):
    P, I, S, nc = k.P, k.I, k.S, k.nc
    if True:
        cm = sbuf(k, st, "gcm", [128, 6, 128], F32)
        P.dma("sp", cm[:], I["cst_m"][0:6].rearrange("c p n -> p c n"), writes=[cm.r])
        wgp = sbuf(k, st, "wgp", [33, 512], F32)
        P.op("dve", lambda e: e.memset(wgp[:], 0.0), writes=[wgp.r])
        for d in range(2):
            P.dma("sp", wgp[d * 16:(d + 1) * 16, d * 256:(d + 1) * 256].rearrange("p (h c) -> p h c", c=64)[:, :, 0:48],
                  I["gla_w_gate"][l, d].rearrange("p (h c) -> p h c", c=48), writes=[wgp.r])
            P.dma("sp", wgp[32:33, d * 256:(d + 1) * 256].rearrange("p (h c) -> p h c", c=64)[:, :, 0:48],
                  I["gla_b_gate"][l, d].rearrange("(o h c) -> o h c", o=1, c=48), writes=[wgp.r])
        gng = sbuf(k, st, "gng", [128, 384], F32)
        mhalf = sbuf(k, st, "gmhalf", [128, 4], F32)
        P.op("dve", lambda e: e.memset(mhalf[:], -0.5), writes=[mhalf.r])
        P.dma("sp", gng[:], I["gla_norm_g"][l].partition_broadcast(128), writes=[gng.r])
        o_res = [Res("glao%d" % b) for b in range(NB)]
        X = []
        for d in range(2):
            x = K()
            x.qT = [sbuf(k, st, "qT%d%d" % (d, i), [64, 4, 128], F32) for i in range(1)] * 2
            x.kT = [sbuf(k, st, "kT%d%d" % (d, i), [64, 4, 128], F32) for i in range(1)] * 2
            x.glr = [sbuf(k, st, "glr%d%d" % (d, i), [33, 128], F32) for i in range(1)] * 2
            x.tok = [sbuf(k, st, "tok%d%d" % (d, i), [128, 576], F32) for i in range(1)] * 2
            x.rt = sbuf(k, st, "rt%d" % d, [128, 384], F32)
            for i in range(2):
                P.op("pool", lambda e, t=x.qT[i]: e.memset(t[:], 0.0), writes=[x.qT[i].r])
                P.op("pool", lambda e, t=x.kT[i]: e.memset(t[:], 0.0), writes=[x.kT[i].r])
                P.op("pool", lambda e, t=x.glr[i]: e.memset(t[:], 1.0), writes=[x.glr[i].r])
            x.ex = sbuf(k, st, "ex%d" % d, [128, 256], F32)
            x.la = sbuf(k, st, "la%d" % d, [128, 256], F32)
            x.ecum = sbuf(k, st, "ecum%d" % d, [64, 4, 128], F32)
            x.eneg = sbuf(k, st, "eneg%d" % d, [64, 4, 128], F32)
            x.ekend = sbuf(k, st, "ekend%d" % d, [128, 256], F32)
            x.att = sbuf(k, st, "att%d" % d, [128, 4, 128], BF16)
            x.S = sbuf(k, st, "S%d" % d, [64, 4, 96], F32)
            x.Sbf = sbuf(k, st, "Sbf%d" % d, [64, 4, 96], BF16)
            x.pp = []
            for i in range(2):
                y = K()
                y.qdec = sbuf(k, st, "qdec%d%d" % (d, i), [64, 4, 128], BF16)
                y.kinv = sbuf(k, st, "kinv%d%d" % (d, i), [64, 4, 128], BF16)
                y.kend = sbuf(k, st, "kend%d%d" % (d, i), [128, 4, 48], BF16)
                y.vbf = sbuf(k, st, "vbf%d%d" % (d, i), [128, 384], BF16)
                y.dec = sbuf(k, st, "dec%d%d" % (d, i), [64, 4], F32)
                x.pp.append(y)
            x.ost = sbuf(k, st, "ost%d" % d, [128, 384], F32)
            x.oprev = sbuf(k, st, "oprev%d" % d, [128, 384], F32)
            x.sq = sbuf(k, st, "sq%d" % d, [128, 96], F32)
            x.ss = sbuf(k, st, "gss%d" % d, [128, 8], F32)
            x.sr = sbuf(k, st, "sr%d" % d, [128, 384], F32)
            x.y = sbuf(k, st, "gy%d" % d, [128, 384], BF16)
            P.op("dve", lambda e, x=x: e.memset(x.S[:], 0.0), writes=[x.S.r])
            P.op("dve", lambda e, x=x: e.memset(x.Sbf[:], 0.0), writes=[x.Sbf.r])
            x.pA, x.pB, x.pC, x.pD = [k.psb[d * 4 + i] for i in range(4)]
            X.append(x)
        fseq = list(range(NB))
        rseq = [1, 0] + list(range(NB - 1, 1, -1))
        visited = set()
        cv_iter = None
        if k.conv_pending:
            cstg = [sbuf(k, st, "gcvs%d" % i, [128, 6144], F32) for i in range(2)]
            cstb = [sbuf(k, st, "gcvb%d" % i, [128, 6144], BF16) for i in range(2)]
            pend, k.conv_pending = k.conv_pending, []
            cv_iter = (th for jb in pend for th in conv_chunks(k, cstg, cstb, *jb))
        def prep(d, step):
            b = (fseq, rseq)[d][step]
            x = X[d]
            par = step % 2
            y = x.pp[par]
            t0 = b * 128
            qT, kT, glr, tok = x.qT[par], x.kT[par], x.glr[par], x.tok[par]
            P.dma("sp", qT[0:48, :, :], S["fm"].t[0:192, t0:t0 + 128].rearrange("(h c) n -> c h n", c=48), reads=[S["fm"].r], writes=[qT.r])
            P.dma("sp", kT[0:48, :, :], S["fm"].t[192:384, t0:t0 + 128].rearrange("(h c) n -> c h n", c=48), reads=[S["fm"].r], writes=[kT.r])
            P.dma("sp", glr[0:32, :], S["fm"].t[384:416, t0:t0 + 128], reads=[S["fm"].r], writes=[glr.r])
            P.dma("sp", tok[:], S["tm"].t[t0:t0 + 128, K0:K0 + 576], reads=[S["tm"].r], writes=[tok.r])
            P.op("pe", lambda e, x=x, y=y, glr=glr, d=d: e.matmul(x.pA[:, 0:256], glr[0:33, :], wgp[0:33, d * 256:(d + 1) * 256], start=True, stop=True),
                 reads=[glr.r, wgp.r], writes=[x.pA.r])
            P.op("act", lambda e, x=x, y=y: e.activation(x.ex[:], x.pA[:, 0:256], AF.Exp, scale=-1.0), reads=[x.pA.r], writes=[x.ex.r])
            P.op("act", lambda e, x=x, y=y: e.activation(x.la[:], x.ex[:], AF.Ln, bias=1.0), reads=[x.ex.r], writes=[x.la.r])
            for h in range(4):
                P.op("pe", lambda e, x=x, y=y, h=h, d=d: e.matmul(x.pB[0:64, h * 128:(h + 1) * 128], x.la[:, h * 64:(h + 1) * 64], cm[:, C_INCLF + d, :], start=True, stop=True),
                     reads=[x.la.r, cm.r], writes=[x.pB.r])
            P.op("pe", lambda e, x=x, y=y, d=d: e.matmul(x.pA[:, 256:512], cm[:, C_SUFX + d, :], x.la[:], start=True, stop=True), reads=[x.la.r, cm.r], writes=[x.pA.r])
            P.op("act", lambda e, x=x, y=y: e.activation(x.ecum[:].rearrange("p g n -> p (g n)"), x.pB[0:64, :], AF.Exp), reads=[x.pB.r], writes=[x.ecum.r])
            P.op("act", lambda e, x=x, y=y: e.activation(x.eneg[:].rearrange("p g n -> p (g n)"), x.pB[0:64, :], AF.Exp, scale=-1.0), reads=[x.pB.r], writes=[x.eneg.r])
            P.op("act", lambda e, x=x, y=y: e.activation(x.ekend[:], x.pA[:, 256:512], AF.Exp), reads=[x.pA.r], writes=[x.ekend.r])
            P.op("dve", lambda e, x=x, y=y, qT=qT: e.scalar_tensor_tensor(y.qdec[:], qT[:], 48.0 ** -0.5, x.ecum[:], ALU.mult, ALU.mult), reads=[qT.r, x.ecum.r], writes=[y.qdec.r])
            P.op("pool", lambda e, x=x, y=y, kT=kT: e.tensor_tensor(y.kinv[:], kT[:], x.eneg[:], ALU.mult), reads=[kT.r, x.eneg.r], writes=[y.kinv.r])
            P.op("dve", lambda e, x=x, y=y, tok=tok: e.tensor_tensor(y.kend[:], tok[:, 0:192].rearrange("p (h c) -> p h c", c=48),
                                                                  x.ekend[:].rearrange("p (h c) -> p h c", c=64)[:, :, 0:48], ALU.mult), reads=[tok.r, x.ekend.r], writes=[y.kend.r])
            P.op("pool", lambda e, x=x, y=y, tok=tok: e.tensor_copy(y.vbf[:], tok[:, 192:576]), reads=[tok.r], writes=[y.vbf.r])
            lastcol = 127 if d == 0 else 0
            P.op("dve", lambda e, x=x, y=y, lastcol=lastcol: e.tensor_copy(y.dec[:], x.ecum[:, :, lastcol]), reads=[x.ecum.r], writes=[y.dec.r])

        def scan(d, step):
            b = (fseq, rseq)[d][step]
            x = X[d]
            par = step % 2
            y = x.pp[par]
            t0 = b * 128
            qT, kT, glr, tok = x.qT[par], x.kT[par], x.glr[par], x.tok[par]
            for h in range(4):
                g, hh = divmod(h, 2)
                P.op("pe", lambda e, x=x, y=y, g=g, hh=hh, h=h: e.matmul(x.pC[:, h * 128:(h + 1) * 128], y.kinv[0:48, h, :], y.qdec[0:48, h, :], start=True, stop=True),
                     reads=[y.kinv.r, y.qdec.r], writes=[x.pC.r])
            P.op("dve", lambda e, x=x, y=y, d=d: e.tensor_tensor(x.att[:], x.pC[:].rearrange("p (h n) -> p h n", n=128), cm[:, C_MASKF + d, :].unsqueeze(1).to_broadcast([128, 4, 128]), ALU.mult),
                 reads=[x.pC.r, cm.r], writes=[x.att.r])
            for h in range(4):
                g, hh = divmod(h, 2)
                P.op("pe", lambda e, x=x, y=y, h=h: e.matmul(x.pD[:, h * 96:(h + 1) * 96], x.att[:, h, :], y.vbf[:, h * 96:(h + 1) * 96], start=True, stop=False),
                     reads=[x.att.r, y.vbf.r], writes=[x.pD.r])
                P.op("pe", lambda e, x=x, y=y, h=h, g=g, hh=hh: e.matmul(x.pD[:, h * 96:(h + 1) * 96], y.qdec[0:48, h, :], x.Sbf[0:48, h, :], start=False, stop=True),
                     reads=[y.qdec.r, x.Sbf.r], writes=[x.pD.r])
            for h in range(4):
                g, hh = divmod(h, 2)
                P.op("pe", lambda e, x=x, y=y, h=h, g=g, hh=hh: e.matmul(x.pC[0:48, h * 96:(h + 1) * 96], y.kend[:, h, :], y.vbf[:, h * 96:(h + 1) * 96], start=True, stop=True),
                     reads=[y.kend.r, y.vbf.r], writes=[x.pC.r])
            for h in range(4):
                P.op("dve", lambda e, x=x, y=y, h=h: e.scalar_tensor_tensor(x.S[0:48, h, :], x.S[0:48, h, :], y.dec[0:48, h:h + 1], x.pC[0:48, h * 96:(h + 1) * 96], ALU.mult, ALU.add),
                     reads=[x.S.r, y.dec.r, x.pC.r], writes=[x.S.r])
            P.op("act", lambda e, x=x, y=y: e.copy(x.Sbf[:], x.S[:]), reads=[x.S.r], writes=[x.Sbf.r])
            if b not in visited:
                visited.add(b)
                P.op("act", lambda e, x=x, y=y: e.copy(x.ost[:], x.pD[:, 0:384]), reads=[x.pD.r], writes=[x.ost.r])
                P.dma("act", S["gla_o"].t[t0:t0 + 128, :], x.ost[:], reads=[x.ost.r], writes=[o_res[b]])
            else:
                if l == DEPTH - 1 and b < 2:
                    return
                P.dma("sp", x.oprev[:], S["gla_o"].t[t0:t0 + 128, :], reads=[o_res[b]], writes=[x.oprev.r])
                P.op("dve", lambda e, x=x, y=y: e.tensor_tensor(x.ost[:], x.pD[:, 0:384], x.oprev[:], ALU.add), reads=[x.pD.r, x.oprev.r], writes=[x.ost.r])
                for h in range(4):
                    P.op("act", lambda e, x=x, y=y, h=h: e.activation(x.sq[:], x.ost[:, h * 96:(h + 1) * 96], AF.Square, accum_out=x.ss[:, h:h + 1]), reads=[x.ost.r], writes=[x.sq.r, x.ss.r])
                P.op("dve", lambda e, x=x, y=y: e.tensor_scalar(x.ss[:, 4:8], x.ss[:, 0:4], 1.0 / 96, EPS, ALU.mult, ALU.add), reads=[x.ss.r], writes=[x.ss.r])
                P.op("pool", lambda e, x=x, y=y: e.tensor_tensor(x.ss[:, 4:8], x.ss[:, 4:8], mhalf[:, 0:4], ALU.pow), reads=[x.ss.r, mhalf.r], writes=[x.ss.r])
                P.dma("sp", x.rt[:], S["tm"].t[t0:t0 + 128, R0:R0 + 384], reads=[S["tm"].r], writes=[x.rt.r])
                P.op("act", lambda e, x=x, y=y: e.activation(x.sr[:], x.rt[:], AF.Exp, scale=-1.0), reads=[x.rt.r], writes=[x.sr.r])
                P.op("pool", lambda e, x=x, y=y: e.tensor_tensor(x.rt[:], x.rt[:], gng[:], ALU.mult), reads=[x.rt.r, gng.r], writes=[x.rt.r])
                P.op("dve", lambda e, x=x, y=y: e.tensor_scalar(x.sr[:], x.sr[:], 1.0, None, ALU.add), reads=[x.sr.r], writes=[x.sr.r])
                P.op("dve", lambda e, x=x, y=y: e.reciprocal(x.sr[:], x.sr[:]), reads=[x.sr.r], writes=[x.sr.r])
                P.op("dve", lambda e, x=x, y=y: e.tensor_tensor(x.sr[:], x.sr[:], x.rt[:], ALU.mult), reads=[x.sr.r, x.rt.r], writes=[x.sr.r])
                P.op("dve", lambda e, x=x, y=y: e.tensor_tensor(x.ost[:].rearrange("p (h c) -> p h c", c=96), x.ost[:].rearrange("p (h c) -> p h c", c=96),
                                                              x.ss[:, 4:8].unsqueeze(2).to_broadcast([128, 4, 96]), ALU.mult), reads=[x.ost.r, x.ss.r], writes=[x.ost.r])
                P.op("dve", lambda e, x=x, y=y: e.tensor_tensor(x.y[:], x.ost[:], x.sr[:], ALU.mult), reads=[x.ost.r, x.sr.r], writes=[x.y.r])
                P.dma("act", S["mix"].t[t0:t0 + 128, 0:384], x.y[:], reads=[x.y.r], dwrites=[S["mix"].r])

        for d in range(2):
            prep(d, 0)
        for step in range(NB):
            for d in range(2):
                if cv_iter is not None:
                    th = next(cv_iter, None)
                    if th is not None:
                        th(("act",))
                if step + 1 < NB:
                    prep(d, step + 1)
            for d in range(2):
                scan(d, step)
            yield
        if cv_iter is not None:
            for th in cv_iter:
                th(("act", "dve"))
        yield


C_STRF, C_STRR = 6, 7


def ssd_gen(k, l, st):
    P, I, S, nc = k.P, k.I, k.S, k.nc
    tm = S["tm"]
    tml = tm.t[TC:, :].rearrange("(r c) n -> c r n", c=64)
    mixl = S["mix"].t[TC:, :].rearrange("(r c) n -> c r n", c=64)

    def rows(view_c, view_l, b, c0, n):
        if b < 2:
            return [(0, 128, view_c[b * 128:(b + 1) * 128, c0:c0 + n])]
        bb = b - 2
        return [(0, 64, view_l[2 * bb][:, c0:c0 + n]), (64, 64, view_l[2 * bb + 1][:, c0:c0 + n])]

    if True:
        cm = sbuf(k, st, "scm", [128, 8, 128], F32)
        P.dma("sp", cm[:], I["cst_m"][0:8].rearrange("c p n -> p c n"), writes=[cm.r])
        maskb = sbuf(k, st, "maskb", [128, 2, 128], BF16)
        P.op("dve", lambda e: e.tensor_copy(maskb[:], cm[:, 0:2, :]), reads=[cm.r], writes=[maskb.r])
        cwr = sbuf(k, st, "cwr", [8, 896], F32)
        P.op("dve", lambda e: e.memset(cwr[:], 0.0), writes=[cwr.r])
        P.dma("sp", cwr[0:5, :], I["ssd_conv_w"][l], writes=[cwr.r])
        P.dma("sp", cwr[5:6, :], I["ssd_conv_b"][l].rearrange("(o n) -> o n", o=1), writes=[cwr.r])
        cw = sbuf(k, st, "cw", [128, 7, 8], F32)
        for cc in range(7):
            P.op("pe", lambda e, cc=cc: e.transpose(k.psb[0][:, cc * 8:(cc + 1) * 8], cwr[0:8, cc * 128:(cc + 1) * 128], k.idf[0:8, 0:8]), reads=[cwr.r, k.idf.r], writes=[k.psb[0].r])
        P.op("dve", lambda e: e.tensor_copy(cw[:].rearrange("p c k -> p (c k)"), k.psb[0][:, 0:56]), reads=[k.psb[0].r], writes=[cw.r])
        dtb = sbuf(k, st, "dtb", [128, 12], F32)
        negA = sbuf(k, st, "negA", [128, 12], F32)
        dsk = sbuf(k, st, "dsk", [128, 6], F32)
        sng = sbuf(k, st, "sng", [128, 384], F32)
        smhalf = sbuf(k, st, "smhalf", [128, 4], F32)
        P.op("dve", lambda e: e.memset(smhalf[:], -0.5), writes=[smhalf.r])
        P.dma("sp", dtb[:], I["ssd_dt_bias"][l].rearrange("d h -> (d h)").partition_broadcast(128), writes=[dtb.r])
        P.dma("sp", negA[:], I["ssd_a_log"][l].rearrange("d h -> (d h)").partition_broadcast(128), writes=[negA.r])
        P.dma("sp", dsk[:], I["ssd_d"][l].partition_broadcast(128), writes=[dsk.r])
        P.dma("sp", sng[:], I["ssd_norm_g"][l].partition_broadcast(128), writes=[sng.r])
        P.op("act", lambda e: e.activation(negA[:], negA[:], AF.Exp), reads=[negA.r], writes=[negA.r])
        P.op("dve", lambda e: e.tensor_scalar(negA[:], negA[:], -1.0, None, ALU.mult), reads=[negA.r], writes=[negA.r])
        xs_tm = sbuf(k, st, "xs_tm", [128, NB, 384], BF16)
        bm_tm = sbuf(k, st, "bm_tm", [128, NB, 256], BF16)
        bmT = sbuf(k, st, "bmT", [128, 2, T], BF16)
        cmT = sbuf(k, st, "cmT", [128, 2, T], BF16)
        dt_all = sbuf(k, st, "dt_all", [128, NB, 12], F32)
        la_all = sbuf(k, st, "la_all", [128, NB, 12], F32)
        with contextlib.ExitStack() as st1:
            xT = [sbuf(k, st1, "xTs%d" % i, [128, 7, 516], F32) for i in range(2)]
            xld = [sbuf(k, st1, "xld%d" % i, [128, 908], F32) for i in range(2)]
            acc = [sbuf(k, st1, "cacc%d" % i, [128, 512], F32) for i in range(2)]
            xsT = [sbuf(k, st1, "xsT%d" % i, [128, 512], F32) for i in range(2)]
            dte = sbuf(k, st1, "dte", [128, 12], F32)
            segs = [[0, 1]] + [[2 + 4 * s_ + j for j in range(4)] for s_ in range(8)]
            nld = [0]

            def stageA(si):
                seg = segs[si]
                buf = xT[si % 2]
                first = si in (0, 1)
                if first:
                    P.op("pool", lambda e, buf=buf: e.memset(buf[:, :, 0:2], 0.0), writes=[buf.r])
                else:
                    pb_ = xT[(si - 1) % 2]
                    P.op("pool", lambda e, buf=buf, pb_=pb_: e.tensor_copy(buf[:, :, 0:2], pb_[:, :, 512:514]), reads=[pb_.r], writes=[buf.r])
                for j, b in enumerate(seg):
                    xl = xld[nld[0] % 2]
                    nld[0] += 1
                    for (p0, np_, ap) in rows(tm.t, tml, b, XBC0, 908):
                        P.dma("sp", xl[p0:p0 + np_, :], ap, reads=[tm.r], writes=[xl.r])
                    P.op("dve", lambda e, xl=xl: e.tensor_tensor(dte[:], xl[:, 896:908], dtb[:], ALU.add), reads=[xl.r, dtb.r], writes=[dte.r])
                    P.op("act", lambda e: e.activation(dte[:], dte[:], AF.Exp), reads=[dte.r], writes=[dte.r])
                    P.op("act", lambda e, b=b: e.activation(dt_all[:, b, :], dte[:], AF.Ln, bias=1.0), reads=[dte.r], writes=[dt_all.r])
                    P.op("dve", lambda e, b=b: e.tensor_tensor(la_all[:, b, :], dt_all[:, b, :], negA[:], ALU.mult), reads=[dt_all.r, negA.r], writes=[la_all.r])
                    for cc in range(7):
                        pb = k.psb[1 + (cc % 2)]
                        P.op("pe", lambda e, xl=xl, cc=cc, pb=pb: e.transpose(pb[:, 0:128], xl[:, cc * 128:(cc + 1) * 128], k.idf[:]), reads=[xl.r, k.idf.r], writes=[pb.r])
                        copy_op(k, ("act", "dve")[cc % 2], buf[:, cc, 2 + j * 128:2 + (j + 1) * 128], pb[:, 0:128], [pb.r], [buf.r])

            def stageB(si):
                seg = segs[si]
                buf = xT[si % 2]
                N = 128 * len(seg)
                last = si in (0, 8)
                if last:
                    P.op("pool", lambda e, buf=buf, N=N: e.memset(buf[:, :, 2 + N:4 + N], 0.0), writes=[buf.r])
                else:
                    nb_ = xT[(si + 1) % 2]
                    P.op("pool", lambda e, buf=buf, nb_=nb_, N=N: e.tensor_copy(buf[:, :, 2 + N:4 + N], nb_[:, :, 2:4]), reads=[nb_.r], writes=[buf.r])
                t0 = seg[0] * 128
                for cc2 in range(0, 7, 2):
                    ccs = [c_ for c_ in (cc2, cc2 + 1) if c_ < 7]
                    for cc in ccs:
                        ac = acc[cc % 2]
                        P.op("dve", lambda e, ac=ac, buf=buf, cc=cc, N=N: e.tensor_scalar(ac[:, 0:N], buf[:, cc, 0:N], cw[:, cc, 0:1], None, ALU.mult), reads=[buf.r, cw.r], writes=[ac.r])
                    for kk in range(1, 5):
                        for cc in ccs:
                            ac = acc[cc % 2]
                            P.op("dve", lambda e, ac=ac, buf=buf, cc=cc, N=N, kk=kk: e.scalar_tensor_tensor(ac[:, 0:N], buf[:, cc, kk:kk + N], cw[:, cc, kk:kk + 1], ac[:, 0:N], ALU.mult, ALU.add),
                                 reads=[buf.r, cw.r, ac.r], writes=[ac.r])
                    for cc in ccs:
                        ac = acc[cc % 2]
                        if cc < 3:
                            xo = xsT[cc % 2]
                            P.op("act", lambda e, ac=ac, xo=xo, cc=cc, N=N: e.activation(xo[:, 0:N], ac[:, 0:N], AF.Silu, bias=cw[:, cc, 5:6]), reads=[ac.r, cw.r], writes=[xo.r])
                            for j, b in enumerate(seg):
                                pb = k.psb[3 + (j % 2)]
                                P.op("pe", lambda e, xo=xo, j=j, pb=pb: e.transpose(pb[:, 0:128], xo[:, j * 128:(j + 1) * 128], k.idf[:]), reads=[xo.r, k.idf.r], writes=[pb.r])
                                copy_op(k, ("act", "dve")[j % 2], xs_tm[:, b, cc * 128:(cc + 1) * 128], pb[:, 0:128], [pb.r], [xs_tm.r])
                        elif cc < 5:
                            g = cc - 3
                            P.op("act", lambda e, ac=ac, g=g, cc=cc, N=N, t0=t0: e.activation(bmT[:, g, t0:t0 + N], ac[:, 0:N], AF.Silu, bias=cw[:, cc, 5:6]), reads=[ac.r, cw.r], writes=[bmT.r])
                            for j, b in enumerate(seg):
                                pb = k.psb[5 + (j % 2)]
                                pv = pb[:].bitcast(BF16)
                                P.op("pe", lambda e, g=g, b=b, pv=pv: e.transpose(pv[:, 0:128], bmT[:, g, b * 128:(b + 1) * 128], k.idb[:]), reads=[bmT.r, k.idb.r], writes=[pb.r])
                                copy_op(k, ("act", "dve")[j % 2], bm_tm[:, b, g * 128:(g + 1) * 128], pv[:, 0:128], [pb.r], [bm_tm.r])
                        else:
                            g = cc - 5
                            P.op("act", lambda e, ac=ac, g=g, cc=cc, N=N, t0=t0: e.activation(cmT[:, g, t0:t0 + N], ac[:, 0:N], AF.Silu, bias=cw[:, cc, 5:6]), reads=[ac.r, cw.r], writes=[cmT.r])

            stageA(0)
            stageB(0)
            yield
            stageA(1)
            for si in range(2, 9):
                stageA(si)
                stageB(si - 1)
                yield
            stageB(8)
            P.barrier()
        y_res = [Res("ssdy%d" % b) for b in range(NB)]
        X = []
        pY, pYo, pS, pSm = k.psb[4], k.psb[5], k.psb[6], k.psb[7]
        for d in range(2):
            x = K()
            x.lhs = sbuf(k, st, "slhs%d" % d, [128, 6, 128], F32)
            x.E = sbuf(k, st, "sE%d" % d, [128, 6, 128], BF16)
            x.sm = sbuf(k, st, "ssm%d" % d, [128, 2, 128], BF16)
            x.pp = []
            for i in range(2):
                y = K()
                y.att = sbuf(k, st, "satt%d%d" % (d, i), [128, 6, 128], BF16)
                y.xin = sbuf(k, st, "xin%d%d" % (d, i), [128, 6, 64], BF16)
                y.xw = sbuf(k, st, "xw%d%d" % (d, i), [128, 6, 64], BF16)
                y.sc = sbuf(k, st, "ssc%d%d" % (d, i), [128, 24], F32)
                x.pp.append(y)
            x.S = sbuf(k, st, "sS%d" % d, [128, 6, 64], F32)
            x.Sbf = sbuf(k, st, "sSbf%d" % d, [128, 6, 64], BF16)
            x.yo = sbuf(k, st, "syo%d" % d, [128, 6, 64], F32)
            x.ys = sbuf(k, st, "sys%d" % d, [128, 384], F32)
            x.yp = sbuf(k, st, "syp%d" % d, [128, 384], F32)
            x.zt = sbuf(k, st, "szt%d" % d, [128, 384], F32)
            x.jk = sbuf(k, st, "sjk%d" % d, [128, 384], F32)
            x.ss = sbuf(k, st, "sss%d" % d, [128, 4], F32)
            P.op("dve", lambda e, x=x: e.memset(x.ss[:], 1.0), writes=[x.ss.r])
            x.yb = sbuf(k, st, "syb%d" % d, [128, 384], BF16)
            P.op("dve", lambda e, x=x: e.memset(x.S[:], 0.0), writes=[x.S.r])
            P.op("dve", lambda e, x=x: e.memset(x.Sbf[:], 0.0), writes=[x.Sbf.r])
            x.bX, x.bY = k.psb[2 * d], k.psb[2 * d + 1]
            X.append(x)
        fseq = list(range(NB))
        rseq = [1, 0] + list(range(NB - 1, 1, -1))
        visited = set()
        def prep(d, step):
            b = (fseq, rseq)[d][step]
            x = X[d]
            t0 = b * 128
            la_d = la_all[:, b, d * 6:(d + 1) * 6]
            dt_d = dt_all[:, b, d * 6:(d + 1) * 6]
            y = x.pp[step % 2]
            xsv = xs_tm[:, b, :].rearrange("p (h c) -> p h c", c=64)
            P.op("pe", lambda e, d=d, la_d=la_d: e.matmul(pSm[:, 0:6], cm[:, C_MASKF + d, :], la_d, start=True, stop=True), reads=[cm.r, la_all.r], writes=[pSm.r])
            P.op("pe", lambda e, d=d, la_d=la_d: e.matmul(pSm[:, 6:12], cm[:, C_STRF + d, :], la_d, start=True, stop=True), reads=[cm.r, la_all.r], writes=[pSm.r])
            P.op("act", lambda e, x=x, y=y: e.activation(y.sc[:, 0:12], pSm[:, 0:12], AF.Exp), reads=[pSm.r], writes=[y.sc.r])
            P.op("dve", lambda e, x=x, y=y: e.tensor_tensor(y.sc[:, 12:18], y.sc[:, 0:6], y.sc[:, 6:12], ALU.mult), reads=[y.sc.r], writes=[y.sc.r])
            P.op("dve", lambda e, x=x, y=y, dt_d=dt_d: e.tensor_tensor(y.sc[:, 18:24], y.sc[:, 6:12], dt_d, ALU.mult), reads=[y.sc.r, dt_all.r], writes=[y.sc.r])
            P.op("pool", lambda e, x=x, y=y, d=d, la_d=la_d: e.tensor_tensor(x.lhs[:], la_d.unsqueeze(2).to_broadcast([128, 6, 128]), cm[:, C_STRF + d, :].unsqueeze(1).to_broadcast([128, 6, 128]), ALU.mult),
                 reads=[la_all.r, cm.r], writes=[x.lhs.r])
            for h in range(6):
                pb, c0 = (x.bX, h * 128) if h < 4 else (x.bY, (h - 4) * 128)
                P.op("pe", lambda e, x=x, y=y, h=h, pb=pb, c0=c0, d=d: e.matmul(pb[:, c0:c0 + 128], x.lhs[:, h, :], cm[:, C_MASKF + d, :], start=True, stop=True),
                     reads=[x.lhs.r, cm.r], writes=[pb.r])
            P.op("act", lambda e, x=x, y=y: e.activation(x.E[:, 0:4, :].rearrange("p h n -> p (h n)"), x.bX[:], AF.Exp), reads=[x.bX.r], writes=[x.E.r])
            P.op("act", lambda e, x=x, y=y: e.activation(x.E[:, 4:6, :].rearrange("p h n -> p (h n)"), x.bY[:, 0:256], AF.Exp), reads=[x.bY.r], writes=[x.E.r])
            for g in range(2):
                P.op("pe", lambda e, x=x, y=y, g=g, t0=t0: e.matmul(x.bY[:, 256 + g * 128:256 + (g + 1) * 128], bmT[:, g, t0:t0 + 128], cmT[:, g, t0:t0 + 128], start=True, stop=True),
                     reads=[bmT.r, cmT.r], writes=[x.bY.r])
            P.op("dve", lambda e, x=x, y=y, d=d: e.tensor_tensor(x.sm[:], x.bY[:, 256:512].rearrange("p (g n) -> p g n", g=2), maskb[:, d, :].unsqueeze(1).to_broadcast([128, 2, 128]), ALU.mult),
                 reads=[x.bY.r, maskb.r], writes=[x.sm.r])
            for g in range(2):
                P.op(("dve", "pool")[g], lambda e, x=x, y=y, g=g: e.tensor_tensor(y.att[:, 3 * g:3 * g + 3, :], x.E[:, 3 * g:3 * g + 3, :], x.sm[:, g, :].unsqueeze(1).to_broadcast([128, 3, 128]), ALU.mult),
                     reads=[x.E.r, x.sm.r], writes=[y.att.r])
            P.op("dve", lambda e, x=x, y=y, xsv=xsv, dt_d=dt_d: e.tensor_tensor(y.xin[:], xsv, dt_d.unsqueeze(2).to_broadcast([128, 6, 64]), ALU.mult), reads=[xs_tm.r, dt_all.r], writes=[y.xin.r])
            P.op("pool", lambda e, x=x, y=y, xsv=xsv: e.tensor_tensor(y.xw[:], xsv, y.sc[:, 18:24].unsqueeze(2).to_broadcast([128, 6, 64]), ALU.mult), reads=[xs_tm.r, y.sc.r], writes=[y.xw.r])

        def scan(d, step):
            b = (fseq, rseq)[d][step]
            x = X[d]
            t0 = b * 128
            la_d = la_all[:, b, d * 6:(d + 1) * 6]
            dt_d = dt_all[:, b, d * 6:(d + 1) * 6]
            y = x.pp[step % 2]
            xsv = xs_tm[:, b, :].rearrange("p (h c) -> p h c", c=64)
            for h in range(6):
                P.op("pe", lambda e, x=x, y=y, h=h: e.matmul(pY[:, h * 64:(h + 1) * 64], y.att[:, h, :], y.xin[:, h, :], start=True, stop=True), reads=[y.att.r, y.xin.r], writes=[pY.r])
            for h in range(6):
                g = h // 3
                P.op("pe", lambda e, x=x, y=y, h=h, g=g, t0=t0: e.matmul(pYo[:, h * 64:(h + 1) * 64], cmT[:, g, t0:t0 + 128], x.Sbf[:, h, :], start=True, stop=True), reads=[cmT.r, x.Sbf.r], writes=[pYo.r])
            P.op("dve", lambda e, x=x, y=y: e.tensor_tensor(x.yo[:], pYo[:, 0:384].rearrange("p (h c) -> p h c", c=64), y.sc[:, 0:6].unsqueeze(2).to_broadcast([128, 6, 64]), ALU.mult),
                 reads=[pYo.r, y.sc.r], writes=[x.yo.r])
            P.op("dve", lambda e, x=x, y=y: e.tensor_tensor(x.ys[:], pY[:, 0:384], x.yo[:].rearrange("p h c -> p (h c)"), ALU.add), reads=[pY.r, x.yo.r], writes=[x.ys.r])
            for h in range(6):
                g = h // 3
                P.op("pe", lambda e, x=x, y=y, h=h, g=g, b=b: e.matmul(pS[:, h * 64:(h + 1) * 64], bm_tm[:, b, g * 128:(g + 1) * 128], y.xw[:, h, :], start=True, stop=True), reads=[bm_tm.r, y.xw.r], writes=[pS.r])
            P.op("pool", lambda e, x=x, y=y: e.tensor_tensor(x.S[:], x.S[:], y.sc[:, 12:18].unsqueeze(2).to_broadcast([128, 6, 64]), ALU.mult), reads=[x.S.r, y.sc.r], writes=[x.S.r])
            P.op("dve", lambda e, x=x, y=y: e.tensor_tensor(x.S[:], x.S[:], pS[:, 0:384].rearrange("p (h c) -> p h c", c=64), ALU.add), reads=[x.S.r, pS.r], writes=[x.S.r])
            P.op("act", lambda e, x=x, y=y: e.copy(x.Sbf[:], x.S[:]), reads=[x.S.r], writes=[x.Sbf.r])
            if b not in visited:
                visited.add(b)
                P.op("pool", lambda e, x=x, y=y, xsv=xsv: e.tensor_tensor(x.yo[:], xsv, dsk[:].unsqueeze(2).to_broadcast([128, 6, 64]), ALU.mult), reads=[xs_tm.r, dsk.r, x.yo.r], writes=[x.yo.r])
                P.op("pool", lambda e, x=x, y=y: e.tensor_tensor(x.ys[:], x.ys[:], x.yo[:].rearrange("p h c -> p (h c)"), ALU.add), reads=[x.ys.r, x.yo.r], writes=[x.ys.r])
                P.dma("act", S["ssd_y"].t[t0:t0 + 128, :], x.ys[:], reads=[x.ys.r], writes=[y_res[b]])
            else:
                if l == DEPTH - 1 and b < 2:
                    return
                P.dma("sp", x.yp[:], S["ssd_y"].t[t0:t0 + 128, :], reads=[y_res[b]], writes=[x.yp.r])
                for (p0, np_, ap) in rows(tm.t, tml, b, Z0, 384):
                    P.dma("sp", x.zt[p0:p0 + np_, :], ap, reads=[tm.r], writes=[x.zt.r])
                P.op("dve", lambda e, x=x, y=y: e.tensor_tensor(x.ys[:], x.ys[:], x.yp[:], ALU.add), reads=[x.ys.r, x.yp.r], writes=[x.ys.r])
                P.op("act", lambda e, x=x, y=y: e.activation(x.jk[:], x.zt[:], AF.Exp, scale=-1.0), reads=[x.zt.r], writes=[x.jk.r])
                P.op("pool", lambda e, x=x, y=y: e.tensor_tensor(x.ys[:], x.ys[:], x.zt[:], ALU.mult), reads=[x.ys.r, x.zt.r], writes=[x.ys.r])
                P.op("dve", lambda e, x=x, y=y: e.tensor_scalar(x.jk[:], x.jk[:], 1.0, None, ALU.add), reads=[x.jk.r], writes=[x.jk.r])
                P.op("dve", lambda e, x=x, y=y: e.reciprocal(x.jk[:], x.jk[:]), reads=[x.jk.r], writes=[x.jk.r])
                P.op("dve", lambda e, x=x, y=y: e.tensor_tensor(x.ys[:], x.ys[:], x.jk[:], ALU.mult), reads=[x.ys.r, x.jk.r], writes=[x.ys.r])
                P.op("act", lambda e, x=x, y=y: e.activation(x.jk[:], x.ys[:], AF.Square, accum_out=x.ss[:, 0:1]), reads=[x.ys.r], writes=[x.jk.r, x.ss.r])
                P.op("dve", lambda e, x=x, y=y: e.tensor_scalar(x.ss[:, 1:2], x.ss[:, 0:1], 1.0 / 384, EPS, ALU.mult, ALU.add), reads=[x.ss.r], writes=[x.ss.r])
                P.op("pool", lambda e, x=x, y=y: e.tensor_tensor(x.ss[:, 0:4], x.ss[:, 0:4], smhalf[:, 0:4], ALU.pow), reads=[x.ss.r, smhalf.r], writes=[x.ss.r])
                P.op("dve", lambda e, x=x, y=y: e.scalar_tensor_tensor(x.yb[:], x.ys[:], x.ss[:, 1:2], sng[:], ALU.mult, ALU.mult), reads=[x.ys.r, x.ss.r, sng.r], writes=[x.yb.r])
                for (p0, np_, ap) in rows(S["mix"].t, mixl, b, 640, 384):
                    P.dma("act", ap, x.yb[p0:p0 + np_, :], reads=[x.yb.r], dwrites=[S["mix"].r])

        for d in range(2):
            prep(d, 0)
        for step in range(NB):
            for d in range(2):
                if step + 1 < NB:
                    prep(d, step + 1)
            for d in range(2):
                scan(d, step)
            yield
        yield


def s5(k, l):
    P, I, S, nc = k.P, k.I, k.S, k.nc
    tm = S["tm"]
    NCH = T // 16
    cbs = [(0, 128), (128, 128), (256, 16)]
    TWO_PI = 6.283185307179586

    def tt(eng, out, a, b, op, reads, writes):
        P.op(eng, lambda e: e.tensor_tensor(out, a, b, op), reads=reads, writes=writes)

    with contextlib.ExitStack() as st:
        cs = sbuf(k, st, "s5cs", [128, 5, 256], F32)
        P.dma("sp", cs[:], I["cst_s5"].rearrange("c p n -> p c n"), writes=[cs.r])
        Yacc = sbuf(k, st, "Yacc", [128, 2, 16, NCH], F32)
        U = sbuf(k, st, "s5U", [128, 16, 2, NCH], BF16)
        wglu = sbuf(k, st, "wglu", [128, 2, 256], BF16)
        bglu = sbuf(k, st, "bglu", [128, 256], F32)
        dsk = sbuf(k, st, "s5dsk", [128, 256], F32)
        P.dma("sp", bglu[:], I["s5_b_glu"][l].partition_broadcast(128), writes=[bglu.r])
        P.dma("sp", dsk[:], I["s5_d"][l].partition_broadcast(128), writes=[dsk.r])
        with contextlib.ExitStack() as su:
            utm = sbuf(k, su, "utm", [128, 16, 256], F32)
            utg = sbuf(k, su, "utg", [128, 4096], F32)
            P.dma("sp", utm[:, 0:2, :], I["s5_w_glu"][l].rearrange("(c p) n -> p c n", p=128), writes=[utm.r])
            P.op("dve", lambda e: e.tensor_copy(wglu[:], utm[:, 0:2, :]), reads=[utm.r], writes=[wglu.r])
            for (c0, n) in cbs:
                for sh_ in range(4):
                    P.dma("sp", utm[0:n, sh_ * 4:(sh_ + 1) * 4, :], tm.t[c0 * 16:(c0 + n) * 16, S50:S50 + 256].rearrange("(c s) n -> c s n", s=16)[:, sh_ * 4:(sh_ + 1) * 4, :], reads=[tm.r], writes=[utm.r])
                for g in range(16):
                    copy_op(k, ("dve", "act")[g % 2], utg[0:n, g * 256:(g + 1) * 256].rearrange("p (s h) -> p s h", h=16), utm[0:n, :, g * 16:(g + 1) * 16], [utm.r], [utg.r])
                for g in range(16):
                    pb = k.psb[g % 4]
                    for kc in range(2):
                        P.op("pe", lambda e, g=g, kc=kc, pb=pb, n=n: e.transpose(pb[:, kc * 128:kc * 128 + n], utg[0:n, g * 256 + kc * 128:g * 256 + (kc + 1) * 128], k.idf[0:n, 0:n]),
                             reads=[utg.r, k.idf.r], writes=[pb.r])
                    copy_op(k, ("act", "dve")[g % 2], U[:, g, :, c0:c0 + n], pb[:, 0:256].rearrange("p (a b) -> p a b", a=2)[:, :, 0:n], [pb.r], [U.r])
            P.barrier()
        pr = sbuf(k, st, "s5pr", [128, 16, 32], F32)
        AR, AI, DT, MAG, CO, SI, LR, LI, FR, FI, QR, QI, T1, T2, T3 = range(15)
        a0 = sbuf(k, st, "s5a0", [32, 2, 128], F32)
        for j, nm in enumerate(("s5_a_re", "s5_a_im")):
            for hcol in range(2):
                P.dma("sp", a0[:, j, hcol * 64:(hcol + 1) * 64], I[nm][l].rearrange("d g p -> (d g) p"), writes=[a0.r])
            P.op("pe", lambda e, j=j: e.transpose(k.psb[4][:, j * 32:(j + 1) * 32], a0[:, j, :], k.idf[0:32, 0:32]), reads=[a0.r, k.idf.r], writes=[k.psb[4].r])
        P.op("dve", lambda e: e.tensor_copy(pr[:, AR:AI + 1, :].rearrange("p a b -> p (a b)"), k.psb[4][:, 0:64]), reads=[k.psb[4].r], writes=[pr.r])
        P.dma("sp", pr[:, DT, :], I["s5_log_dt"][l].rearrange("d g -> (d g)").partition_broadcast(128), writes=[pr.r])
        R = [pr.r]
        P.op("act", lambda e: e.activation(pr[:, DT, :], pr[:, DT, :], AF.Exp), reads=R, writes=R)
        tt("dve", pr[:, T1, :], pr[:, DT, :], pr[:, AR, :], ALU.mult, R, R)
        P.op("act", lambda e: e.activation(pr[:, MAG, :], pr[:, T1, :], AF.Exp), reads=R, writes=R)
        tt("dve", pr[:, T1, :], pr[:, DT, :], pr[:, AI, :], ALU.mult, R, R)
        P.op("act", lambda e: e.activation(pr[:, SI, :], pr[:, T1, :], AF.Sin, scale=1.0 / 16), reads=R, writes=R)
        P.op("dve", lambda e: e.tensor_scalar(pr[:, T2, :], pr[:, T1, :], 1.0 / 16, 1.5707963267948966, ALU.mult, ALU.add), reads=R, writes=R)
        P.op("act", lambda e: e.activation(pr[:, CO, :], pr[:, T2, :], AF.Sin), reads=R, writes=R)
        for _ in range(4):
            tt("dve", pr[:, T1, :], pr[:, CO, :], pr[:, CO, :], ALU.mult, R, R)
            tt("dve", pr[:, T2, :], pr[:, SI, :], pr[:, SI, :], ALU.mult, R, R)
            P.op("dve", lambda e: e.scalar_tensor_tensor(pr[:, SI, :], pr[:, CO, :], 2.0, pr[:, SI, :], ALU.mult, ALU.mult), reads=R, writes=R)
            tt("dve", pr[:, CO, :], pr[:, T1, :], pr[:, T2, :], ALU.subtract, R, R)
        tt("dve", pr[:, LR, :], pr[:, MAG, :], pr[:, CO, :], ALU.mult, R, R)
        tt("dve", pr[:, LI, :], pr[:, MAG, :], pr[:, SI, :], ALU.mult, R, R)
        tt("dve", pr[:, T1, :], pr[:, AR, :], pr[:, AR, :], ALU.mult, R, R)
        tt("dve", pr[:, T2, :], pr[:, AI, :], pr[:, AI, :], ALU.mult, R, R)
        tt("dve", pr[:, T1, :], pr[:, T1, :], pr[:, T2, :], ALU.add, R, R)
        P.op("dve", lambda e: e.reciprocal(pr[:, T3, :], pr[:, T1, :]), reads=R, writes=R)
        P.op("dve", lambda e: e.tensor_scalar(pr[:, T1, :], pr[:, LR, :], -1.0, None, ALU.add), reads=R, writes=R)
        tt("dve", pr[:, FR, :], pr[:, T1, :], pr[:, AR, :], ALU.mult, R, R)
        tt("dve", pr[:, T2, :], pr[:, LI, :], pr[:, AI, :], ALU.mult, R, R)
        tt("dve", pr[:, FR, :], pr[:, FR, :], pr[:, T2, :], ALU.add, R, R)
        tt("dve", pr[:, FR, :], pr[:, FR, :], pr[:, T3, :], ALU.mult, R, R)
        tt("dve", pr[:, FI, :], pr[:, LI, :], pr[:, AR, :], ALU.mult, R, R)
        tt("dve", pr[:, T2, :], pr[:, T1, :], pr[:, AI, :], ALU.mult, R, R)
        tt("dve", pr[:, FI, :], pr[:, FI, :], pr[:, T2, :], ALU.subtract, R, R)
        tt("dve", pr[:, FI, :], pr[:, FI, :], pr[:, T3, :], ALU.mult, R, R)
        tt("dve", pr[:, T1, :], pr[:, MAG, :], pr[:, MAG, :], ALU.mult, R, R)
        P.op("dve", lambda e: e.reciprocal(pr[:, T1, :], pr[:, T1, :]), reads=R, writes=R)
        tt("dve", pr[:, QR, :], pr[:, LR, :], pr[:, T1, :], ALU.mult, R, R)
        P.op("dve", lambda e: e.scalar_tensor_tensor(pr[:, QI, :], pr[:, LI, :], -1.0, pr[:, T1, :], ALU.mult, ALU.mult), reads=R, writes=R)
        PW = sbuf(k, st, "s5pw", [128, 4, 2, 32, 17], F32)
        W = [PW.r]
        ptmp = sbuf(k, st, "s5ptmp", [128, 2, 32, 8], F32)
        for tb, (br, bi) in ((0, (LR, LI)), (2, (QR, QI))):
            P.op("dve", lambda e, tb=tb: e.memset(PW[:, tb, 0, :, 0:1], 1.0), writes=W)
            P.op("dve", lambda e, tb=tb: e.memset(PW[:, tb, 1, :, 0:1], 0.0), writes=W)
            P.op("dve", lambda e, tb=tb, br=br: e.tensor_copy(PW[:, tb, 0, :, 1], pr[:, br, :]), reads=R, writes=W)
            P.op("dve", lambda e, tb=tb, bi=bi: e.tensor_copy(PW[:, tb, 1, :, 1], pr[:, bi, :]), reads=R, writes=W)
            m = 1
            while m < 16:
                Ar, Ai = PW[:, tb, 0, :, 1:m + 1], PW[:, tb, 1, :, 1:m + 1]
                brr = PW[:, tb, 0, :, m:m + 1].to_broadcast([128, 32, m])
                bii = PW[:, tb, 1, :, m:m + 1].to_broadcast([128, 32, m])
                t1, t2 = ptmp[:, 0, :, 0:m], ptmp[:, 1, :, 0:m]
                RW = [PW.r, ptmp.r]
                tt("dve", t1, Ar, brr, ALU.mult, RW, RW)
                tt("dve", t2, Ai, bii, ALU.mult, RW, RW)
                tt("dve", PW[:, tb, 0, :, m + 1:2 * m + 1], t1, t2, ALU.subtract, RW, RW)
                tt("dve", t1, Ar, bii, ALU.mult, RW, RW)
                tt("dve", t2, Ai, brr, ALU.mult, RW, RW)
                tt("dve", PW[:, tb, 1, :, m + 1:2 * m + 1], t1, t2, ALU.add, RW, RW)
                m *= 2
        for j in range(17):
            P.op("pool", lambda e, j=j: e.tensor_copy(PW[:, 1, :, :, j], PW[:, 0, :, :, 16 - j]), reads=W, writes=W)
        for j in range(16):
            P.op("pool", lambda e, j=j: e.tensor_copy(PW[:, 3, :, :, j], PW[:, 2, :, :, 15 - j]), reads=W, writes=W)
        Bz = sbuf(k, st, "s5B", [128, 2, 16, 16], F32)
        Bb = sbuf(k, st, "s5Bb", [128, 2, 2, 16, 16], F32)
        P.dma("sp", Bz[0:64, 0], I["s5_b_re"][l].rearrange("g p h -> p g h"), writes=[Bz.r])
        P.dma("sp", Bz[64:128, 0], I["s5_b_im"][l].rearrange("g p h -> p g h"), writes=[Bz.r])
        P.dma("sp", Bz[0:64, 1], I["s5_b_im"][l].rearrange("g p h -> p g h"), writes=[Bz.r])
        P.dma("sp", Bz[64:128, 1], I["s5_b_re"][l].rearrange("g p h -> p g h"), writes=[Bz.r])
        P.op("dve", lambda e: e.tensor_scalar(Bz[0:64, 1], Bz[0:64, 1], -1.0, None, ALU.mult), reads=[Bz.r], writes=[Bz.r])
        btmp = sbuf(k, st, "s5btmp", [128, 2, 16, 16], F32)
        frv = pr[:, FR, :].rearrange("p (d g) -> p d g", d=2).unsqueeze(3).to_broadcast([128, 2, 16, 16])
        fiv = pr[:, FI, :].rearrange("p (d g) -> p d g", d=2).unsqueeze(3).to_broadcast([128, 2, 16, 16])
        bst = Bz[:, 0].unsqueeze(1).to_broadcast([128, 2, 16, 16])
        bsw = Bz[:, 1].unsqueeze(1).to_broadcast([128, 2, 16, 16])
        RB = [Bz.r, Bb.r, btmp.r, pr.r]
        tt("dve", Bb[:, 0], frv, bst, ALU.mult, RB, RB)
        tt("dve", btmp[:], fiv, bsw, ALU.mult, RB, RB)
        tt("dve", Bb[:, 0], Bb[:, 0], btmp[:], ALU.add, RB, RB)
        tt("dve", Bb[:, 1], frv, bsw, ALU.mult, RB, RB)
        tt("dve", btmp[:], fiv, bst, ALU.mult, RB, RB)
        tt("dve", Bb[:, 1], Bb[:, 1], btmp[:], ALU.subtract, RB, RB)
        Cz = sbuf(k, st, "s5C", [128, 2, 16, 16], F32)
        cin = sbuf(k, st, "s5cin", [128, 2, 2, 128], F32)
        cre = I["s5_c_re"][l].rearrange("g h p -> (g h) p")
        cim = I["s5_c_im"][l].rearrange("g h p -> (g h) p")
        for hf in range(2):
            P.dma("sp", cin[:, hf, 0, 0:64], cre[hf * 128:(hf + 1) * 128, :], writes=[cin.r])
            P.dma("sp", cin[:, hf, 0, 64:128], cim[hf * 128:(hf + 1) * 128, :], writes=[cin.r])
            P.dma("sp", cin[:, hf, 1, 0:64], cim[hf * 128:(hf + 1) * 128, :], writes=[cin.r])
            P.dma("sp", cin[:, hf, 1, 64:128], cre[hf * 128:(hf + 1) * 128, :], writes=[cin.r])
        for v in range(2):
            for hf in range(2):
                P.op("pe", lambda e, v=v, hf=hf: e.transpose(k.psb[5][:, (v * 2 + hf) * 128:(v * 2 + hf + 1) * 128], cin[:, hf, v, :], k.idf[:]), reads=[cin.r, k.idf.r], writes=[k.psb[5].r])
        P.op("dve", lambda e: e.tensor_copy(Cz[:].rearrange("p v g h -> p (v g h)"), k.psb[5][:]), reads=[k.psb[5].r], writes=[Cz.r])
        P.op("dve", lambda e: e.tensor_scalar(Cz[64:128, 0], Cz[64:128, 0], -1.0, None, ALU.mult), reads=[Cz.r], writes=[Cz.r])
        P.op("dve", lambda e: e.tensor_scalar(Cz[:, 1], Cz[:, 1], -1.0, None, ALU.mult), reads=[Cz.r], writes=[Cz.r])
        def s5_dir(d):
            with contextlib.ExitStack() as sd:
                Mm = sbuf(k, sd, "s5M", [128, 16, 2, 256], BF16)
                TGb = sbuf(k, sd, "s5G", [128, 16, 256], BF16)
                mu = sbuf(k, sd, "s5mu", [128, 2, 2, 16], F32)
                mp = sbuf(k, sd, "s5mp", [128, 3, 3, 16], F32)
                E2 = sbuf(k, sd, "s5E", [128, NCH, 2, 16], F32)
                gs = slice(d * 16, (d + 1) * 16)
                RM = [mp.r]
                P.op("dve", lambda e: e.tensor_copy(mp[:, 0, 0, :], PW[:, 0, 0, gs, 16]), reads=W, writes=RM)
                P.op("dve", lambda e: e.tensor_copy(mp[:, 0, 1, :], PW[:, 0, 1, gs, 16]), reads=W, writes=RM)
                for lv in range(1, 3):
                    tt("dve", mp[:, lv, 0, :], mp[:, lv - 1, 0, :], mp[:, lv - 1, 0, :], ALU.mult, RM, RM)
                    tt("dve", mp[:, lv, 2, :], mp[:, lv - 1, 1, :], mp[:, lv - 1, 1, :], ALU.mult, RM, RM)
                    tt("dve", mp[:, lv, 0, :], mp[:, lv, 0, :], mp[:, lv, 2, :], ALU.subtract, RM, RM)
                    P.op("dve", lambda e, lv=lv: e.scalar_tensor_tensor(mp[:, lv, 1, :], mp[:, lv - 1, 0, :], 2.0, mp[:, lv - 1, 1, :], ALU.mult, ALU.mult), reads=RM, writes=RM)
                for lv in range(3):
                    P.op("dve", lambda e, lv=lv: e.tensor_scalar(mp[:, lv, 2, :], mp[:, lv, 1, :], -1.0, None, ALU.mult), reads=RM, writes=RM)
                for sl_ in range(2):
                    P.op("dve", lambda e, sl_=sl_: e.tensor_copy(mu[:, 0, sl_, :], mp[:, 2, 0, :]), reads=RM, writes=[mu.r])
                P.op("dve", lambda e: e.tensor_copy(mu[:, 1, 0, :], mp[:, 2, 1, :]), reads=RM, writes=[mu.r])
                P.op("dve", lambda e: e.tensor_copy(mu[:, 1, 1, :], mp[:, 2, 2, :]), reads=RM, writes=[mu.r])
                with contextlib.ExitStack() as stt:
                    TA = sbuf(k, stt, "s5TA", [128, 16, 16, 16], F32)
                    TBt = sbuf(k, stt, "s5TB", [128, 16, 16, 16], F32)
                    TT = sbuf(k, stt, "s5TT", [128, 8, 16, 16], F32)
                    H2 = sbuf(k, stt, "s5H", [128, 8, 2, 256], BF16)

                    def table(dst, tb, s0, Z, zi):
                        RT = [dst.r, TT.r, PW.r, zi]
                        for gh_ in range(2):
                            g0 = d * 16 + gh_ * 8
                            hs = slice(gh_ * 8, (gh_ + 1) * 8)
                            xr = PW[:, tb, 0, g0:g0 + 8, s0:s0 + 16].unsqueeze(3).to_broadcast([128, 8, 16, 16])
                            xi = PW[:, tb, 1, g0:g0 + 8, s0:s0 + 16].unsqueeze(3).to_broadcast([128, 8, 16, 16])
                            zst = Z[0][:, hs].unsqueeze(2).to_broadcast([128, 8, 16, 16])
                            zsw = Z[1][:, hs].unsqueeze(2).to_broadcast([128, 8, 16, 16])
                            tt("dve", dst[:, hs], xr, zst, ALU.mult, RT, RT)
                            tt("dve", TT[:], xi, zsw, ALU.mult, [PW.r, zi, TT.r], [TT.r])
                            tt("dve", dst[:, hs], dst[:, hs], TT[:], ALU.add, RT, RT)

                    ZB = (Bb[:, 0, d], Bb[:, 1, d])
                    ZC = (Cz[:, 0], Cz[:, 1])
                    table(TA, 2 if d == 0 else 3, 0, ZB, Bb.r)
                    table(TBt, 0 if d == 0 else 1, 0 if d == 0 else 1, ZC, Cz.r)
                    for g in range(16):
                        for kc in range(2):
                            pb = k.psb[(g * 2 + kc) % 4]
                            P.op("pe", lambda e, g=g, kc=kc, pb=pb: e.matmul(pb[:, 0:256], TA[:, g].rearrange("p j x -> p (j x)")[:, kc * 128:(kc + 1) * 128], TBt[:, g].rearrange("p j x -> p (j x)"), start=True, stop=True),
                                 reads=[TA.r, TBt.r], writes=[pb.r])
                            P.op("dve", lambda e, g=g, kc=kc, pb=pb: e.tensor_tensor(Mm[:, g, kc, :], pb[:, 0:256], cs[:, 1 + d * 2 + kc, :], ALU.mult), reads=[pb.r, cs.r], writes=[Mm.r])
                    table(TBt, 0 if d == 0 else 1, 1 if d == 0 else 0, ZC, Cz.r)
                    P.op("act", lambda e: e.copy(TGb[:].rearrange("p g n -> p (g n)"), TBt[:].rearrange("p g j x -> p (g j x)")), reads=[TBt.r], writes=[TGb.r])
                    table(TA, 1 if d == 0 else 0, 1 if d == 0 else 0, ZB, Bb.r)
                    for gh_ in range(2):
                        for g8 in range(8):
                            g = gh_ * 8 + g8
                            for kc in range(2):
                                pb = k.psb[(g * 2 + kc) % 4]
                                P.op("pe", lambda e, g=g, kc=kc, pb=pb: e.matmul(pb[:, 0:256], TA[:, g].rearrange("p j x -> p (j x)")[:, kc * 128:(kc + 1) * 128], cs[:, 0, :], start=True, stop=True),
                                     reads=[TA.r, cs.r], writes=[pb.r])
                                copy_op(k, ("act", "dve")[kc], H2[:, g8, kc, :], pb[:, 0:256], [pb.r], [H2.r])
                        for g8 in range(8):
                            g = gh_ * 8 + g8
                            for hf in range(2):
                                pb = k.psb[4 + (g * 2 + hf) % 4]
                                for kc in range(2):
                                    P.op("pe", lambda e, g=g, g8=g8, hf=hf, kc=kc, pb=pb: e.matmul(pb[:, 0:NCH], H2[:, g8, kc, hf * 128:(hf + 1) * 128], U[:, g, kc, :], start=(kc == 0), stop=(kc == 1)),
                                         reads=[H2.r, U.r], writes=[pb.r])
                                copy_op(k, ("act", "dve")[hf], E2[:, :, hf, g], pb[:, 0:NCH], [pb.r], [E2.r])
                    P.barrier()
                Xbf = sbuf(k, sd, "s5X", [128, 16, NCH], BF16)
                E1 = sbuf(k, sd, "s5E1", [128, NCH // 2, 2, 16], F32)
                Eq = sbuf(k, sd, "s5Eq", [128, NCH // 4, 2, 16], F32)
                Qp = sbuf(k, sd, "s5Qp", [128, NCH // 4, 2, 16], F32)
                P1 = sbuf(k, sd, "s5P1", [128, NCH // 2, 2, 8], F32)
                Zb = [sbuf(k, sd, "s5Z%d" % i, [128, 3, 16], F32) for i in range(4)]
                zt_ = [sbuf(k, sd, "s5zt%d" % i, [128, 2, 16], F32) for i in range(2)]
                fi, si = (0, 1) if d == 0 else (1, 0)

                def pairs(ap, which):
                    return ap.rearrange("p (m two) s g -> p m two s g", two=2)[:, :, which]

                eqv = Eq[:].rearrange("p q s g -> p (q s g)").rearrange("p (m s g) -> p m s g", s=2, g=8)

                def cmadd(out, a, lv, gsl, bterm, n, res_r, res_w, tfull, tres):
                    mr = mp[:, lv, 0, gsl].unsqueeze(1).unsqueeze(1).to_broadcast([128, n, 2, 8])
                    mi = mp[:, lv, 1, gsl].unsqueeze(1).to_broadcast([128, n, 8])
                    nmi = mp[:, lv, 2, gsl].unsqueeze(1).to_broadcast([128, n, 8])
                    tv = tfull[:, 0:n]
                    rr_ = res_r + [mp.r, tres]
                    tt("dve", out, a, mr, ALU.mult, rr_, res_w)
                    tt("dve", tv[:, :, 0, :], a[:, :, 1, :], mi, ALU.mult, rr_, [tres])
                    tt("dve", tv[:, :, 1, :], a[:, :, 0, :], nmi, ALU.mult, rr_, [tres])
                    tt("dve", out, out, tv, ALU.add, rr_ + res_w, res_w)
                    tt("dve", out, out, bterm, ALU.add, rr_ + res_w, res_w)

                for gh in range(2):
                    gsl = slice(gh * 8, (gh + 1) * 8)
                    cmadd(E1[:, :, :, gsl], pairs(E2[:], fi)[:, :, :, gsl], 0, gsl, pairs(E2[:], si)[:, :, :, gsl], NCH // 2, [E2.r], [E1.r], P1[:], P1.r)
                for gh in range(2):
                    gsl = slice(gh * 8, (gh + 1) * 8)
                    cmadd(Eq[:, :, :, gsl], pairs(E1[:], fi)[:, :, :, gsl], 1, gsl, pairs(E1[:], si)[:, :, :, gsl], NCH // 4, [E1.r], [Eq.r], P1[:], P1.r)
                eng = "dve"
                P.op(eng, lambda e: e.memset(Zb[0][:], 0.0), writes=[Zb[0].r])
                NQ = NCH // 4
                qorder = list(range(NQ)) if d == 0 else [3, 2, 1, 0] + list(range(NQ - 1, 3, -1))
                for i, q in enumerate(qorder):
                    zo, zn = Zb[i % 4], Zb[(i + 1) % 4]
                    P.op("pool", lambda e, zo=zo, q=q: e.tensor_copy(Qp[:, q, :, :], zo[:, 0:2, :]), reads=[zo.r], writes=[Qp.r])
                    t1, t2 = zt_[0], zt_[1]
                    tt(eng, t1[:], zo[:, 0:2, :], mu[:, 0], ALU.mult, [zo.r, mu.r], [t1.r])
                    tt(eng, t2[:], zo[:, 1:3, :], mu[:, 1], ALU.mult, [zo.r, mu.r], [t2.r])
                    tt(eng, t1[:], t1[:], t2[:], ALU.add, [t1.r, t2.r], [t1.r])
                    tt(eng, zn[:, 0:2, :], t1[:], Eq[:, q, :, :], ALU.add, [t1.r, Eq.r], [zn.r])
                    P.op(eng, lambda e, zn=zn: e.tensor_copy(zn[:, 2, :], zn[:, 0, :]), reads=[zn.r], writes=[zn.r])
                for gh in range(2):
                    gsl = slice(gh * 8, (gh + 1) * 8)
                    p1f = pairs(P1[:], fi)
                    p1s = pairs(P1[:], si)
                    P.op("dve", lambda e, p1f=p1f, gsl=gsl: e.tensor_copy(p1f, Qp[:, :, :, gsl]), reads=[Qp.r], writes=[P1.r])
                    cmadd(p1s, Qp[:, :, :, gsl], 1, gsl, pairs(E1[:], fi)[:, :, :, gsl], NCH // 4, [Qp.r, E1.r], [P1.r], eqv, Eq.r)
                    xv = Xbf[:, gsl, :].rearrange("p g (m two) -> p g m two", two=2)
                    P.op("act", lambda e, xv=xv: e.copy(xv[:, :, :, fi], P1[:, :, 0, :].rearrange("p m g -> p g m")), reads=[P1.r], writes=[Xbf.r])
                    n2 = NCH // 2
                    mr = mp[:, 0, 0, gsl].unsqueeze(1).to_broadcast([128, n2, 8])
                    mi = mp[:, 0, 1, gsl].unsqueeze(1).to_broadcast([128, n2, 8])
                    RX = [P1.r, Eq.r, mp.r, E2.r]
                    tt("dve", eqv[:, :, 0, :], P1[:, :, 0, :], mr, ALU.mult, RX, [Eq.r])
                    tt("dve", eqv[:, :, 1, :], P1[:, :, 1, :], mi, ALU.mult, RX, [Eq.r])
                    tt("dve", eqv[:, :, 0, :], eqv[:, :, 0, :], eqv[:, :, 1, :], ALU.add, RX, [Eq.r])
                    tt("dve", xv[:, :, :, si], eqv[:, :, 0, :].rearrange("p m g -> p g m"), pairs(E2[:], fi)[:, :, 0, gsl].rearrange("p m g -> p g m"), ALU.add, RX, [Xbf.r])
                for g in range(16):
                    for mc in range(2):
                        pb = k.psb[(g * 2 + mc) % 4]
                        for kc in range(2):
                            P.op("pe", lambda e, g=g, mc=mc, kc=kc, pb=pb: e.matmul(pb[:, 0:NCH], Mm[:, g, kc, mc * 128:(mc + 1) * 128], U[:, g, kc, :], start=(kc == 0), stop=False),
                                 reads=[Mm.r, U.r], writes=[pb.r])
                        P.op("pe", lambda e, g=g, mc=mc, pb=pb: e.matmul(pb[:, 0:NCH], TGb[:, g, mc * 128:(mc + 1) * 128], Xbf[:, g, :], start=False, stop=True),
                             reads=[TGb.r, Xbf.r], writes=[pb.r])
                        if d == 0:
                            copy_op(k, ("act", "dve")[mc], Yacc[:, mc, g, :], pb[:, 0:NCH], [pb.r], [Yacc.r])
                        else:
                            P.op("dve", lambda e, g=g, mc=mc, pb=pb: e.tensor_tensor(Yacc[:, mc, g, :], Yacc[:, mc, g, :], pb[:, 0:NCH], ALU.add), reads=[pb.r, Yacc.r], writes=[Yacc.r])
                P.barrier()
        for d_ in range(2):
            s5_dir(d_)
        dbg_dump(k, "dbgY", Yacc, [128, 2, 16, NCH])
        dbg_dump(k, "dbgPW", PW, [128, 4, 2, 32, 17])
        dbg_dump(k, "dbgpr", pr, [128, 16, 32])
        dbg_dump(k, "dbgBb", Bb, [128, 2, 2, 16, 16])
        dbg_dump(k, "dbgCz", Cz, [128, 2, 16, 16])
        P.barrier()
        with contextlib.ExitStack() as so:
            ytm = sbuf(k, so, "s5ytm", [128, 16, 256], F32)
            utm2 = sbuf(k, so, "utm2", [128, 16, 256], F32)
            tq = sbuf(k, so, "s5tq", [128, 16, 256], F32)
            glb = sbuf(k, so, "s5glb", [128, 16, 256], BF16)
            glT = sbuf(k, so, "s5glT", [128, 2, 128], BF16)
            zz = sbuf(k, so, "s5zz", [128, 256], F32)
            ob = sbuf(k, so, "s5ob", [128, 16, 256], BF16)
            for (c0, n) in cbs:
                if l == DEPTH - 1 and False:
                    continue
                for sh_ in range(4):
                    P.dma("sp", utm2[0:n, sh_ * 4:(sh_ + 1) * 4, :], tm.t[c0 * 16:(c0 + n) * 16, S50:S50 + 256].rearrange("(c s) n -> c s n", s=16)[:, sh_ * 4:(sh_ + 1) * 4, :], reads=[tm.r], writes=[utm2.r])
                for g in range(16):
                    for mc in range(2):
                        pb = k.psb[(g * 2 + mc) % 4]
                        P.op("pe", lambda e, g=g, mc=mc, pb=pb, c0=c0, n=n: e.transpose(pb[0:n, 0:128], Yacc[:, mc, g, c0:c0 + n], k.idf[:]), reads=[Yacc.r, k.idf.r], writes=[pb.r])
                        copy_op(k, ("act", "dve")[mc], ytm[0:n, mc * 8:(mc + 1) * 8, g * 16:(g + 1) * 16], pb[0:n, 0:128].rearrange("p (t h) -> p t h", h=16), [pb.r], [ytm.r])
                dv = dsk[0:n, :].unsqueeze(1).to_broadcast([n, 16, 256])
                tt("pool", tq[0:n], utm2[0:n], dv, ALU.mult, [utm2.r, dsk.r], [tq.r])
                tt("dve", ytm[0:n], ytm[0:n], tq[0:n], ALU.add, [ytm.r, tq.r], [ytm.r])
                tt("pool", tq[0:n], ytm[0:n], ytm[0:n], ALU.mult, [ytm.r], [tq.r])
                P.op("dve", lambda e, n=n: e.tensor_scalar(tq[0:n], tq[0:n], 0.044715, 1.0, ALU.mult, ALU.add), reads=[tq.r], writes=[tq.r])
                tt("pool", tq[0:n], tq[0:n], ytm[0:n], ALU.mult, [tq.r, ytm.r], [tq.r])
                P.op("act", lambda e, n=n: e.activation(tq[0:n], tq[0:n], AF.Sigmoid, scale=1.5957691216057308), reads=[tq.r], writes=[tq.r])
                tt("dve", ytm[0:n], ytm[0:n], tq[0:n], ALU.mult, [ytm.r, tq.r], [ytm.r])
                P.op("act", lambda e, n=n: e.copy(glb[0:n], ytm[0:n]), reads=[ytm.r], writes=[glb.r])
                for s_ in range(16):
                    pt = k.psb[4 + s_ % 2]
                    pv = pt[:].bitcast(BF16)
                    for kc in range(2):
                        P.op("pe", lambda e, s_=s_, kc=kc, pv=pv, n=n: e.transpose(pv[:, kc * 128:kc * 128 + n], glb[0:n, s_, kc * 128:(kc + 1) * 128], k.idb[0:n, 0:n]), reads=[glb.r, k.idb.r], writes=[pt.r])
                    copy_op(k, "act", glT[:, :, 0:n], pv[:, 0:256].rearrange("p (a b) -> p a b", a=2)[:, :, 0:n], [pt.r], [glT.r])
                    pz = k.psb[6 + s_ % 2]
                    for kc in range(2):
                        P.op("pe", lambda e, kc=kc, pz=pz, n=n: e.matmul(pz[0:n, 0:256], glT[:, kc, 0:n], wglu[:, kc, :], start=(kc == 0), stop=(kc == 1)), reads=[glT.r, wglu.r], writes=[pz.r])
                    tt("dve", zz[0:n], pz[0:n, 0:256], bglu[0:n], ALU.add, [pz.r, bglu.r], [zz.r])
                    P.op("act", lambda e, n=n: e.activation(zz[0:n], zz[0:n], AF.Sigmoid), reads=[zz.r], writes=[zz.r])
                    tt("dve", ob[0:n, s_, :], ytm[0:n, s_, :], zz[0:n], ALU.mult, [ytm.r, zz.r], [ob.r])
                for sh_ in range(4):
                    P.dma("act", S["mix"].t[c0 * 16:(c0 + n) * 16, 384:640].rearrange("(c s) n -> c s n", s=16)[:, sh_ * 4:(sh_ + 1) * 4, :], ob[0:n, sh_ * 4:(sh_ + 1) * 4, :], reads=[ob.r], dwrites=[S["mix"].r])
            P.barrier()


def make_consts():
    import ml_dtypes
    idf = np.eye(128, dtype=np.float32)
    idb = idf.astype(ml_dtypes.bfloat16)
    m = np.zeros((NCST, 128, 128), np.float32)
    r = np.arange(128)[:, None]
    c = np.arange(128)[None, :]
    m[0] = (r <= c)
    m[1] = (r >= c)
    m[2] = (r <= c) * (-1.0 / 16.0)
    m[3] = (r >= c) * (-1.0 / 16.0)
    m[4] = (r > c) * (-1.0 / 16.0)
    m[5] = (r < c) * (-1.0 / 16.0)
    m[6] = (r > c)
    m[7] = (r < c)
    c5 = np.zeros((5, 128, 256), np.float32)
    c5[0, :, 0:128] = np.eye(128)
    for p in range(64):
        c5[0, 64 + p, 128 + p] = -1.0
        c5[0, p, 128 + 64 + p] = 1.0
    sl = np.arange(128)[:, None] // 16
    tt_ = np.arange(256)[None, :] // 16
    for kc in range(2):
        c5[1 + kc] = (tt_ >= kc * 8 + sl)
        c5[3 + kc] = (tt_ <= kc * 8 + sl)
    return idb, idf, m, c5


W_KEYS = ["ada_w", "ada_b", "norm_g", "w_in", "w_out", "ff_w_gate", "ff_w_up", "ff_w_down", "gla_w_gate", "gla_b_gate",
          "gla_norm_g", "s5_a_re", "s5_a_im", "s5_log_dt", "s5_b_re", "s5_b_im", "s5_c_re", "s5_c_im", "s5_d", "s5_w_glu",
          "s5_b_glu", "ssd_conv_w", "ssd_conv_b", "ssd_dt_bias", "ssd_a_log", "ssd_d", "ssd_norm_g"]


def make_in_maps(inputs, cores):
    idb, idf, m, c5 = make_consts()
    shared = {kk: np.ascontiguousarray(np.asarray(inputs[kk], dtype=np.float32)) for kk in W_KEYS}
    shared["final_norm_g"] = np.ascontiguousarray(np.asarray(inputs["final_norm_g"], np.float32).reshape(1, D))
    shared["cst_idb"] = idb
    shared["cst_idf"] = idf
    shared["cst_m"] = m
    shared["cst_s5"] = c5
    x = np.asarray(inputs["x"], np.float32)
    c = np.asarray(inputs["c"], np.float32)
    ctx = np.asarray(inputs["ctx"], np.float32)
    c_ctx = np.asarray(inputs["c_ctx"], np.float32)
    maps = []
    for b in cores:
        d = dict(shared)
        d["x"] = np.ascontiguousarray(x[b])
        d["ctx"] = np.ascontiguousarray(ctx[b])
        d["cc"] = np.ascontiguousarray(np.stack([c[b], c_ctx], 0))
        maps.append(d)
    return maps


def kernel(**inputs):
    nc = build()
    maps = make_in_maps(inputs, list(range(8)))
    res = run_bass_kernel_spmd(nc, maps, core_ids=list(range(8)))
    return np.stack([np.asarray(r["out"], np.float32) for r in res.results], 0)
```
